# Optimizing a Trainium2 kernel written in Bass

```python
import math
import jax, jax.numpy as jnp
from jax import lax
import numpy as np

D_MODEL = 1024
BATCH = 4
SEQ = 8192
DEPTH = 1
DEC_BATCH = 8
DEC_SEQ = 64
PAST_LEN = 2048

CHUNK = 64
CONV_W = 4
FFN_CONV_W = 3
EPS = 1e-6
GDN_HEADS = 8
GDN_DK = 128
GDN_DV = 128
GDN_QK = GDN_HEADS * GDN_DK
GDN_V = GDN_HEADS * GDN_DV
SSD_INNER = 2 * D_MODEL
SSD_HEADDIM = 64
SSD_HEADS = SSD_INNER // SSD_HEADDIM
SSD_STATE = 128
SSD_GROUPS = 8
SSD_BC = SSD_GROUPS * SSD_STATE
D_FF = 2816
CONV_CH = 2 * GDN_QK + GDN_V + SSD_INNER + 2 * SSD_BC
IN_COLS = CONV_CH + GDN_V + SSD_INNER + 2 * GDN_HEADS + SSD_HEADS + 2 * D_MODEL

kernel_name = 'hybrid_gdn_ssd_convffn_stream_step'


def _split(a, sizes):
    idx = np.cumsum(sizes)[:-1].tolist()
    return jnp.split(a, idx, axis=-1)


def _rms(xf):
    return xf * lax.rsqrt(jnp.mean(xf * xf, axis=-1, keepdims=True) + EPS)


def rmsnorm(x, w):
    return (_rms(x.astype(jnp.float32)) * w.astype(jnp.float32)).astype(x.dtype)


def l2norm(x):
    xf = x.astype(jnp.float32)
    return xf * lax.rsqrt(jnp.sum(xf * xf, axis=-1, keepdims=True) + EPS)


def causal_dwconv(u, buf, w, b):
    W = w.shape[0]
    T = u.shape[1]
    full = jnp.concatenate([buf.astype(u.dtype), u], axis=1)
    y = b + sum(full[:, i:i + T] * w[i] for i in range(W))
    return y, full[:, T:]


def to_chunks(a, blk):
    bsz, t = a.shape[:2]
    return jnp.moveaxis(a.reshape((bsz, t // blk, blk) + a.shape[2:]), 1, 0)


def from_chunks(a):
    a = jnp.moveaxis(a, 0, 1)
    return a.reshape((a.shape[0], a.shape[1] * a.shape[2]) + a.shape[3:])


def gated_delta_chunked(q, k, v, g, beta, s0):
    t = q.shape[1]
    blk = min(CHUNK, t)
    xs = tuple(to_chunks(a, blk) for a in (q, k, v, g, beta))
    causal = jnp.tril(jnp.ones((blk, blk), dtype=bool))
    strict = jnp.tril(jnp.ones((blk, blk), dtype=bool), -1)
    eye = jnp.eye(blk, dtype=jnp.float32)

    def step(s, inp):
        qc, kc, vc, gc, bc = inp
        gam = jnp.moveaxis(jnp.cumsum(gc, axis=1), 1, -1)
        dec = jnp.exp(jnp.where(causal, gam[..., :, None] - gam[..., None, :], -jnp.inf))
        bt = jnp.moveaxis(bc, 1, -1)
        qh = jnp.moveaxis(qc, 1, 2)
        kh = jnp.moveaxis(kc, 1, 2)
        vh = jnp.moveaxis(vc, 1, 2)
        kk = jnp.einsum('bhik,bhjk->bhij', kh, kh)
        lhs = jnp.where(strict, kk * dec * bt[..., :, None], 0.0) + eye
        rhs = jnp.concatenate([vh * bt[..., None], kh * (bt * jnp.exp(gam))[..., None]], axis=-1)
        sol = lax.linalg.triangular_solve(lhs, rhs, left_side=True, lower=True, unit_diagonal=True)
        w_v, w_k = sol[..., :GDN_DV], sol[..., GDN_DV:]
        u = w_v - jnp.einsum('bhik,bhkv->bhiv', w_k, s)
        qk = jnp.einsum('bhik,bhjk->bhij', qh, kh) * dec
        o = jnp.exp(gam)[..., None] * jnp.einsum('bhik,bhkv->bhiv', qh, s) + jnp.einsum('bhij,bhjv->bhiv', qk, u)
        last = gam[..., -1:]
        s = jnp.exp(last)[..., None] * s + jnp.einsum('bhik,bhiv->bhkv', kh * jnp.exp(last - gam)[..., None], u)
        return s, jnp.moveaxis(o, 2, 1)

    s_fin, outs = lax.scan(step, s0.astype(jnp.float32), xs)
    return from_chunks(outs), s_fin


def ssd_chunked(x, dt, a_neg, bm, cm, h0):
    bsz, t = x.shape[:2]
    blk = min(CHUNK, t)
    hpg = SSD_HEADS // SSD_GROUPS
    f32 = jnp.float32
    xs = (to_chunks(x.astype(f32).reshape(bsz, t, SSD_GROUPS, hpg, SSD_HEADDIM), blk),
          to_chunks(dt.reshape(bsz, t, SSD_GROUPS, hpg), blk),
          to_chunks(bm.astype(f32), blk),
          to_chunks(cm.astype(f32), blk))
    a = a_neg.reshape(SSD_GROUPS, hpg)
    causal = jnp.tril(jnp.ones((blk, blk), dtype=bool))

    def step(h, inp):
        xc, dtc, bc, cc = inp
        gam = jnp.cumsum(dtc * a, axis=1)
        gam_t = jnp.moveaxis(gam, 1, -1)
        dt_t = jnp.moveaxis(dtc, 1, -1)
        dec = jnp.exp(jnp.where(causal, gam_t[..., :, None] - gam_t[..., None, :], -jnp.inf))
        cb = jnp.einsum('bign,bjgn->bgij', cc, bc)
        w = cb[:, :, None] * dec * dt_t[..., None, :]
        y = jnp.einsum('bghij,bjghp->bighp', w, xc)
        y = y + jnp.einsum('bign,bghnp->bighp', cc, h) * jnp.exp(gam)[..., None]
        last = gam_t[..., -1:]
        wj = jnp.exp(last - gam_t) * dt_t
        h = jnp.exp(last)[..., None] * h + jnp.einsum('bjgn,bghj,bjghp->bghnp', bc, wj, xc)
        return h, y

    h0g = h0.astype(f32).reshape(bsz, SSD_GROUPS, hpg, SSD_STATE, SSD_HEADDIM)
    h_fin, ys = lax.scan(step, h0g, xs)
    y = from_chunks(ys).reshape(bsz, t, SSD_HEADS, SSD_HEADDIM)
    return y, h_fin.reshape(bsz, SSD_HEADS, SSD_STATE, SSD_HEADDIM)


def trunk_layer(x, c, conv_mix0, s_gdn0, s_ssd0, conv_ffn0,
                w_ada, b_ada, w_norm_mix, w_in, w_conv_mix, b_conv_mix,
                gdn_a_log, gdn_dt_bias, gdn_norm, ssd_a_log, ssd_dt_bias, ssd_d, ssd_norm,
                w_branch_gdn, w_branch_ssd, w_out,
                w_norm_ffn, w_ffn_up, w_ffn_conv, b_ffn_conv, w_ffn_down):
    f32 = jnp.float32
    bsz, t, _ = x.shape
    mod = (jax.nn.silu(c) @ w_ada + b_ada)[:, None, :]
    sh1, sc1, gt1, sh2, sc2, gt2 = jnp.split(mod, 6, axis=-1)

    h = rmsnorm(x, w_norm_mix) * (1 + sc1) + sh1
    proj = h @ w_in
    xconv, gdn_z, ssd_z, gdn_b, gdn_a, ssd_dt, merge = _split(
        proj, [CONV_CH, GDN_V, SSD_INNER, GDN_HEADS, GDN_HEADS, SSD_HEADS, 2 * D_MODEL])
    xconv, conv_mix_new = causal_dwconv(xconv, conv_mix0, w_conv_mix, b_conv_mix)
    xconv = jax.nn.silu(xconv)
    q, k, v, xs, bm, cm = _split(xconv, [GDN_QK, GDN_QK, GDN_V, SSD_INNER, SSD_BC, SSD_BC])

    q = l2norm(q.reshape(bsz, t, GDN_HEADS, GDN_DK)) * (GDN_DK ** -0.5)
    k = l2norm(k.reshape(bsz, t, GDN_HEADS, GDN_DK))
    v = v.reshape(bsz, t, GDN_HEADS, GDN_DV).astype(f32)
    beta = jax.nn.sigmoid(gdn_b.astype(f32))
    g = -jnp.exp(gdn_a_log.astype(f32)) * jax.nn.softplus(gdn_a.astype(f32) + gdn_dt_bias.astype(f32))
    o_gdn, s_gdn_new = gated_delta_chunked(q, k, v, g, beta, s_gdn0)
    zg = gdn_z.reshape(bsz, t, GDN_HEADS, GDN_DV).astype(f32)
    o_gdn = (_rms(o_gdn) * gdn_norm.astype(f32) * jax.nn.silu(zg)).reshape(bsz, t, GDN_V).astype(x.dtype)

    xs = xs.reshape(bsz, t, SSD_HEADS, SSD_HEADDIM)
    bm = bm.reshape(bsz, t, SSD_GROUPS, SSD_STATE)
    cm = cm.reshape(bsz, t, SSD_GROUPS, SSD_STATE)
    dt = jax.nn.softplus(ssd_dt.astype(f32) + ssd_dt_bias.astype(f32))
    a_neg = -jnp.exp(ssd_a_log.astype(f32))
    y_ssd, s_ssd_new = ssd_chunked(xs, dt, a_neg, bm, cm, s_ssd0)
    y_ssd = y_ssd + ssd_d.astype(f32)[:, None] * xs.astype(f32)
    gsz = SSD_INNER // SSD_GROUPS
    yz = (y_ssd.reshape(bsz, t, SSD_INNER) * jax.nn.silu(ssd_z.astype(f32))).reshape(bsz, t, SSD_GROUPS, gsz)
    y_ssd = (_rms(yz) * ssd_norm.astype(f32).reshape(SSD_GROUPS, gsz)).reshape(bsz, t, SSD_INNER).astype(x.dtype)

    g_gdn, g_ssd = jnp.split(jax.nn.sigmoid(merge.astype(f32)).astype(x.dtype), 2, axis=-1)
    mixed = g_gdn * (o_gdn @ w_branch_gdn) + g_ssd * (y_ssd @ w_branch_ssd)
    x = x + gt1 * (mixed @ w_out)

    h2 = rmsnorm(x, w_norm_ffn) * (1 + sc2) + sh2
    u = h2 @ w_ffn_up
    u, conv_ffn_new = causal_dwconv(u, conv_ffn0, w_ffn_conv, b_ffn_conv)
    ua, uv = jnp.split(u, 2, axis=-1)
    x = x + gt2 * ((jax.nn.silu(ua) * uv) @ w_ffn_down)
    return (x, conv_mix_new, s_gdn_new.astype(x.dtype), s_ssd_new.astype(x.dtype), conv_ffn_new)


def setup_inputs(seed: int = 0) -> dict:
    key = jax.random.key(seed)
    ks = jax.random.split(key, 32)
    f32 = jnp.float32
    L = DEPTH

    def nrm(k, shape, scale):
        return scale * jax.random.normal(k, shape, f32)

    def inv_softplus_dt(k, shape):
        dt = jnp.exp(jax.random.uniform(k, shape, f32, minval=math.log(1e-3), maxval=math.log(1e-1)))
        return dt + jnp.log(-jnp.expm1(-dt))

    return {
        'x_prompt': nrm(ks[0], (BATCH, SEQ, D_MODEL), 1.0),
        'x_sample': nrm(ks[1], (DEC_BATCH, DEC_SEQ, D_MODEL), 1.0),
        'state_conv_mix': nrm(ks[2], (L, DEC_BATCH, CONV_W - 1, CONV_CH), 1.0),
        'state_gdn': nrm(ks[3], (L, DEC_BATCH, GDN_HEADS, GDN_DK, GDN_DV), 0.1),
        'state_ssd': nrm(ks[4], (L, DEC_BATCH, SSD_HEADS, SSD_STATE, SSD_HEADDIM), 0.1),
        'state_conv_ffn': nrm(ks[5], (L, DEC_BATCH, FFN_CONV_W - 1, 2 * D_FF), 1.0),
        'c_prompt': nrm(ks[6], (BATCH, D_MODEL), 1.0),
        'c_sample': nrm(ks[7], (DEC_BATCH, D_MODEL), 1.0),
        'w_ada': nrm(ks[8], (L, D_MODEL, 6 * D_MODEL), D_MODEL ** -0.5),
        'b_ada': nrm(ks[9], (L, 6 * D_MODEL), 0.02),
        'w_norm_mix': 1.0 + nrm(ks[10], (L, D_MODEL), 0.02),
        'w_in': nrm(ks[11], (L, D_MODEL, IN_COLS), D_MODEL ** -0.5),
        'w_conv_mix': nrm(ks[12], (L, CONV_W, CONV_CH), CONV_W ** -0.5),
        'b_conv_mix': nrm(ks[13], (L, CONV_CH), 0.02),
        'gdn_a_log': jnp.log(jax.random.uniform(ks[14], (L, GDN_HEADS), f32, minval=1.0, maxval=16.0)),
        'gdn_dt_bias': inv_softplus_dt(ks[15], (L, GDN_HEADS)),
        'gdn_norm': 1.0 + nrm(ks[16], (L, GDN_DV), 0.02),
        'ssd_a_log': jnp.log(jax.random.uniform(ks[17], (L, SSD_HEADS), f32, minval=1.0, maxval=16.0)),
        'ssd_dt_bias': inv_softplus_dt(ks[18], (L, SSD_HEADS)),
        'ssd_d': 1.0 + nrm(ks[19], (L, SSD_HEADS), 0.02),
        'ssd_norm': 1.0 + nrm(ks[20], (L, SSD_INNER), 0.02),
        'w_branch_gdn': nrm(ks[21], (L, GDN_V, D_MODEL), GDN_V ** -0.5),
        'w_branch_ssd': nrm(ks[22], (L, SSD_INNER, D_MODEL), SSD_INNER ** -0.5),
        'w_out': nrm(ks[23], (L, D_MODEL, D_MODEL), D_MODEL ** -0.5),
        'w_norm_ffn': 1.0 + nrm(ks[24], (L, D_MODEL), 0.02),
        'w_ffn_up': nrm(ks[25], (L, D_MODEL, 2 * D_FF), D_MODEL ** -0.5),
        'w_ffn_conv': nrm(ks[26], (L, FFN_CONV_W, 2 * D_FF), FFN_CONV_W ** -0.5),
        'b_ffn_conv': nrm(ks[27], (L, 2 * D_FF), 0.02),
        'w_ffn_down': nrm(ks[28], (L, D_FF, D_MODEL), D_FF ** -0.5),
        'w_norm_final': 1.0 + nrm(ks[29], (D_MODEL,), 0.02),
    }


def reference(x_prompt, x_sample, state_conv_mix, state_gdn, state_ssd, state_conv_ffn,
              c_prompt, c_sample,
              w_ada, b_ada, w_norm_mix, w_in, w_conv_mix, b_conv_mix,
              gdn_a_log, gdn_dt_bias, gdn_norm, ssd_a_log, ssd_dt_bias, ssd_d, ssd_norm,
              w_branch_gdn, w_branch_ssd, w_out,
              w_norm_ffn, w_ffn_up, w_ffn_conv, b_ffn_conv, w_ffn_down, w_norm_final):
    nb = x_prompt.shape[0]
    dtype = x_prompt.dtype
    xp, xsm = x_prompt, x_sample
    cm_p, gd_p, ss_p, cf_p = [], [], [], []
    cm_s, gd_s, ss_s, cf_s = [], [], [], []
    for l in range(DEPTH):
        params = (w_ada[l], b_ada[l], w_norm_mix[l], w_in[l], w_conv_mix[l], b_conv_mix[l],
                  gdn_a_log[l], gdn_dt_bias[l], gdn_norm[l], ssd_a_log[l], ssd_dt_bias[l], ssd_d[l], ssd_norm[l],
                  w_branch_gdn[l], w_branch_ssd[l], w_out[l],
                  w_norm_ffn[l], w_ffn_up[l], w_ffn_conv[l], b_ffn_conv[l], w_ffn_down[l])
        z_conv = jnp.zeros((nb, CONV_W - 1, CONV_CH), dtype)
        z_gdn = jnp.zeros((nb, GDN_HEADS, GDN_DK, GDN_DV), dtype)
        z_ssd = jnp.zeros((nb, SSD_HEADS, SSD_STATE, SSD_HEADDIM), dtype)
        z_ffn = jnp.zeros((nb, FFN_CONV_W - 1, 2 * D_FF), dtype)
        xp, a1, a2, a3, a4 = trunk_layer(xp, c_prompt, z_conv, z_gdn, z_ssd, z_ffn, *params)
        xsm, b1, b2, b3, b4 = trunk_layer(xsm, c_sample, state_conv_mix[l], state_gdn[l], state_ssd[l],
                                          state_conv_ffn[l], *params)
        cm_p.append(a1); gd_p.append(a2); ss_p.append(a3); cf_p.append(a4)
        cm_s.append(b1); gd_s.append(b2); ss_s.append(b3); cf_s.append(b4)
    y_prompt = rmsnorm(xp, w_norm_final)
    y_sample = rmsnorm(xsm, w_norm_final)
    return (y_prompt, y_sample,
            jnp.stack(cm_p), jnp.stack(gd_p), jnp.stack(ss_p), jnp.stack(cf_p),
            jnp.stack(cm_s), jnp.stack(gd_s), jnp.stack(ss_s), jnp.stack(cf_s))
```

```python
from contextlib import ExitStack
import numpy as np
import concourse.bass as bass
import concourse.mybir as mybir
from concourse.bass_utils import run_bass_kernel_spmd

F32 = mybir.dt.float32
BF16 = mybir.dt.bfloat16
AF = mybir.ActivationFunctionType
ALU = mybir.AluOpType
ENGS = ("pe", "act", "dve", "pool", "sp")

D = 1024
SEQ = 8192
DFF = 2816
CONV_CH = 7168
IN_COLS = 12336
OFF_GZ = 7168
OFF_SZ = 8192
OFF_SM = 10240
OFF_MG = 10288
EPS = 1e-6
NEG = -30000.0


class Sched:
    WINDOW = 32

    def __init__(self):
        self.trace = []
        self.final_queue = None
        self.tag = ""

    def op(self, eng, fn, r=(), w=(), cost=0.3):
        self.trace.append(dict(eng=eng, fn=fn, kind="c", key=None, r=tuple(r), w=tuple(w), cost=cost, tag=self.tag))

    def dma(self, queue, key, fn, r=(), w=(), cost=3.0):
        self.trace.append(dict(eng=queue, fn=fn, kind="d", key=key, r=tuple(r), w=tuple(w), cost=cost, tag=self.tag))

    def finish(self, queue):
        self.final_queue = queue

    def schedule(self):
        tr = self.trace
        n = len(tr)
        deps = [set() for _ in range(n)]
        state = {}
        dma_by_key = {}
        for i, o in enumerate(tr):
            d = deps[i]
            for nm in o["r"]:
                st = state.get(nm)
                if st and st[0] is not None:
                    d.add(st[0])
            for nm in o["w"]:
                st = state.get(nm)
                if st:
                    if st[0] is not None:
                        d.add(st[0])
                    d.update(st[1])
            for j in list(d):
                if tr[j]["kind"] == "d":
                    d.add(dma_by_key[tr[j]["key"]][-1])
            d.discard(i)
            for nm in o["r"]:
                st = state.setdefault(nm, [None, []])
                st[1].append(i)
            for nm in o["w"]:
                state[nm] = [i, []]
            if o["kind"] == "d":
                dma_by_key.setdefault(o["key"], []).append(i)
        self.deps = deps
        self.dma_by_key = dma_by_key
        users = [[] for _ in range(n)]
        ndep = [0] * n
        for i in range(n):
            ndep[i] = len(deps[i])
            for j in deps[i]:
                users[j].append(i)
        queues = {e: [] for e in ENGS}
        for i, o in enumerate(tr):
            queues[o["eng"]].append(i)
        ptr = {e: 0 for e in ENGS}
        window = {e: [] for e in ENGS}
        inorder = {"sp", "pool"}
        fin = [0.0] * n
        ready = [0.0] * n
        etime = {e: 0.0 for e in ENGS}
        order = {e: [] for e in ENGS}
        done = [False] * n

        def refill(e):
            w_ = window[e]
            q = queues[e]
            lim = 1 if e in inorder else self.WINDOW
            while len(w_) < lim and ptr[e] < len(q):
                w_.append(q[ptr[e]])
                ptr[e] += 1
        for e in ENGS:
            refill(e)
        remaining = n
        cur_tbl = [None]
        while remaining:
            best = None
            for e in ENGS:
                et = etime[e]
                for i in window[e]:
                    if ndep[i]:
                        continue
                    s_ = ready[i] if ready[i] > et else et
                    if e == "act":
                        tb = tr[i].get("tbl")
                        if tb is not None and tb != cur_tbl[0]:
                            s_ += 1.3
                    if best is None or s_ < best[0] - 1e-9 or (abs(s_ - best[0]) <= 1e-9 and i < best[1]):
                        best = (s_, i, e)
                    if e in inorder:
                        break
            s_, i, e = best
            o = tr[i]
            if o["kind"] == "d":
                etime[e] = s_ + 0.06
                fin[i] = s_ + o["cost"]
            else:
                etime[e] = s_ + o["cost"]
                fin[i] = s_ + o["cost"] + 0.15
                if e == "act" and o.get("tbl") is not None:
                    cur_tbl[0] = o["tbl"]
            done[i] = True
            order[e].append(i)
            window[e].remove(i)
            refill(e)
            for u in users[i]:
                ndep[u] -= 1
                if fin[i] > ready[u]:
                    ready[u] = fin[i]
            remaining -= 1
        self.order = order
        self.est_time = max(fin) if n else 0.0


def emit_program(nc, S, stack):
    S.schedule()
    tr = S.trace
    pos = {}
    for e in ENGS:
        for k, i in enumerate(S.order[e]):
            pos[i] = k
    dma_seq = {}
    for key, lst in S.dma_by_key.items():
        for k, i in enumerate(lst):
            dma_seq[i] = k + 1
    needed = {e: set() for e in ENGS}
    waits = {}
    for e in ENGS:
        waited = {}
        for i in S.order[e]:
            w_ = {}
            for j in S.deps[i]:
                oj = tr[j]
                if oj["kind"] == "d":
                    tgt = ("dma", oj["key"]); val = dma_seq[j]
                else:
                    tgt = oj["eng"]; val = pos[j] + 1
                    if e == "pe" and tgt == "pe":
                        continue
                if w_.get(tgt, 0) < val:
                    w_[tgt] = val
            out = {}
            for tgt, val in w_.items():
                if waited.get(tgt, 0) >= val:
                    continue
                waited[tgt] = val
                out[tgt] = val
                if not isinstance(tgt, tuple):
                    needed[tgt].add(val)
            waits[i] = out
    sems = {}
    for e in ENGS:
        sems[e] = stack.enter_context(nc.semaphore("s_" + e))
    for key in S.dma_by_key:
        sems[("dma", key)] = stack.enter_context(nc.semaphore("d_" + str(key)))
    cnt = {}
    for e in ENGS:
        c = 0
        m = {}
        for k, i in enumerate(S.order[e]):
            if tr[i]["kind"] == "c" and (k + 1) in needed[e]:
                c += 1
                m[k + 1] = c
        cnt[e] = m

    def wait_val(tgt, val):
        if isinstance(tgt, tuple):
            return 16 * val
        return cnt[tgt][val]

    block = stack.enter_context(nc.Block())

    def run(e):
        def body(h):
            for k, i in enumerate(S.order[e]):
                o = tr[i]
                for tgt, val in waits[i].items():
                    h.wait_ge(sems[tgt], wait_val(tgt, val))
                ins = o["fn"](h)
                if o["kind"] == "d":
                    ins.then_inc(sems[("dma", o["key"])], 16)
                elif (k + 1) in cnt[e]:
                    ins.then_inc(sems[e], 1)
            if e == S.final_queue:
                for key, lst in S.dma_by_key.items():
                    h.wait_ge(sems[("dma", key)], 16 * len(lst))
        return body

    block.tensor(run("pe"))
    block.scalar(run("act"))
    block.vector(run("dve"))
    block.gpsimd(run("pool"))
    block.sync(run("sp"))


def build(ntok_p, stages=99, dbg=False):
    nc = bass.Bass("TRN2", target_bir_lowering=False)
    S = Sched()
    st = ExitStack()
    with st:
        def din(name, shape, dt=F32):
            return nc.dram_tensor(name, list(shape), dt, kind="ExternalInput").ap()

        def dout(name, shape, dt=F32):
            return nc.dram_tensor(name, list(shape), dt, kind="ExternalOutput").ap()

        def dscr(name, shape, dt=BF16):
            return nc.dram_tensor(name, list(shape), dt, kind="Internal").ap()

        def sb(name, shape, dt):
            return st.enter_context(nc.sbuf_tensor(name + "_t", list(shape), dt))

        xp = din("xp", [ntok_p, D]); xs = din("xs", [64, D])
        cT_d = din("cT", [128, 16]); b_adaT_d = din("b_adaT", [128, 48]); wnT_d = din("wnT", [128, 16])
        wfin_d = din("wfin", [1, D])
        w_ada_d = din("w_ada", [D, 6 * D]); w_in_d = din("w_in", [D, IN_COLS])
        wcm_d = din("wcm", [128, 56 * 4]); bcm_d = din("bcm", [128, 56])
        wcf_d = din("wcf", [128, 44 * 3]); bcf_d = din("bcf", [128, 44])
        gpar_d = din("gpar", [1, 16]); spar_d = din("spar", [1, 64])
        dcol_d = din("dcol", [128, 16]); gnT_d = din("gnT", [128, 1]); snT_d = din("snT", [128, 16])
        w_bg_d = din("w_bg", [D, D]); w_bs_d = din("w_bs", [2 * D, D]); w_out_d = din("w_out", [D, D])
        w_up_d = din("w_up", [D, 2 * DFF]); w_down_d = din("w_down", [DFF, D])
        cm0_d = din("cm0T", [128, 56 * 3]); s0_d = din("s0", [128, 8 * 128]); h0_d = din("h0", [128, 32 * 64])
        cf0_d = din("cf0T", [128, 44 * 2])
        ident_d = din("ident", [128, 128]); lc_d = din("lc", [128, 64])
        triu_d = din("triu", [64, 64]); trisl_d = din("trisl", [64, 64])
        mut_d = din("mut", [128, 64]); msut_d = din("msut", [128, 64]); blkm_d = din("blkm", [64, 192])

        yp = dout("yp", [ntok_p, D]); ys = dout("ys", [64, D])
        outs = {}
        for sfx in ("p", "s"):
            outs["cm_" + sfx] = dout("cm_" + sfx, [3 * 56, 128])
            outs["gd_" + sfx] = dout("gd_" + sfx, [8, 128, 128])
            outs["ss_" + sfx] = dout("ss_" + sfx, [32, 128, 64])
            outs["cf_" + sfx] = dout("cf_" + sfx, [2 * 44, 128])
        dbg_o = dout("dbg", [128, 16384]) if dbg else None

        wi_s = dscr("wi_s", [D, IN_COLS])
        wbg_s = dscr("wbg_s", [D, D]); wbs_s = dscr("wbs_s", [2 * D, D]); wo_s = dscr("wo_s", [D, D])
        wu_s = dscr("wu_s", [D, 2 * DFF]); wd_s = dscr("wd_s", [DFF, D])

        x_sb = sb("x_sb", [128, 4, D], F32)
        hT = sb("hT", [128, 8, 512], BF16)
        XC = sb("XC", [128, 32, 512], BF16)
        oT = sb("oT", [128, 8, 512], BF16)
        yT = sb("yT", [128, 16, 512], BF16)
        wblk = sb("wblk", [128, 2, 8192], BF16)
        S_f = sb("S_f", [128, 8, 128], F32); S_b = sb("S_b", [128, 8, 128], BF16)
        H_f = sb("H_f", [128, 32, 64], F32); H_b = sb("H_b", [128, 32, 64], BF16)
        ARF = [sb("arF%d" % i, [128, 1024], F32) for i in range(3)]
        ARB = [sb("arB%d" % i, [128, 2048], BF16) for i in range(6)]
        stg = sb("stg", [128, 2, 516], BF16)
        YG = sb("YG", [128, 1024], F32)
        YGb = sb("YGb", [128, 1024], BF16)
        LCb = sb("LCb", [128, 64], BF16)
        carry = sb("carry", [128, 56, 3], BF16); carryf = sb("carryf", [128, 44, 2], BF16)
        identf = sb("identf", [128, 128], F32); identb = sb("identb", [128, 128], BF16)
        onesf = sb("onesf", [128, 128], F32); onesb = sb("onesb", [128, 128], BF16)
        LC = sb("LC", [128, 64], F32)
        triu = sb("triu", [64, 64], F32); trisl = sb("trisl", [64, 64], F32)
        blkm = sb("blkm", [64, 3, 64], F32)
        cT = sb("cT", [128, 16], F32); scT = sb("scT", [128, 16], BF16)
        b_adaT = sb("b_adaT", [128, 48], F32); wnT = sb("wnT", [128, 16], F32)
        modT = sb("modT", [128, 48, 2], F32)
        sc1 = sb("sc1", [128, 8], F32); sc2 = sb("sc2", [128, 8], F32)
        gtb = sb("gtb", [128, 2, D], F32)
        wfin = sb("wfin", [128, D], F32)
        wcm = sb("wcm", [128, 56, 4], F32); bcm = sb("bcm", [128, 56], F32)
        wcf = sb("wcf", [128, 44, 3], F32); bcf = sb("bcf", [128, 44], F32)
        gpar = sb("gpar", [64, 16], F32); spar = sb("spar", [64, 64], F32)
        nA = sb("nA", [64, 48], F32)
        bias48 = sb("bias48", [64, 48], F32)
        dcol = sb("dcol", [128, 16], F32); gnT = sb("gnT", [128, 1], F32); snT = sb("snT", [128, 16], F32)
        wsm = sb("wsm", [128, 8, 48], BF16)
        smT = sb("smT", [64, 8, 48], F32); smL = sb("smL", [64, 8, 48], F32); smG = sb("smG", [64, 8, 48], F32)
        smGAM = sb("smGAM", [64, 8, 48], F32); smREV = sb("smREV", [64, 8, 48], F32)
        smWJ = sb("smWJ", [64, 8, 32], F32); smGG = sb("smGG", [64, 8, 2, 8], F32)
        smB = sb("smB", [64, 8, 2, 8], F32)
        smGS = sb("smGS", [64, 8, 32], F32)
        ssq = sb("ssq", [128, 16], F32)
        cst = sb("cst", [128, 3 * 56], F32)
        psf = st.enter_context(nc.psum_tensor("psf", [128, 6, 512], F32))
        psb = st.enter_context(nc.psum_tensor("psb", [128, 2, 1024], BF16))

        def Bv(k, lo, n, parts=128):
            return ARB[k][:parts, lo:lo + n], ["arB%d_%d" % (k, o) for o in range((lo // 512) * 512, lo + n, 512)]

        def Fv(k, lo, n, parts=128):
            return ARF[k][:parts, lo:lo + n], ["arF%d_%d" % (k, o) for o in range((lo // 512) * 512, lo + n, 512)]

        PF = lambda i: "psf%d" % i
        PB = lambda i: "psb%d" % i
        rot = {"f": 0, "b": 0}

        def fbank(n=1):
            if n == 1:
                i = rot["f"] % 6
                rot["f"] += 1
                return i
            i = ((rot["f"] + 1) // 2 * 2) % 6
            rot["f"] = i + 2
            return i

        def bbank():
            i = rot["b"] % 2
            rot["b"] += 1
            return i

        def fsz(ap):
            n_ = 1
            for d_ in ap.shape[1:]:
                n_ *= d_
            return n_

        def mm(out, lhsT, rhs, start, stop, r, w, tp=None):
            c_ = 0.07 + fsz(rhs) * (4 if rhs.dtype == F32 else 1) / 1800.0
            if tp is None:
                S.op("pe", lambda e: e.matmul(out, lhsT=lhsT, rhs=rhs, start=start, stop=stop), r, w, c_)
            else:
                S.op("pe", lambda e: e.matmul(out, lhsT=lhsT, rhs=rhs, start=start, stop=stop, tile_position=tp), r, w, c_)

        def tr(out, in_, idn, r, w):
            S.op("pe", lambda e: e.transpose(out, in_, idn), r, w, 0.1 + fsz(in_) * (4 if in_.dtype == F32 else 1) / 1800.0)

        def acti(out, in_, func, r, w, bias=None, scale=None, accum=None):
            kw = {}
            if bias is not None:
                kw["bias"] = bias
            if scale is not None:
                kw["scale"] = scale
            if accum is not None:
                kw["accum_out"] = accum
            S.op("act", lambda e: e.activation(out=out, in_=in_, func=func, **kw), r, w, 0.2 + fsz(out) / 1100.0)
            S.trace[-1]["tbl"] = {AF.Exp: "exp", AF.Ln: "exp", AF.Silu: "silu", AF.Sigmoid: "sigm"}.get(func, None)

        def tt(out, in0, in1, op, r, w, eng="dve"):
            S.op(eng, lambda e: e.tensor_tensor(out=out, in0=in0, in1=in1, op=op), r, w, 0.1 + fsz(out) / 900.0)

        def stt(out, in0, scalar, in1, op0, op1, r, w, eng="dve"):
            S.op(eng, lambda e: e.scalar_tensor_tensor(out=out, in0=in0, scalar=scalar, in1=in1, op0=op0, op1=op1), r, w, 0.1 + fsz(out) / 700.0)

        def ts(out, in0, s1, op0, r, w, s2=None, op1=None, eng="dve"):
            if op1 is None:
                S.op(eng, lambda e: e.tensor_scalar(out=out, in0=in0, scalar1=s1, scalar2=None, op0=op0), r, w, 0.1 + fsz(out) / 900.0)
            else:
                S.op(eng, lambda e: e.tensor_scalar(out=out, in0=in0, scalar1=s1, scalar2=s2, op0=op0, op1=op1), r, w, 0.1 + fsz(out) / 900.0)

        def cp(out, in_, r, w, eng="dve"):
            if eng == "act":
                S.op("act", lambda e: e.activation(out=out, in_=in_, func=AF.Copy), r, w, 0.2 + fsz(out) / 1100.0)
            else:
                S.op(eng, lambda e: e.tensor_copy(out=out, in_=in_), r, w, 0.1 + fsz(out) / 1500.0)

        def mset(ap, val, w, eng="dve"):
            S.op(eng, lambda e: e.memset(ap, val), (), w)

        def load(dst_ap, src_ap, names, key="ld", q="sp", r=()):
            S.dma(q, key, lambda e: e.dma_start(out=dst_ap, in_=src_ap), r=r, w=names)

        dbg_col = {"c": 0}

        def dump(ap, names, npart, ncol):
            if not dbg:
                return
            c0 = dbg_col["c"]
            dbg_col["c"] += ncol
            S.dma("pool", "dbg", lambda e: e.dma_start(out=dbg_o[0:npart, c0:c0 + ncol], in_=ap), r=names)
            return c0

        def cast(dst, src, rows, name, r=()):
            for r0 in range(0, rows, 128):
                S.dma("pool", "wc", lambda e, r0=r0: e.dma_start(out=dst[r0:r0 + 128, :], in_=src[r0:r0 + 128, :]), r=r, w=[name])

        for (t, d_, n) in ((identf, ident_d, "identf"), (triu, triu_d, "triu"), (trisl, trisl_d, "trisl"),
                           (cT, cT_d, "cT"), (b_adaT, b_adaT_d, "b_adaT"), (wnT, wnT_d, "wnT"),
                           (bcm, bcm_d, "bcm"), (bcf, bcf_d, "bcf"), (dcol, dcol_d, "dcol"), (gnT, gnT_d, "gnT"),
                           (snT, snT_d, "snT"), (LC, lc_d, "LC")):
            load(t[:], d_[:, :], [n])
        load(blkm[:].rearrange("p m i -> p (m i)"), blkm_d[:, :], ["blkm"])
        load(wcm[:].rearrange("p t i -> p (t i)"), wcm_d[:, :], ["wcm"])
        load(wcf[:].rearrange("p t i -> p (t i)"), wcf_d[:, :], ["wcf"])
        load(wfin[:], wfin_d.partition_broadcast(128).rearrange("p o n -> p (o n)"), ["wfin"])
        load(gpar[:], gpar_d.partition_broadcast(64).rearrange("p o n -> p (o n)"), ["gpar"])
        load(spar[:], spar_d.partition_broadcast(64).rearrange("p o n -> p (o n)"), ["spar"])
        S.dma("pool", "ldc", lambda e: e.dma_start(out=identb[:], in_=ident_d[:, :]), w=["identb"])
        S.dma("pool", "ldc", lambda e: e.dma_start(out=wsm[:], in_=w_in_d[:, OFF_SM:OFF_SM + 48].rearrange("(k p) n -> p k n", p=128)), w=["wsm"])
        mset(onesf[:], 1.0, ["onesf"])
        cp(LCb[:], LC[:], ["LC"], ["LCb"])
        for hh in range(16):
            S.dma("pool", "ldc", lambda e, hh=hh: e.dma_start(out=YGb[64:128, hh * 64:(hh + 1) * 64], in_=mut_d[64:128, :]), w=["YGblo"])
        mset(onesb[:], 1.0, ["onesb"])
        mset(bias48[:], 0.0, ["bias48"])
        cp(bias48[:, 8:16], gpar[:, 8:16], ["gpar"], ["bias48"])
        cp(bias48[:, 16:48], spar[:, 32:64], ["spar"], ["bias48"])
        acti(nA[:, 8:16], gpar[:, 0:8], AF.Exp, ["gpar"], ["nA"])
        acti(nA[:, 16:48], spar[:, 0:32], AF.Exp, ["spar"], ["nA"])
        ts(nA[:, 8:48], nA[:, 8:48], -1.0, ALU.mult, ["nA"], ["nA"])

        wst = {"i": 0}

        def wload(scr, srcname, c0, ncol, nk, k0=0):
            b = wst["i"] % 2
            wst["i"] += 1
            view = wblk[:, b, 0:nk * ncol].rearrange("p (k n) -> p k n", k=nk)
            src = scr[k0 * 128:(k0 + nk) * 128, c0:c0 + ncol].rearrange("(k p) n -> p k n", p=128)
            S.dma("sp", "w%d" % b, lambda e: e.dma_start(out=view, in_=src), r=[srcname], w=["wblk%d" % b], cost=3.0 + nk * ncol / 1500.0)
            return view, "wblk%d" % b

        acti(scT[:], cT[:], AF.Silu, ["cT"], ["scT"])
        pb_ada = fbank()
        for blk in range(6):
            b_ = wst["i"] % 2
            wst["i"] += 1
            wv = wblk[:, b_, 0:8192].rearrange("p (k n) -> p k n", k=8)
            wn = "wblk%d" % b_
            S.dma("pool", "w%d" % b_, lambda e, wv=wv, blk=blk: e.dma_start(out=wv, in_=w_ada_d[:, blk * 1024:(blk + 1) * 1024].rearrange("(k p) n -> p k n", p=128)),
                  w=[wn], cost=12.0)
            for jj in range(8):
                j = blk * 8 + jj
                for kc in range(8):
                    mm(psf[:, pb_ada, 2 * j:2 * j + 2], wv[:, kc, jj * 128:(jj + 1) * 128], scT[:, 2 * kc:2 * kc + 2],
                       kc == 0, kc == 7, [wn, "scT"], [PF(pb_ada)])
        tt(modT[:], psf[:, pb_ada, 0:96].rearrange("p (j s) -> p j s", s=2), b_adaT[:].unsqueeze(2).to_broadcast([128, 48, 2]),
           ALU.add, [PF(pb_ada), "b_adaT"], ["modT"])
        cast(wi_s, w_in_d, D, "wi_s")
        cast(wbg_s, w_bg_d, D, "wbg_s", r=["wi_s"]); cast(wbs_s, w_bs_d, 2 * D, "wbs_s", r=["wi_s"]); cast(wo_s, w_out_d, D, "wo_s", r=["wi_s"])
        cast(wu_s, w_up_d, D, "wu_s", r=["wi_s"]); cast(wd_s, w_down_d, DFF, "wd_s", r=["wi_s"])

        def stream(sidx, x_d, y_d, ntok, T, has_init, sfx):
            NCH = T // 64
            TT = min(128, T)
            NT = T // TT
            nsc = ntok // T

            stt(sc1[:], modT[:, 8:16, sidx], 1.0, wnT[:, 0:8], ALU.add, ALU.mult, ["modT", "wnT"], ["sc1"])
            stt(sc2[:], modT[:, 32:40, sidx], 1.0, wnT[:, 8:16], ALU.add, ALU.mult, ["modT", "wnT"], ["sc2"])
            for gi, base in ((0, 16), (1, 40)):
                for kc in range(8):
                    dg, dgn = Fv(0, (kc % 2) * 512, 128)
                    ts(dg, identf[:], modT[:, base + kc, sidx:sidx + 1], ALU.mult, ["identf", "modT"], dgn)
                    pbk = fbank()
                    mm(psf[:, pbk, 0:128], onesf[:], dg, True, True, ["onesf"] + dgn, [PF(pbk)])
                    cp(gtb[:, gi, kc * 128:(kc + 1) * 128], psf[:, pbk, 0:128], [PF(pbk)], ["gtb"], eng="act")

            CARRY = ["carry%d" % t for t in range(56)]
            CARRYF = ["carryf%d" % t for t in range(44)]
            if not has_init:
                mset(S_f[:], 0.0, ["S_f"]); mset(S_b[:], 0.0, ["S_b"])
                for hh in range(2):
                    mset(H_f[:, hh * 16:(hh + 1) * 16, :], 0.0, ["H_f%d" % hh])
                    mset(H_b[:, hh * 16:(hh + 1) * 16, :], 0.0, ["H_b%d" % hh])
                mset(carry[:], 0.0, CARRY); mset(carryf[:], 0.0, CARRYF)
            else:
                load(S_f[:].rearrange("p h v -> p (h v)"), s0_d[:, :], ["S_f"])
                load(H_f[:].rearrange("p h v -> p (h v)"), h0_d[:, :], ["H_f0", "H_f1"])
                cp(S_b[:], S_f[:], ["S_f"], ["S_b"], eng="act")
                for hh in range(2):
                    cp(H_b[:, hh * 16:(hh + 1) * 16, :], H_f[:, hh * 16:(hh + 1) * 16, :], ["H_f%d" % hh], ["H_b%d" % hh], eng="act")
                S.dma("pool", "ldc", lambda e: e.dma_start(out=carry[:].rearrange("p t r -> p (t r)"), in_=cm0_d[:, :]), w=CARRY)
                S.dma("pool", "ldc", lambda e: e.dma_start(out=carryf[:].rearrange("p t r -> p (t r)"), in_=cf0_d[:, :]), w=CARRYF)

            def rms_to_hT(scv, shbase):
                S.tag = "rms"
                junk, junkn = Bv(2, 0, 1024, TT)
                for nt in range(NT):
                    acti(junk, x_sb[:TT, nt, :], AF.Square, [("x", nt)], junkn + ["ssq"], accum=ssq[:TT, nt:nt + 1])
                acti(ssq[:TT, 4:4 + NT], ssq[:TT, 0:NT], AF.Ln, ["ssq"], ["ssq"], bias=EPS, scale=1.0 / D)
                acti(ssq[:TT, 8:8 + NT], ssq[:TT, 4:4 + NT], AF.Exp, ["ssq"], ["ssq"], scale=-0.5)
                xns = []
                for nt in range(NT):
                    xn, xnn = Bv(nt // 2, (nt % 2) * 1024, 1024, TT)
                    acti(xn, x_sb[:TT, nt, :], AF.Copy, [("x", nt), "ssq"], xnn, scale=ssq[:TT, 8 + nt:9 + nt])
                    xns.append((xn, xnn))
                for kc in range(8):
                    bk = bbank()
                    for nt in range(NT):
                        xn, xnn = xns[nt]
                        tr(psb[:, bk, nt * TT:(nt + 1) * TT], xn[:, kc * 128:(kc + 1) * 128], identb[:TT, :TT], xnn + ["identb"], [PB(bk)])
                    acti(hT[:, kc, 0:T], psb[:, bk, 0:T], AF.Identity, [PB(bk), "modT", "sc1", "sc2"], ["hT"],
                         bias=modT[:, shbase + kc, sidx:sidx + 1], scale=scv[:, kc:kc + 1])

            def conv_tile(ps_ap, psn, wt, bt, ntap, cry, cryn, out_ap, outn, accslot, func, k):
                nb = ntap - 1
                sgi = k % 2
                sg = stg[:, sgi, :]
                sgn = "stg%d" % sgi
                acc, accl = Fv(accslot, (k % 2) * 512, T)
                accn = accl[0]
                cp(sg[:, 0:nb], cry, [cryn], [sgn], eng="dve")
                cp(sg[:, nb:nb + T], ps_ap, [psn], [sgn], eng="act")
                acti(acc, ps_ap, AF.Identity, [psn], [accn], bias=bt, scale=wt[:, nb:nb + 1])
                for i in range(nb - 1, -1, -1):
                    stt(acc, sg[:, i:i + T], wt[:, i:i + 1], acc, ALU.mult, ALU.add, [sgn, accn], [accn])
                if func is None:
                    cp(out_ap, acc, [accn], [outn], eng="act")
                else:
                    acti(out_ap, acc, func, [accn], [outn])
                cp(cry, sg[:, T:T + nb], [sgn], [cryn], eng="dve")

            def proj_conv(col0, ntiles, xc0, cch0):
                S.tag = "projconv"
                k = 0
                for b0 in range(0, ntiles, 4):
                    nb_ = min(4, ntiles - b0)
                    wv, wn = wload(wi_s, "wi_s", col0 + b0 * 128, nb_ * 128, 8)
                    for jj in range(nb_):
                        t = b0 + jj
                        pb = fbank()
                        for kc in range(8):
                            mm(psf[:, pb, 0:T], wv[:, kc, jj * 128:(jj + 1) * 128], hT[:, kc, 0:T], kc == 0, kc == 7, [wn, "hT"], [PF(pb)])
                        ct = cch0 + t
                        conv_tile(psf[:, pb, 0:T], PF(pb), wcm[:, ct, :], bcm[:, ct:ct + 1], 4, carry[:, ct, :], "carry%d" % ct,
                                  XC[:, xc0 + t, 0:T], ("XC", xc0 + t), 0, AF.Silu, k)
                        k += 1

            def l2norm_tiles(tiles, qscale_tiles):
                S.tag = "l2norm"
                for k, t in enumerate(tiles):
                    o = (k % 2) * 512
                    sq, sql = Bv(2, 1024 + o, T); sqn = sql[0]
                    acti(sq, XC[:, t, 0:T], AF.Square, [("XC", t)], [sqn])
                    pb = fbank()
                    mm(psf[:, pb, 0:T], onesb[:], sq, True, True, ["onesb", sqn], [PF(pb)])
                    ln, lnl = Fv(1, o, T); lnn = lnl[0]
                    acti(ln, psf[:, pb, 0:T], AF.Ln, [PF(pb)], [lnn], bias=EPS)
                    rs, rsl = Bv(3, o, T); rsn = rsl[0]
                    if t in qscale_tiles:
                        acti(rs, ln, AF.Exp, [lnn], [rsn], scale=-0.5, bias=float(-0.5 * np.log(128.0)))
                    else:
                        acti(rs, ln, AF.Exp, [lnn], [rsn], scale=-0.5)
                    tt(XC[:, t, 0:T], XC[:, t, 0:T], rs, ALU.mult, [("XC", t), rsn], [("XC", t)])

            def smalls():
                S.tag = "smalls"
                pb = fbank()
                for c in range(NCH):
                    for kc in range(8):
                        mm(psf[:64, pb, c * 48:(c + 1) * 48], hT[:, kc, c * 64:(c + 1) * 64], wsm[:, kc, :], kc == 0, kc == 7, ["hT", "wsm"], [PF(pb)])
                n3 = lambda a: a[:, 0:NCH, :]
                tt(n3(smT), psf[:64, pb, 0:NCH * 48].rearrange("p (c n) -> p c n", n=48), bias48[:].unsqueeze(1).to_broadcast([64, NCH, 48]),
                   ALU.add, [PF(pb), "bias48"], ["smT"])
                ts(smT[:, 0:NCH, 0:8], smT[:, 0:NCH, 0:8], -1.0, ALU.mult, ["smT"], ["smT"])
                acti(n3(smT), n3(smT), AF.Exp, ["smT"], ["smT"])
                acti(n3(smL), n3(smT), AF.Ln, ["smT"], ["smL"], bias=1.0)
                tt(smG[:, 0:NCH, 8:48], smL[:, 0:NCH, 8:48], nA[:, 8:48].unsqueeze(1).to_broadcast([64, NCH, 40]), ALU.mult, ["smL", "nA"], ["smG"])
                mset(smG[:, 0:NCH, 0:8], 0.0, ["smG"])
                pg = fbank()
                mm(psf[:64, pg, 0:NCH * 48], triu[:], smG[:, 0:NCH, :].rearrange("p c n -> p (c n)"), True, True, ["triu", "smG"], [PF(pg)])
                cp(n3(smGAM), psf[:64, pg, 0:NCH * 48].rearrange("p (c n) -> p c n", n=48), [PF(pg)], ["smGAM"])
                pr = fbank()
                mm(psf[:64, pr, 0:NCH * 48], trisl[:], smG[:, 0:NCH, :].rearrange("p c n -> p (c n)"), True, True, ["trisl", "smG"], [PF(pr)])
                acti(n3(smREV), psf[:64, pr, 0:NCH * 48].rearrange("p (c n) -> p c n", n=48), AF.Exp, [PF(pr)], ["smREV"])
                tt(smWJ[:, 0:NCH, :], smREV[:, 0:NCH, 16:48], smL[:, 0:NCH, 16:48], ALU.mult, ["smREV", "smL"], ["smWJ"])
                cp(smGG[:, 0:NCH, 0, :], smGAM[:, 0:NCH, 8:16], ["smGAM"], ["smGG"])
                tt(smGG[:, 0:NCH, 1, :], smGAM[:, 0:NCH, 8:16], smL[:, 0:NCH, 0:8], ALU.subtract, ["smGAM", "smL"], ["smGG"])
                acti(smB[:, 0:NCH, 0, :], smL[:, 0:NCH, 0:8], AF.Exp, ["smL"], ["smB"], scale=-1.0)
                acti(smB[:, 0:NCH, 1, :], smGG[:, 0:NCH, 1, :], AF.Exp, ["smGG"], ["smB"])
                acti(smGS[:, 0:NCH, :], smL[:, 0:NCH, 16:48], AF.Ln, ["smL"], ["smGS"])
                tt(smGS[:, 0:NCH, :], smGAM[:, 0:NCH, 16:48], smGS[:, 0:NCH, :], ALU.subtract, ["smGAM", "smGS"], ["smGS"])

            def gdn_chunk(c, precise):
                S.tag = "gdnG1"
                cs = slice(c * 64, (c + 1) * 64)
                qT = lambda h: XC[:, h, cs]
                kT = lambda h: XC[:, 8 + h, cs]
                vT = lambda h: XC[:, 16 + h, cs]
                QN = [("XC", h) for h in range(8)]; KN = [("XC", 8 + h) for h in range(8)]; VN = [("XC", 16 + h) for h in range(8)]
                h8 = lambda ap: ap.rearrange("p (h k) -> p h k", h=8)
                bk = bbank()
                for h in range(8):
                    tr(psb[:64, bk, h * 128:(h + 1) * 128], kT(h), identb[:, :], [KN[h], "identb"], [PB(bk)])
                kd_, kdn = Bv(3, 0, 1024, 64); kd = h8(kd_)
                tt(kd, h8(psb[:64, bk, :]), smREV[:, c, 8:16].unsqueeze(2).to_broadcast([64, 8, 128]), ALU.mult, [PB(bk), "smREV"], kdn)
                bv_ = bbank()
                for h in range(8):
                    tr(psb[:64, bv_, h * 128:(h + 1) * 128], vT(h), identb[:, :], [VN[h], "identb"], [PB(bv_)])
                bvt_, bvn = Bv(3, 1024, 1024, 64); bvt = h8(bvt_)
                tt(bvt, h8(psb[:64, bv_, :]), smB[:, c, 0, :].unsqueeze(2).to_broadcast([64, 8, 128]), ALU.mult, [PB(bv_), "smB"], bvn)
                tt(YG[0:64, :].rearrange("p (a h i) -> p a h i", a=2, h=8), identf[0:64, 0:64].unsqueeze(1).unsqueeze(1).to_broadcast([64, 2, 8, 64]),
                   smGG[:, c, :, :].unsqueeze(3).to_broadcast([64, 2, 8, 64]), ALU.mult, ["identf", "smGG"], ["YGhi"])
                r0 = fbank(2)
                mm(psf[:64, r0, :], LC[:, :], YG[:, 0:512], True, True, ["LC", "YGhi", "YGlo"], [PF(r0)])
                mm(psf[:64, r0 + 1, :], LC[:, :], YG[:, 512:1024], True, True, ["LC", "YGhi", "YGlo"], [PF(r0 + 1)])
                rb = fbank()
                mm(psf[:, rb, :], onesf[0:64, :], YG[0:64, 0:512], True, True, ["onesf", "YGhi"], [PF(rb)])
                Eb, Ebn = Fv(2, 0, 512)
                acti(Eb, psf[:, rb, :], AF.Exp, [PF(rb)], Ebn)
                X, Xn = Fv(1, 0, 1024, 64)
                tt(X.rearrange("p (a h i) -> p a h i", a=2, h=8), psf[:64, r0:r0 + 2, :].rearrange("p a (h i) -> p a h i", h=8),
                   smGG[:, c, 0, :].unsqueeze(1).unsqueeze(3).to_broadcast([64, 2, 8, 64]), ALU.subtract, [PF(r0), PF(r0 + 1), "smGG"], Xn)
                acti(X, X, AF.Exp, Xn, Xn)
                k0 = fbank(2)
                for h in range(8):
                    mm(psf[:64, k0, h * 64:(h + 1) * 64], kT(h), qT(h), True, True, [KN[h], QN[h]], [PF(k0)])
                for h in range(8):
                    mm(psf[:64, k0 + 1, h * 64:(h + 1) * 64], kT(h), kT(h), True, True, [KN[h]], [PF(k0 + 1)])
                Aqk_, Aqkn = Bv(0, 1024, 512, 64)
                tt(Aqk_, psf[:64, k0, :], X[:, 0:512], ALU.mult, [PF(k0), Xn[0]], Aqkn)
                if precise:
                    LTf = X[:, 512:1024]; LTn = [Xn[1]]
                    tt(LTf, psf[:64, k0 + 1, :], LTf, ALU.mult, [PF(k0 + 1)] + LTn, LTn)
                    Aqk = lambda h: Aqk_[:, h * 64:(h + 1) * 64]
                    qTe, qTen = Bv(5, 0, 512)
                    tt(qTe.rearrange("p (h i) -> p h i", h=8), XC[:, 0:8, cs], Eb.rearrange("p (h i) -> p h i", h=8), ALU.mult, QN + Ebn, qTen)
                    bl = fbank()
                    for h in range(8):
                        hs = slice(h * 64, (h + 1) * 64)
                        tr(psf[:64, bl, hs], LTf[:, hs], identf[:64, :64], LTn + ["identf"], [PF(bl)])
                    Lf = X[:, 0:512]; Lfn = [Xn[0]]
                    cp(Lf, psf[:64, bl, :], [PF(bl)], Lfn, eng="act")
                    def f32v(k, half):
                        ap_, nm = Bv(k, half * 1024, 1024, 64)
                        return ap_.bitcast(F32), nm
                    v8 = lambda ap_: ap_.rearrange("p (h i) -> p h i", h=8)
                    mk = lambda m: blkm[:, m, :].unsqueeze(1).to_broadcast([64, 8, 64])
                    idb = identf[0:64, 0:64].unsqueeze(1).to_broadcast([64, 8, 64])
                    Ld, Ldn = f32v(1, 0); LdT, LdTn = f32v(1, 1)
                    L1, L1n = f32v(2, 0); L1T, L1Tn = f32v(2, 1)
                    L2, L2n = Fv(2, 512, 512, 64)
                    tt(v8(Ld), v8(Lf), mk(0), ALU.mult, Lfn + ["blkm"], Ldn)
                    tt(v8(LdT), v8(LTf), mk(0), ALU.mult, LTn + ["blkm"], LdTn)
                    tt(v8(L1), v8(Lf), mk(1), ALU.mult, Lfn + ["blkm"], L1n)
                    tt(v8(L1T), v8(LTf), mk(1), ALU.mult, LTn + ["blkm"], L1Tn)
                    tt(v8(L2), v8(Lf), mk(2), ALU.mult, Lfn + ["blkm"], L2n)
                    Pb = [(Fv(0, 0, 512, 64), Fv(0, 512, 512, 64)), (f32v(4, 0), f32v(4, 1))]
                    stt(v8(Pb[0][0][0]), v8(Ld), -1.0, idb, ALU.mult, ALU.add, Ldn + ["identf"], Pb[0][0][1])
                    stt(v8(Pb[0][1][0]), v8(LdT), -1.0, idb, ALU.mult, ALU.add, LdTn + ["identf"], Pb[0][1][1])
                    pairs = [((Ld, Ldn), (LdT, LdTn)), ((Lf, Lfn), (LTf, LTn))]

                else:
                    LTf, LTn = Bv(0, 0, 512, 64)
                    tt(LTf, psf[:64, k0 + 1, :], X[:, 512:1024], ALU.mult, [PF(k0 + 1), Xn[1]], LTn)
                    Aqk = lambda h: Aqk_[:, h * 64:(h + 1) * 64]
                    qTe, qTen = Bv(5, 0, 512)
                    tt(qTe.rearrange("p (h i) -> p h i", h=8), XC[:, 0:8, cs], Eb.rearrange("p (h i) -> p h i", h=8), ALU.mult, QN + Ebn, qTen)
                    bl = bbank()
                    for h in range(8):
                        hs = slice(h * 64, (h + 1) * 64)
                        tr(psb[:64, bl, hs], LTf[:, hs], identb[:64, :64], LTn + ["identb"], [PB(bl)])
                    Lf, Lfn = Bv(0, 512, 512, 64)
                    cp(Lf, psb[:64, bl, 0:512], [PB(bl)], Lfn, eng="act")
                    v8 = lambda ap_: ap_.rearrange("p (h i) -> p h i", h=8)
                    mk = lambda m: blkm[:, m, :].unsqueeze(1).to_broadcast([64, 8, 64])
                    idb = identb[0:64, 0:64].unsqueeze(1).to_broadcast([64, 8, 64])
                    Ld, Ldn = Bv(1, 0, 512, 64); LdT, LdTn = Bv(1, 512, 512, 64)
                    L1, L1n = Bv(1, 1024, 512, 64); L1T, L1Tn = Bv(1, 1536, 512, 64)
                    L2, L2n = Bv(0, 1536, 512, 64)
                    tt(v8(Ld), v8(Lf), mk(0), ALU.mult, Lfn + ["blkm"], Ldn)
                    tt(v8(LdT), v8(LTf), mk(0), ALU.mult, LTn + ["blkm"], LdTn)
                    tt(v8(L1), v8(Lf), mk(1), ALU.mult, Lfn + ["blkm"], L1n)
                    tt(v8(L1T), v8(LTf), mk(1), ALU.mult, LTn + ["blkm"], L1Tn)
                    tt(v8(L2), v8(Lf), mk(2), ALU.mult, Lfn + ["blkm"], L2n)
                    Pb = [(Bv(2, 0, 512, 64), Bv(2, 512, 512, 64)), (Bv(2, 1024, 512, 64), Bv(2, 1536, 512, 64))]
                    stt(v8(Pb[0][0][0]), v8(Ld), -1.0, idb, ALU.mult, ALU.add, Ldn + ["identb"], Pb[0][0][1])
                    stt(v8(Pb[0][1][0]), v8(LdT), -1.0, idb, ALU.mult, ALU.add, LdTn + ["identb"], Pb[0][1][1])
                    pairs = [((Ld, Ldn), (LdT, LdTn)), ((Lf, Lfn), (LTf, LTn))]

                def grp(lhs, lhsn, rhs, rhsn):
                    bnk = fbank()
                    for h in range(8):
                        hs = slice(h * 64, (h + 1) * 64)
                        mm(psf[:64, bnk, hs], lhs[:, hs], rhs[:, hs], True, True, lhsn + rhsn, [PF(bnk)])
                    return bnk
                cur = 0
                pc = 0
                for lvl in range(1, 4):
                    (A_, A_n), (AT_, AT_n) = pairs[cur]
                    (nA_, nAn), (nAT, nATn) = pairs[1 - cur]
                    a0 = grp(AT_, AT_n, A_, A_n)
                    a1 = grp(A_, A_n, AT_, AT_n)
                    cp(nA_, psf[:64, a0, :], [PF(a0)], nAn, eng="act")
                    cp(nAT, psf[:64, a1, :], [PF(a1)], nATn, eng="dve")
                    (P_, P_n), (PT_, PT_n) = Pb[pc]
                    (P2, P2n), (PT2, PT2n) = Pb[1 - pc]
                    p0 = grp(nAT, nATn, P_, P_n)
                    p1 = grp(nA_, nAn, PT_, PT_n)
                    tt(P2, psf[:64, p0, :], P_, ALU.add, [PF(p0)] + P_n, P2n)
                    tt(PT2, psf[:64, p1, :], PT_, ALU.add, [PF(p1)] + PT_n, PT2n)
                    cur = 1 - cur
                    pc = 1 - pc
                (Dm_, Dn), (DT_, DTn) = Pb[pc]
                (E_, En_), (ET_, ETn) = Pb[1 - pc]
                (U_, Un), (UT_, UTn) = pairs[0]
                g0 = grp(L1T, L1Tn, Dm_, Dn)
                g1 = grp(L1, L1n, DT_, DTn)
                cp(U_, psf[:64, g0, :], [PF(g0)], Un, eng="act")
                cp(UT_, psf[:64, g1, :], [PF(g1)], UTn, eng="dve")
                g2 = grp(DT_, DTn, U_, Un)
                g3 = grp(Dm_, Dn, UT_, UTn)
                tt(E_, Dm_, psf[:64, g2, :], ALU.subtract, Dn + [PF(g2)], En_)
                tt(ET_, DT_, psf[:64, g3, :], ALU.subtract, DTn + [PF(g3)], ETn)
                g4 = grp(L2, L2n, ET_, ETn)
                cp(UT_, psf[:64, g4, :], [PF(g4)], UTn, eng="act")
                g5 = grp(E_, En_, UT_, UTn)
                MinvT, MinvTn = Bv(5, 512, 512, 64)
                tt(MinvT, ET_, psf[:64, g5, :], ALU.subtract, ETn + [PF(g5)], MinvTn)
                S.tag = "gdnG2"
                s0 = fbank(2)
                for h in range(8):
                    mm(psf[:64, s0 + h // 4, (h % 4) * 128:(h % 4 + 1) * 128], kT(h), S_b[:, h, :], True, True, [KN[h], "S_b"], [PF(s0 + h // 4)])
                vb_, vbn = Bv(4, 0, 1024, 64); vb = h8(vb_)
                tt(vb, psf[:64, s0:s0 + 2, :].rearrange("p a (h v) -> p (a h) v", h=4), smB[:, c, 1, :].unsqueeze(2).to_broadcast([64, 8, 128]),
                   ALU.mult, [PF(s0), PF(s0 + 1), "smB"], vbn)
                tt(vb, bvt, vb, ALU.subtract, bvn + vbn, vbn)
                u0 = fbank(2)
                for h in range(8):
                    mm(psf[:64, u0 + h // 4, (h % 4) * 128:(h % 4 + 1) * 128], MinvT[:, h * 64:(h + 1) * 64], vb[:, h, :], True, True,
                       MinvTn + vbn, [PF(u0 + h // 4)])
                u_, un = Bv(4, 1024, 1024, 64); u = h8(u_)
                cp(u_.rearrange("p (a n) -> p a n", a=2), psf[:64, u0:u0 + 2, :], [PF(u0), PF(u0 + 1)], un, eng="act")
                o0 = fbank()
                for h in range(8):
                    hs = slice(h * 64, (h + 1) * 64)
                    mm(psf[:, o0, hs], u[:, h, :], Aqk(h), True, False, un + Aqkn, [PF(o0)])
                    mm(psf[:, o0, hs], S_b[:, h, :], qTe[:, hs], False, True, ["S_b"] + qTen, [PF(o0)])
                cp(oT[:, :, cs], psf[:, o0, :].rearrange("p (h i) -> p h i", h=8), [PF(o0)], [("oT", h) for h in range(8)], eng="act")
                n0 = fbank(2)
                for h in range(8):
                    mm(psf[:, n0 + h // 4, (h % 4) * 128:(h % 4 + 1) * 128], kd[:, h, :], u[:, h, :], True, True, kdn + un, [PF(n0 + h // 4)])
                elast = Eb.rearrange("p (h i) -> p h i", h=8)[:, :, 63:64].to_broadcast([128, 8, 128])
                tt(S_f[:], S_f[:], elast, ALU.mult, ["S_f"] + Ebn, ["S_f"])
                tt(S_f[:], psf[:, n0:n0 + 2, :].rearrange("p a (h v) -> p (a h) v", h=4), S_f[:], ALU.add, [PF(n0), PF(n0 + 1), "S_f"], ["S_f"])
                cp(S_b[:], S_f[:], ["S_f"], ["S_b"], eng="act")

            def ssd_chunk(c):
                S.tag = "ssd"
                cs = slice(c * 64, (c + 1) * 64)
                xts = [Bv(0, 0, 1024, 64), Bv(0, 1024, 1024, 64)]
                for half in range(2):
                    bk = bbank()
                    for j in range(8):
                        t = half * 8 + j
                        tr(psb[:64, bk, j * 128:(j + 1) * 128], XC[:, t, cs], identb[:, :], [("XC", t), "identb"], [PB(bk)])
                    cp(xts[half][0], psb[:64, bk, :], [PB(bk)], xts[half][1], eng="act")
                bk = bbank()
                for g in range(8):
                    tr(psb[:64, bk, g * 128:(g + 1) * 128], XC[:, 16 + g, cs], identb[:, :], [("XC", 16 + g), "identb"], [PB(bk)])
                B_tok, Btn = Bv(1, 0, 1024, 64)
                cp(B_tok, psb[:64, bk, :], [PB(bk)], Btn, eng="dve")
                cb = fbank()
                for g in range(8):
                    mm(psf[:64, cb, g * 64:(g + 1) * 64], XC[:, 16 + g, cs], XC[:, 24 + g, cs], True, True, [("XC", 16 + g), ("XC", 24 + g)], [PF(cb)])
                cbs, cbsn = Fv(0, 0, 512, 64)
                cp(cbs, psf[:64, cb, :], [PF(cb)], cbsn, eng="act")
                for half in range(2):
                    h0 = half * 16
                    x_tok, xtn = xts[half]
                    tt(YGb[0:64, :].rearrange("p (h i) -> p h i", h=16), triu[:, :].unsqueeze(1).to_broadcast([64, 16, 64]),
                       smG[:, c, 16 + h0:32 + h0].unsqueeze(2).to_broadcast([64, 16, 64]), ALU.mult, ["triu", "smG"], ["YGbhi"])
                    rm = fbank(2)
                    mm(psf[:64, rm, :], LCb[:, :], YGb[:, 0:512], True, True, ["LCb", "YGbhi", "YGblo"], [PF(rm)])
                    mm(psf[:64, rm + 1, :], LCb[:, :], YGb[:, 512:1024], True, True, ["LCb", "YGbhi", "YGblo"], [PF(rm + 1)])
                    r1 = fbank(2)
                    mm(psf[:, r1, :], onesb[0:64, :], YGb[0:64, 0:512], True, True, ["onesb", "YGbhi"], [PF(r1)])
                    mm(psf[:, r1 + 1, :], onesb[0:64, :], YGb[0:64, 512:1024], True, True, ["onesb", "YGbhi"], [PF(r1 + 1)])
                    E128, E128n = Fv(1, 0, 1024)
                    acti(E128.rearrange("p (a n) -> p a n", a=2), psf[:, r1:r1 + 2, :], AF.Exp, [PF(r1), PF(r1 + 1)], E128n)
                    CTe, CTen = Bv(2, 1024, 1024)
                    tt(CTe.rearrange("p (g e i) -> p g e i", g=4, e=4), XC[:, 24 + 4 * half:28 + 4 * half, cs].unsqueeze(2).to_broadcast([128, 4, 4, 64]),
                       E128.rearrange("p (g e i) -> p g e i", g=4, e=4), ALU.mult, [("XC", 24 + 4 * half + g) for g in range(4)] + E128n, CTen)
                    Dm, Dmn = Fv(2, 0, 1024, 64)
                    tt(Dm.rearrange("p (h i) -> p h i", h=16), psf[:64, rm:rm + 2, :].rearrange("p a (h i) -> p (a h) i", h=8),
                       smGS[:, c, h0:h0 + 16].unsqueeze(2).to_broadcast([64, 16, 64]), ALU.subtract, [PF(rm), PF(rm + 1), "smGS"], Dmn)
                    ex, exn = Bv(3, 1024, 1024, 64)
                    acti(ex, Dm, AF.Exp, Dmn, exn)
                    wT, wTn = Bv(3, 0, 1024, 64)
                    tt(wT.rearrange("p (g e i) -> p g e i", g=4, e=4),
                       cbs[:, 256 * half:256 * half + 256].rearrange("p (g i) -> p g i", g=4).unsqueeze(2).to_broadcast([64, 4, 4, 64]),
                       ex.rearrange("p (g e i) -> p g e i", g=4, e=4), ALU.mult, cbsn + exn, wTn)
                    xw, xwn = Bv(1, 1024, 1024, 64)
                    tt(xw.rearrange("p (h q) -> p h q", h=16), x_tok.rearrange("p (h q) -> p h q", h=16),
                       smWJ[:, c, h0:h0 + 16].unsqueeze(2).to_broadcast([64, 16, 64]), ALU.mult, xtn + ["smWJ"], xwn)
                    yb = fbank()
                    for hp in range(8):
                        for e_ in range(2):
                            hl = 2 * hp + e_
                            hg = h0 + hl
                            o_ap = psf[64 * e_:64 * e_ + 64, yb, hp * 64:(hp + 1) * 64]
                            mm(o_ap, x_tok[:, hl * 64:(hl + 1) * 64], wT[:, hl * 64:(hl + 1) * 64], True, False, xtn + wTn, [PF(yb)], tp=(0, 64 * e_))
                            mm(o_ap, H_b[:, hg, :], CTe[:, hl * 64:(hl + 1) * 64], False, True, ["H_b%d" % half] + CTen, [PF(yb)], tp=(0, 64 * e_))
                    cp(yT[:, 8 * half:8 * half + 8, cs], psf[:, yb, :].rearrange("p (h i) -> p h i", h=8), [PF(yb)],
                       [("yT", 8 * half + j) for j in range(8)], eng="act")
                    hb = fbank(2)
                    for gl in range(4):
                        g = 4 * half + gl
                        mm(psf[:, hb + gl // 2, (gl % 2) * 256:(gl % 2 + 1) * 256], B_tok[:, g * 128:(g + 1) * 128], xw[:, gl * 256:(gl + 1) * 256],
                           True, True, Btn + xwn, [PF(hb + gl // 2)])
                    Hh = H_f[:, h0:h0 + 16, :]
                    elast = E128.rearrange("p (h i) -> p h i", h=16)[:, :, 63:64].to_broadcast([128, 16, 64])
                    tt(Hh, Hh, elast, ALU.mult, ["H_f%d" % half] + E128n, ["H_f%d" % half])
                    tt(Hh, psf[:, hb:hb + 2, :].rearrange("p a (h q) -> p (a h) q", h=8), Hh, ALU.add, [PF(hb), PF(hb + 1), "H_f%d" % half], ["H_f%d" % half])
                    cp(H_b[:, h0:h0 + 16, :], Hh, ["H_f%d" % half], ["H_b%d" % half], eng="act")

            def zproj_tile(col, k):
                raise NotImplementedError

            def rms_feat(sq_list, ktiles, scale):
                pb = fbank()
                for i, (sq, sqn) in enumerate(sq_list):
                    mm(psf[:, pb, 0:T], onesb[:], sq, i == 0, i == len(sq_list) - 1, ["onesb", sqn], [PF(pb)])
                return pb

            def gdn_finalize():
                S.tag = "gdnfin"
                for h in range(8):
                    o = (h % 2) * 512
                    sq, sql = Bv(2, 1024 + o, T); sqn = sql[0]
                    acti(sq, oT[:, h, 0:T], AF.Square, [("oT", h)], [sqn])
                    pb = rms_feat([(sq, sqn)], 1, 1.0)
                    ln, lnl = Fv(1, o, T); lnn = lnl[0]
                    acti(ln, psf[:, pb, 0:T], AF.Ln, [PF(pb)], [lnn], bias=EPS, scale=1.0 / 128)
                    rs, rsl = Bv(3, o, T); rsn = rsl[0]
                    acti(rs, ln, AF.Exp, [lnn], [rsn], scale=-0.5)
                    tt(oT[:, h, 0:T], oT[:, h, 0:T], rs, ALU.mult, [("oT", h), rsn], [("oT", h)])
                wv, wn = wload(wi_s, "wi_s", OFF_GZ, 1024, 8)
                for h in range(8):
                    o = (h % 2) * 512
                    pz = fbank()
                    for kc in range(8):
                        mm(psf[:, pz, 0:T], wv[:, kc, h * 128:(h + 1) * 128], hT[:, kc, 0:T], kc == 0, kc == 7, [wn, "hT"], [PF(pz)])
                    zs, zsl = Bv(4, o, T); zsn = zsl[0]
                    acti(zs, psf[:, pz, 0:T], AF.Silu, [PF(pz)], [zsn])
                    stt(oT[:, h, 0:T], oT[:, h, 0:T], gnT[:, 0:1], zs, ALU.mult, ALU.mult, [("oT", h), "gnT", zsn], [("oT", h)])

            def ssd_finalize():
                S.tag = "ssdfin"
                for blk in range(2):
                    wv, wn = wload(wi_s, "wi_s", OFF_SZ + blk * 1024, 1024, 8)
                    for jj in range(8):
                        hp = blk * 8 + jj
                        o = (hp % 2) * 512
                        pz = fbank()
                        for kc in range(8):
                            mm(psf[:, pz, 0:T], wv[:, kc, jj * 128:(jj + 1) * 128], hT[:, kc, 0:T], kc == 0, kc == 7, [wn, "hT"], [PF(pz)])
                        zs, zsl = Bv(4, o, T); zsn = zsl[0]
                        acti(zs, psf[:, pz, 0:T], AF.Silu, [PF(pz)], [zsn])
                        stt(yT[:, hp, 0:T], XC[:, hp, 0:T], dcol[:, hp:hp + 1], yT[:, hp, 0:T], ALU.mult, ALU.add, [("XC", hp), "dcol", ("yT", hp)], [("yT", hp)])
                        tt(yT[:, hp, 0:T], yT[:, hp, 0:T], zs, ALU.mult, [("yT", hp), zsn], [("yT", hp)])
                for g in range(8):
                    sqs = []
                    for e_ in range(2):
                        hp = 2 * g + e_
                        sq, sql = Bv(2, 1024 + e_ * 512, T); sqn = sql[0]
                        acti(sq, yT[:, hp, 0:T], AF.Square, [("yT", hp)], [sqn])
                        sqs.append((sq, sqn))
                    pb = rms_feat(sqs, 2, 1.0)
                    o = (g % 2) * 512
                    ln, lnl = Fv(1, o, T); lnn = lnl[0]
                    acti(ln, psf[:, pb, 0:T], AF.Ln, [PF(pb)], [lnn], bias=EPS, scale=1.0 / 256)
                    rs, rsl = Bv(3, o, T); rsn = rsl[0]
                    acti(rs, ln, AF.Exp, [lnn], [rsn], scale=-0.5)
                    for e_ in range(2):
                        hp = 2 * g + e_
                        stt(yT[:, hp, 0:T], yT[:, hp, 0:T], snT[:, hp:hp + 1], rs, ALU.mult, ALU.mult, [("yT", hp), "snT", rsn], [("yT", hp)])

            MX = lambda f: ARB[4 + f // 4][:, (f % 4) * 512:(f % 4) * 512 + T]
            MXN = lambda f: "arB%d_%d" % (4 + f // 4, (f % 4) * 512)

            def merge_out():
                S.tag = "merge"
                for blk in range(2):
                    wv, wn = wload(wi_s, "wi_s", OFF_MG + blk * 1024, 1024, 8)
                    for jj in range(8):
                        t = blk * 8 + jj
                        pz = fbank()
                        for kc in range(8):
                            mm(psf[:, pz, 0:T], wv[:, kc, jj * 128:(jj + 1) * 128], hT[:, kc, 0:T], kc == 0, kc == 7, [wn, "hT"], [PF(pz)])
                        acti(XC[:, t, 0:T], psf[:, pz, 0:T], AF.Sigmoid, [PF(pz)], [("XC", t)])
                wv, wn = wload(wbg_s, "wbg_s", 0, 1024, 8)
                for f in range(8):
                    pz = fbank()
                    for kc in range(8):
                        mm(psf[:, pz, 0:T], wv[:, kc, f * 128:(f + 1) * 128], oT[:, kc, 0:T], kc == 0, kc == 7, [wn, ("oT", kc)], [PF(pz)])
                    tt(MX(f), psf[:, pz, 0:T], XC[:, f, 0:T], ALU.mult, [PF(pz), ("XC", f)], [MXN(f)])
                for blk in range(2):
                    wv, wn = wload(wbs_s, "wbs_s", blk * 512, 512, 16)
                    for jj in range(4):
                        f = blk * 4 + jj
                        pz = fbank()
                        for kc in range(16):
                            mm(psf[:, pz, 0:T], wv[:, kc, jj * 128:(jj + 1) * 128], yT[:, kc, 0:T], kc == 0, kc == 15, [wn, ("yT", kc)], [PF(pz)])
                        tmp = ARB[3][:, (f % 2) * 512:(f % 2) * 512 + T]; tmpn = "arB3_%d" % ((f % 2) * 512)
                        tt(tmp, psf[:, pz, 0:T], XC[:, 8 + f, 0:T], ALU.mult, [PF(pz), ("XC", 8 + f)], [tmpn])
                        tt(MX(f), MX(f), tmp, ALU.add, [MXN(f), tmpn], [MXN(f)])
                wv, wn = wload(wo_s, "wo_s", 0, 1024, 8)
                for nt in range(NT):
                    for hf in range(2):
                        pz = fbank()
                        for kc in range(8):
                            mm(psf[:TT, pz, :], MX(kc)[:, nt * TT:(nt + 1) * TT], wv[:, kc, hf * 512:(hf + 1) * 512], kc == 0, kc == 7,
                               [wn, MXN(kc)], [PF(pz)])
                        tmp = ARF[1][:TT, 0:512] if hf == 0 else ARF[1][:TT, 512:1024]
                        tmpn = "arF1_%d" % (hf * 512)
                        tt(tmp, psf[:TT, pz, :], gtb[:TT, 0, hf * 512:(hf + 1) * 512], ALU.mult, [PF(pz), "gtb"], [tmpn])
                        tt(x_sb[:TT, nt, hf * 512:(hf + 1) * 512], x_sb[:TT, nt, hf * 512:(hf + 1) * 512], tmp, ALU.add, [("x", nt), tmpn], [("x", nt)])

            def ffn():
                S.tag = "ffn"
                k = 0
                for b0 in range(0, 22, 4):
                    nb_ = min(4, 22 - b0)
                    b = wst["i"] % 2
                    wst["i"] += 1
                    wn = "wblk%d" % b
                    view = wblk[:, b, 0:8 * 2 * nb_ * 128].rearrange("p (k a n) -> p k a n", k=8, a=2)
                    for a_ in range(2):
                        src = wu_s[:, a_ * DFF + b0 * 128:a_ * DFF + (b0 + nb_) * 128].rearrange("(k p) n -> p k n", p=128)
                        S.dma("sp", "w%d" % b, lambda e, a_=a_, src=src, view=view: e.dma_start(out=view[:, :, a_, :], in_=src), r=["wu_s"], w=[wn])
                    for jj in range(nb_):
                        j = b0 + jj
                        res = []
                        for a_ in range(2):
                            pz = fbank()
                            for kc in range(8):
                                mm(psf[:, pz, 0:T], view[:, kc, a_, jj * 128:(jj + 1) * 128], hT[:, kc, 0:T], kc == 0, kc == 7, [wn, "hT"], [PF(pz)])
                            ct = a_ * 22 + j
                            o = (k % 2) * 512
                            if a_ == 0:
                                dst = ARB[3][:, o:o + T]; dstn = "arB3_%d" % o
                                fn_ = AF.Silu
                            else:
                                dst = ARB[3][:, 1024 + o:1024 + o + T]; dstn = "arB3_%d" % (1024 + o)
                                fn_ = None
                            conv_tile(psf[:, pz, 0:T], PF(pz), wcf[:, ct, :], bcf[:, ct:ct + 1], 3, carryf[:, ct, :], "carryf%d" % ct,
                                      dst, dstn, 0 if a_ == 0 else 2, fn_, k)
                            res.append((dst, dstn))
                        tt(XC[:, j, 0:T], res[0][0], res[1][0], ALU.mult, [res[0][1], res[1][1]], [("XC", j)])
                        k += 1
                for cb_ in range(4):
                    wv, wn = wload(wd_s, "wd_s", cb_ * 256, 256, 22)
                    for nt in range(NT):
                        pz = fbank()
                        for kc in range(22):
                            mm(psf[:TT, pz, 0:256], XC[:, kc, nt * TT:(nt + 1) * TT], wv[:, kc, :], kc == 0, kc == 21, [wn, ("XC", kc)], [PF(pz)])
                        tmp = ARF[1][:TT, (nt % 2) * 512:(nt % 2) * 512 + 256]; tmpn = "arF1_%d" % ((nt % 2) * 512)
                        tt(tmp, psf[:TT, pz, 0:256], gtb[:TT, 1, cb_ * 256:(cb_ + 1) * 256], ALU.mult, [PF(pz), "gtb"], [tmpn])
                        tt(x_sb[:TT, nt, cb_ * 256:(cb_ + 1) * 256], x_sb[:TT, nt, cb_ * 256:(cb_ + 1) * 256], tmp, ALU.add, [("x", nt), tmpn], [("x", nt)])

            def final_norm_store(sc):
                S.tag = "final"
                junk, junkn = Bv(2, 0, 1024, TT)
                for nt in range(NT):
                    acti(junk, x_sb[:TT, nt, :], AF.Square, [("x", nt)], junkn + ["ssq"], accum=ssq[:TT, nt:nt + 1])
                acti(ssq[:TT, 4:4 + NT], ssq[:TT, 0:NT], AF.Ln, ["ssq"], ["ssq"], bias=EPS, scale=1.0 / D)
                acti(ssq[:TT, 8:8 + NT], ssq[:TT, 4:4 + NT], AF.Exp, ["ssq"], ["ssq"], scale=-0.5)
                for nt in range(NT):
                    yb_ = ARF[1 + nt % 2][:TT, :]; ybn = ["arF%d_0" % (1 + nt % 2), "arF%d_512" % (1 + nt % 2)]
                    acti(yb_, x_sb[:TT, nt, :], AF.Copy, [("x", nt), "ssq"], ybn, scale=ssq[:TT, 8 + nt:9 + nt])
                    tt(yb_, yb_, wfin[:TT, :], ALU.mult, ybn + ["wfin"], ybn)
                    r0 = sc * T + nt * TT
                    S.dma("pool", "yst", lambda e, yb_=yb_, r0=r0: e.dma_start(out=y_d[r0:r0 + TT, :], in_=yb_), r=ybn)

            def yg_masks(gdn):
                for hh in range(16):
                    src = mut_d if (not gdn or hh < 8) else msut_d
                    S.dma("sp", "ld", lambda e, hh=hh, src=src: e.dma_start(out=YG[64:128, hh * 64:(hh + 1) * 64], in_=src[64:128, :]), w=["YGlo"])

            for sc in range(nsc):
                for nt in range(NT):
                    r0 = sc * T + nt * TT
                    load(x_sb[:TT, nt, :], x_d[r0:r0 + TT, :], [("x", nt)], key="xld%d" % nt)
                rms_to_hT(sc1, 0)
                if stages >= 1:
                    proj_conv(0, 24, 0, 0)
                    l2norm_tiles(list(range(16)), set(range(8)))
                    smalls()
                if dbg and sc == nsc - 1 and stages == 1:
                    dump(hT[:, 0, 0:64], ["hT"], 128, 64)
                    dump(XC[:, 0, 0:64], [("XC", 0)], 128, 64)
                    dump(XC[:, 8, 0:64], [("XC", 8)], 128, 64)
                    dump(XC[:, 16, 0:64], [("XC", 16)], 128, 64)
                    dump(smL[:, 0, :], ["smL"], 64, 48)
                    dump(smGAM[:, 0, :], ["smGAM"], 64, 48)
                    dump(smREV[:, 0, :], ["smREV"], 64, 48)
                if stages >= 2:
                    if sc == 0:
                        yg_masks(True)
                    for c in range(NCH):
                        gdn_chunk(c, has_init)
                if dbg and sc == nsc - 1 and stages == 2:
                    for h in range(8):
                        dump(oT[:, h, T - 64:T], [("oT", h)], 128, 64)
                if stages >= 3:
                    gdn_finalize()
                    proj_conv(3072, 32, 0, 24)
                    for c in range(NCH):
                        ssd_chunk(c)
                if dbg and sc == nsc - 1 and stages == 3:
                    for h in range(16):
                        dump(yT[:, h, T - 64:T], [("yT", h)], 128, 64)
                if stages >= 4:
                    ssd_finalize()
                    merge_out()
                    rms_to_hT(sc2, 24)
                    ffn()
                final_norm_store(sc)

            S.dma("pool", "sst", lambda e: e.dma_start(out=outs["gd_" + sfx].rearrange("h k v -> k h v"), in_=S_f[:]), r=["S_f"])
            S.dma("pool", "sst", lambda e: e.dma_start(out=outs["ss_" + sfx].rearrange("h n q -> n h q"), in_=H_f[:]), r=["H_f0", "H_f1"])
            for (cr, names, ntile, nr, key) in ((carry, CARRY, 56, 3, "cm_"), (carryf, CARRYF, 44, 2, "cf_")):
                ncol = ntile * nr
                cp(cst[:, 0:ncol].rearrange("p (r t) -> p r t", r=nr), cr[:].rearrange("p t r -> p r t"), names, ["cst"])
                for c0 in range(0, ncol, 128):
                    n = min(128, ncol - c0)
                    pb = fbank()
                    S.op("pe", lambda e, pb=pb, c0=c0, n=n: e.transpose(psf[:n, pb, 0:128], cst[:, c0:c0 + n], identf[:, :]), ["cst", "identf"], [PF(pb)])
                    o_sb = ARF[1][:n, 0:128]
                    cp(o_sb, psf[:n, pb, 0:128], [PF(pb)], ["arF1_0"], eng="act")
                    S.dma("pool", "sst", lambda e, o_sb=o_sb, c0=c0, n=n, key=key: e.dma_start(out=outs[key + sfx][c0:c0 + n, :], in_=o_sb), r=["arF1_0"])

        stream(0, xp, yp, ntok_p, 512, False, "p")
        stream(1, xs, ys, 64, 64, True, "s")
        S.finish("pool")
        emit_program(nc, S, st)
    return nc


def _prep_shared(inp):
    f = np.float32
    A = lambda x: np.ascontiguousarray(x, dtype=f)
    sh = {}
    sh["w_ada"] = A(inp["w_ada"][0]); sh["w_in"] = A(inp["w_in"][0])
    sh["b_adaT"] = A(inp["b_ada"][0].reshape(48, 128).T)
    sh["wnT"] = A(np.concatenate([inp["w_norm_mix"][0].reshape(8, 128).T, inp["w_norm_ffn"][0].reshape(8, 128).T], axis=1))
    sh["wfin"] = A(inp["w_norm_final"].reshape(1, D))
    sh["wcm"] = A(inp["w_conv_mix"][0].reshape(4, 56, 128).transpose(2, 1, 0).reshape(128, 56 * 4))
    sh["bcm"] = A(inp["b_conv_mix"][0].reshape(56, 128).T)
    sh["wcf"] = A(inp["w_ffn_conv"][0].reshape(3, 44, 128).transpose(2, 1, 0).reshape(128, 44 * 3))
    sh["bcf"] = A(inp["b_ffn_conv"][0].reshape(44, 128).T)
    sh["gpar"] = A(np.concatenate([inp["gdn_a_log"][0], inp["gdn_dt_bias"][0]]).reshape(1, 16))
    sh["spar"] = A(np.concatenate([inp["ssd_a_log"][0], inp["ssd_dt_bias"][0]]).reshape(1, 64))
    sh["dcol"] = A(np.repeat(inp["ssd_d"][0], 64).reshape(16, 128).T)
    sh["gnT"] = A(inp["gdn_norm"][0].reshape(128, 1))
    sh["snT"] = A(inp["ssd_norm"][0].reshape(16, 128).T)
    sh["w_bg"] = A(inp["w_branch_gdn"][0]); sh["w_bs"] = A(inp["w_branch_ssd"][0]); sh["w_out"] = A(inp["w_out"][0])
    sh["w_up"] = A(inp["w_ffn_up"][0]); sh["w_down"] = A(inp["w_ffn_down"][0])
    sh["ident"] = np.eye(128, dtype=f)
    lc = np.zeros((128, 64), f); lc[:64] = 1.0; lc[64:] = np.eye(64, dtype=f)
    sh["lc"] = lc
    t = np.arange(64)
    sh["triu"] = (t[:, None] <= t[None, :]).astype(f)
    sh["trisl"] = (t[:, None] > t[None, :]).astype(f)
    mut = np.zeros((128, 64), f); msut = np.zeros((128, 64), f)
    mut[64:] = np.where(t[:, None] <= t[None, :], 0.0, NEG)
    msut[64:] = np.where(t[:, None] < t[None, :], 0.0, NEG)
    sh["mut"] = mut; sh["msut"] = msut
    b16 = t // 16; b32 = t // 32
    md = (b16[:, None] == b16[None, :]); m2 = (b32[:, None] != b32[None, :]); m1 = (~md) & (~m2)
    sh["blkm"] = np.ascontiguousarray(np.stack([md, m1, m2], axis=1).astype(f).reshape(64, 192))
    return sh


def kernel(x_prompt, x_sample, state_conv_mix, state_gdn, state_ssd, state_conv_ffn, c_prompt, c_sample,
           w_ada, b_ada, w_norm_mix, w_in, w_conv_mix, b_conv_mix, gdn_a_log, gdn_dt_bias, gdn_norm,
           ssd_a_log, ssd_dt_bias, ssd_d, ssd_norm, w_branch_gdn, w_branch_ssd, w_out, w_norm_ffn,
           w_ffn_up, w_ffn_conv, b_ffn_conv, w_ffn_down, w_norm_final, _ntok_p=SEQ, _stages=99, _dbg=False, _trace=False):
    inp = dict(w_ada=w_ada, b_ada=b_ada, w_norm_mix=w_norm_mix, w_in=w_in, w_conv_mix=w_conv_mix, b_conv_mix=b_conv_mix,
               gdn_a_log=gdn_a_log, gdn_dt_bias=gdn_dt_bias, gdn_norm=gdn_norm, ssd_a_log=ssd_a_log, ssd_dt_bias=ssd_dt_bias,
               ssd_d=ssd_d, ssd_norm=ssd_norm, w_branch_gdn=w_branch_gdn, w_branch_ssd=w_branch_ssd, w_out=w_out,
               w_norm_ffn=w_norm_ffn, w_ffn_up=w_ffn_up, w_ffn_conv=w_ffn_conv, b_ffn_conv=b_ffn_conv, w_ffn_down=w_ffn_down,
               w_norm_final=w_norm_final)
    inp = {k: np.asarray(v) for k, v in inp.items()}
    sh = _prep_shared(inp)
    f = np.float32
    x_prompt = np.asarray(x_prompt); x_sample = np.asarray(x_sample)
    c_prompt = np.asarray(c_prompt); c_sample = np.asarray(c_sample)
    scm = np.asarray(state_conv_mix)[0]; sgd = np.asarray(state_gdn)[0]; sss = np.asarray(state_ssd)[0]; scf = np.asarray(state_conv_ffn)[0]
    in_maps = []
    for c in range(8):
        m = dict(sh)
        b = c % 4
        m["xp"] = np.ascontiguousarray(x_prompt[b, :_ntok_p], dtype=f)
        m["xs"] = np.ascontiguousarray(x_sample[c], dtype=f)
        cT = np.zeros((128, 16), f)
        cT[:, 0::2] = c_prompt[b].reshape(8, 128).T
        cT[:, 1::2] = c_sample[c].reshape(8, 128).T
        m["cT"] = cT
        m["cm0T"] = np.ascontiguousarray(scm[c].reshape(3, 56, 128).transpose(2, 1, 0).reshape(128, 56 * 3), dtype=f)
        m["s0"] = np.ascontiguousarray(sgd[c].transpose(1, 0, 2).reshape(128, 8 * 128), dtype=f)
        m["h0"] = np.ascontiguousarray(sss[c].transpose(1, 0, 2).reshape(128, 32 * 64), dtype=f)
        m["cf0T"] = np.ascontiguousarray(scf[c].reshape(2, 44, 128).transpose(2, 1, 0).reshape(128, 44 * 2), dtype=f)
        in_maps.append(m)
    nc = build(_ntok_p, _stages, _dbg)
    if _trace:
        res = run_bass_kernel_spmd(nc, in_maps, core_ids=list(range(8)), trace=True)
        print('EXEC_TIME_NS', res.exec_time_ns)
    else:
        res = run_bass_kernel_spmd(nc, in_maps, core_ids=list(range(8)))
    R = res.results
    y_p = np.stack([R[b]["yp"] for b in range(4)])
    y_s = np.stack([R[c]["ys"] for c in range(8)])

    def gather(key, cores, shape):
        return np.stack([np.asarray(R[c][key]).reshape(shape) for c in cores])[None]
    out = (y_p, y_s,
           gather("cm_p", range(4), (3, CONV_CH)), gather("gd_p", range(4), (8, 128, 128)),
           gather("ss_p", range(4), (32, 128, 64)), gather("cf_p", range(4), (2, 2 * DFF)),
           gather("cm_s", range(8), (3, CONV_CH)), gather("gd_s", range(8), (8, 128, 128)),
           gather("ss_s", range(8), (32, 128, 64)), gather("cf_s", range(8), (2, 2 * DFF)))
    out = tuple(np.ascontiguousarray(o, dtype=f) for o in out)
    if _dbg:
        return out, R
    return out
```

```python
from contextlib import ExitStack
import numpy as np
import concourse.bass as bass
import concourse.mybir as mybir
from concourse.bass_utils import run_bass_kernel_spmd

F32 = mybir.dt.float32
BF16 = mybir.dt.bfloat16
AF = mybir.ActivationFunctionType
ALU = mybir.AluOpType
ENGS = ("pe", "act", "dve", "pool", "sp")

D = 1024
SEQ = 8192
DFF = 2816
CONV_CH = 7168
IN_COLS = 12336
OFF_GZ = 7168
OFF_SZ = 8192
OFF_SM = 10240
OFF_MG = 10288
EPS = 1e-6
NEG = -30000.0


class Sched:
    WINDOW = 32

    def __init__(self):
        self.trace = []
        self.final_queue = None
        self.tag = ""

    def op(self, eng, fn, r=(), w=(), cost=0.3):
        self.trace.append(dict(eng=eng, fn=fn, kind="c", key=None, r=tuple(r), w=tuple(w), cost=cost, tag=self.tag))

    def dma(self, queue, key, fn, r=(), w=(), cost=3.0):
        self.trace.append(dict(eng=queue, fn=fn, kind="d", key=key, r=tuple(r), w=tuple(w), cost=cost, tag=self.tag))

    def finish(self, queue):
        self.final_queue = queue

    def schedule(self):
        tr = self.trace
        n = len(tr)
        deps = [set() for _ in range(n)]
        state = {}
        dma_by_key = {}
        for i, o in enumerate(tr):
            d = deps[i]
            for nm in o["r"]:
                st = state.get(nm)
                if st and st[0] is not None:
                    d.add(st[0])
            for nm in o["w"]:
                st = state.get(nm)
                if st:
                    if st[0] is not None:
                        d.add(st[0])
                    d.update(st[1])
            for j in list(d):
                if tr[j]["kind"] == "d":
                    d.add(dma_by_key[tr[j]["key"]][-1])
            d.discard(i)
            for nm in o["r"]:
                st = state.setdefault(nm, [None, []])
                st[1].append(i)
            for nm in o["w"]:
                state[nm] = [i, []]
            if o["kind"] == "d":
                dma_by_key.setdefault(o["key"], []).append(i)
        self.deps = deps
        self.dma_by_key = dma_by_key
        users = [[] for _ in range(n)]
        ndep = [0] * n
        for i in range(n):
            ndep[i] = len(deps[i])
            for j in deps[i]:
                users[j].append(i)
        queues = {e: [] for e in ENGS}
        for i, o in enumerate(tr):
            queues[o["eng"]].append(i)
        ptr = {e: 0 for e in ENGS}
        window = {e: [] for e in ENGS}
        inorder = {"sp", "pool"}
        fin = [0.0] * n
        ready = [0.0] * n
        etime = {e: 0.0 for e in ENGS}
        order = {e: [] for e in ENGS}
        done = [False] * n

        def refill(e):
            w_ = window[e]
            q = queues[e]
            lim = 1 if e in inorder else self.WINDOW
            while len(w_) < lim and ptr[e] < len(q):
                w_.append(q[ptr[e]])
                ptr[e] += 1
        for e in ENGS:
            refill(e)
        remaining = n
        cur_tbl = [None]
        while remaining:
            best = None
            for e in ENGS:
                et = etime[e]
                for i in window[e]:
                    if ndep[i]:
                        continue
                    s_ = ready[i] if ready[i] > et else et
                    if e == "act":
                        tb = tr[i].get("tbl")
                        if tb is not None and tb != cur_tbl[0]:
                            s_ += 1.3
                    if best is None or s_ < best[0] - 1e-9 or (abs(s_ - best[0]) <= 1e-9 and i < best[1]):
                        best = (s_, i, e)
                    if e in inorder:
                        break
            s_, i, e = best
            o = tr[i]
            if o["kind"] == "d":
                etime[e] = s_ + 0.06
                fin[i] = s_ + o["cost"]
            else:
                etime[e] = s_ + o["cost"]
                fin[i] = s_ + o["cost"] + 0.15
                if e == "act" and o.get("tbl") is not None:
                    cur_tbl[0] = o["tbl"]
            done[i] = True
            order[e].append(i)
            window[e].remove(i)
            refill(e)
            for u in users[i]:
                ndep[u] -= 1
                if fin[i] > ready[u]:
                    ready[u] = fin[i]
            remaining -= 1
        self.order = order
        self.est_time = max(fin) if n else 0.0


def emit_program(nc, S, stack):
    S.schedule()
    tr = S.trace
    pos = {}
    for e in ENGS:
        for k, i in enumerate(S.order[e]):
            pos[i] = k
    dma_seq = {}
    for key, lst in S.dma_by_key.items():
        for k, i in enumerate(lst):
            dma_seq[i] = k + 1
    needed = {e: set() for e in ENGS}
    waits = {}
    for e in ENGS:
        waited = {}
        for i in S.order[e]:
            w_ = {}
            for j in S.deps[i]:
                oj = tr[j]
                if oj["kind"] == "d":
                    tgt = ("dma", oj["key"]); val = dma_seq[j]
                else:
                    tgt = oj["eng"]; val = pos[j] + 1
                    if e == "pe" and tgt == "pe":
                        continue
                if w_.get(tgt, 0) < val:
                    w_[tgt] = val
            out = {}
            for tgt, val in w_.items():
                if waited.get(tgt, 0) >= val:
                    continue
                waited[tgt] = val
                out[tgt] = val
                if not isinstance(tgt, tuple):
                    needed[tgt].add(val)
            waits[i] = out
    sems = {}
    for e in ENGS:
        sems[e] = stack.enter_context(nc.semaphore("s_" + e))
    for key in S.dma_by_key:
        sems[("dma", key)] = stack.enter_context(nc.semaphore("d_" + str(key)))
    cnt = {}
    for e in ENGS:
        c = 0
        m = {}
        for k, i in enumerate(S.order[e]):
            if tr[i]["kind"] == "c" and (k + 1) in needed[e]:
                c += 1
                m[k + 1] = c
        cnt[e] = m

    def wait_val(tgt, val):
        if isinstance(tgt, tuple):
            return 16 * val
        return cnt[tgt][val]

    block = stack.enter_context(nc.Block())

    def run(e):
        def body(h):
            for k, i in enumerate(S.order[e]):
                o = tr[i]
                for tgt, val in waits[i].items():
                    h.wait_ge(sems[tgt], wait_val(tgt, val))
                ins = o["fn"](h)
                if o["kind"] == "d":
                    ins.then_inc(sems[("dma", o["key"])], 16)
                elif (k + 1) in cnt[e]:
                    ins.then_inc(sems[e], 1)
            if e == S.final_queue:
                for key, lst in S.dma_by_key.items():
                    h.wait_ge(sems[("dma", key)], 16 * len(lst))
        return body

    block.tensor(run("pe"))
    block.scalar(run("act"))
    block.vector(run("dve"))
    block.gpsimd(run("pool"))
    block.sync(run("sp"))


def build(ntok_p, stages=99, dbg=False):
    nc = bass.Bass("TRN2", target_bir_lowering=False)
    S = Sched()
    st = ExitStack()
    with st:
        def din(name, shape, dt=F32):
            return nc.dram_tensor(name, list(shape), dt, kind="ExternalInput").ap()

        def dout(name, shape, dt=F32):
            return nc.dram_tensor(name, list(shape), dt, kind="ExternalOutput").ap()

        def dscr(name, shape, dt=BF16):
            return nc.dram_tensor(name, list(shape), dt, kind="Internal").ap()

        def sb(name, shape, dt):
            return st.enter_context(nc.sbuf_tensor(name + "_t", list(shape), dt))

        xp = din("xp", [ntok_p, D]); xs = din("xs", [64, D])
        cT_d = din("cT", [128, 16]); b_adaT_d = din("b_adaT", [128, 48]); wnT_d = din("wnT", [128, 16])
        wfin_d = din("wfin", [1, D])
        w_ada_d = din("w_ada", [D, 6 * D]); w_in_d = din("w_in", [D, IN_COLS])
        wcm_d = din("wcm", [128, 56 * 4]); bcm_d = din("bcm", [128, 56])
        wcf_d = din("wcf", [128, 44 * 3]); bcf_d = din("bcf", [128, 44])
        gpar_d = din("gpar", [1, 16]); spar_d = din("spar", [1, 64])
        dcol_d = din("dcol", [128, 16]); gnT_d = din("gnT", [128, 1]); snT_d = din("snT", [128, 16])
        w_bg_d = din("w_bg", [D, D]); w_bs_d = din("w_bs", [2 * D, D]); w_out_d = din("w_out", [D, D])
        w_up_d = din("w_up", [D, 2 * DFF]); w_down_d = din("w_down", [DFF, D])
        cm0_d = din("cm0T", [128, 56 * 3]); s0_d = din("s0", [128, 8 * 128]); h0_d = din("h0", [128, 32 * 64])
        cf0_d = din("cf0T", [128, 44 * 2])
        ident_d = din("ident", [128, 128]); lc_d = din("lc", [128, 64])
        triu_d = din("triu", [64, 64]); trisl_d = din("trisl", [64, 64])
        mut_d = din("mut", [128, 64]); msut_d = din("msut", [128, 64]); blkm_d = din("blkm", [64, 192])

        yp = dout("yp", [ntok_p, D]); ys = dout("ys", [64, D])
        outs = {}
        for sfx in ("p", "s"):
            outs["cm_" + sfx] = dout("cm_" + sfx, [3 * 56, 128])
            outs["gd_" + sfx] = dout("gd_" + sfx, [8, 128, 128])
            outs["ss_" + sfx] = dout("ss_" + sfx, [32, 128, 64])
            outs["cf_" + sfx] = dout("cf_" + sfx, [2 * 44, 128])
        dbg_o = dout("dbg", [128, 16384]) if dbg else None

        wi_s = dscr("wi_s", [D, IN_COLS])
        wbg_s = dscr("wbg_s", [D, D]); wbs_s = dscr("wbs_s", [2 * D, D]); wo_s = dscr("wo_s", [D, D])
        wu_s = dscr("wu_s", [D, 2 * DFF]); wd_s = dscr("wd_s", [DFF, D])

        x_sb = sb("x_sb", [128, 4, D], F32)
        hT = sb("hT", [128, 8, 512], BF16)
        XC = sb("XC", [128, 32, 512], BF16)
        oT = sb("oT", [128, 8, 512], BF16)
        yT = sb("yT", [128, 16, 512], BF16)
        wblk = sb("wblk", [128, 2, 8192], BF16)
        S_f = sb("S_f", [128, 8, 128], F32); S_b = sb("S_b", [128, 8, 128], BF16)
        H_f = sb("H_f", [128, 32, 64], F32); H_b = sb("H_b", [128, 32, 64], BF16)
        ARF = [sb("arF%d" % i, [128, 1024], F32) for i in range(3)]
        ARB = [sb("arB%d" % i, [128, 2048], BF16) for i in range(6)]
        stg = sb("stg", [128, 2, 516], BF16)
        YG = sb("YG", [128, 1024], F32)
        YGb = sb("YGb", [128, 1024], BF16)
        LCb = sb("LCb", [128, 64], BF16)
        carry = sb("carry", [128, 56, 3], BF16); carryf = sb("carryf", [128, 44, 2], BF16)
        identf = sb("identf", [128, 128], F32); identb = sb("identb", [128, 128], BF16)
        onesf = sb("onesf", [128, 128], F32); onesb = sb("onesb", [128, 128], BF16)
        LC = sb("LC", [128, 64], F32)
        triu = sb("triu", [64, 64], F32); trisl = sb("trisl", [64, 64], F32)
        blkm = sb("blkm", [64, 3, 64], F32)
        cT = sb("cT", [128, 16], F32); scT = sb("scT", [128, 16], BF16)
        b_adaT = sb("b_adaT", [128, 48], F32); wnT = sb("wnT", [128, 16], F32)
        modT = sb("modT", [128, 48, 2], F32)
        sc1 = sb("sc1", [128, 8], F32); sc2 = sb("sc2", [128, 8], F32)
        gtb = sb("gtb", [128, 2, D], F32)
        wfin = sb("wfin", [128, D], F32)
        wcm = sb("wcm", [128, 56, 4], F32); bcm = sb("bcm", [128, 56], F32)
        wcf = sb("wcf", [128, 44, 3], F32); bcf = sb("bcf", [128, 44], F32)
        gpar = sb("gpar", [64, 16], F32); spar = sb("spar", [64, 64], F32)
        nA = sb("nA", [64, 48], F32)
        bias48 = sb("bias48", [64, 48], F32)
        dcol = sb("dcol", [128, 16], F32); gnT = sb("gnT", [128, 1], F32); snT = sb("snT", [128, 16], F32)
        wsm = sb("wsm", [128, 8, 48], BF16)
        smT = sb("smT", [64, 8, 48], F32); smL = sb("smL", [64, 8, 48], F32); smG = sb("smG", [64, 8, 48], F32)
        smGAM = sb("smGAM", [64, 8, 48], F32); smREV = sb("smREV", [64, 8, 48], F32)
        smWJ = sb("smWJ", [64, 8, 32], F32); smGG = sb("smGG", [64, 8, 2, 8], F32)
        smB = sb("smB", [64, 8, 2, 8], F32)
        smGS = sb("smGS", [64, 8, 32], F32)
        ssq = sb("ssq", [128, 16], F32)
        cst = sb("cst", [128, 3 * 56], F32)
        psf = st.enter_context(nc.psum_tensor("psf", [128, 6, 512], F32))
        psb = st.enter_context(nc.psum_tensor("psb", [128, 2, 1024], BF16))

        def Bv(k, lo, n, parts=128):
            return ARB[k][:parts, lo:lo + n], ["arB%d_%d" % (k, o) for o in range((lo // 512) * 512, lo + n, 512)]

        def Fv(k, lo, n, parts=128):
            return ARF[k][:parts, lo:lo + n], ["arF%d_%d" % (k, o) for o in range((lo // 512) * 512, lo + n, 512)]

        PF = lambda i: "psf%d" % i
        PB = lambda i: "psb%d" % i
        rot = {"f": 0, "b": 0}

        def fbank(n=1):
            if n == 1:
                i = rot["f"] % 6
                rot["f"] += 1
                return i
            i = ((rot["f"] + 1) // 2 * 2) % 6
            rot["f"] = i + 2
            return i

        def bbank():
            i = rot["b"] % 2
            rot["b"] += 1
            return i

        def fsz(ap):
            n_ = 1
            for d_ in ap.shape[1:]:
                n_ *= d_
            return n_

        def mm(out, lhsT, rhs, start, stop, r, w, tp=None):
            c_ = 0.07 + fsz(rhs) * (4 if rhs.dtype == F32 else 1) / 1800.0
            if tp is None:
                S.op("pe", lambda e: e.matmul(out, lhsT=lhsT, rhs=rhs, start=start, stop=stop), r, w, c_)
            else:
                S.op("pe", lambda e: e.matmul(out, lhsT=lhsT, rhs=rhs, start=start, stop=stop, tile_position=tp), r, w, c_)

        def tr(out, in_, idn, r, w):
            S.op("pe", lambda e: e.transpose(out, in_, idn), r, w, 0.1 + fsz(in_) * (4 if in_.dtype == F32 else 1) / 1800.0)

        def acti(out, in_, func, r, w, bias=None, scale=None, accum=None):
            kw = {}
            if bias is not None:
                kw["bias"] = bias
            if scale is not None:
                kw["scale"] = scale
            if accum is not None:
                kw["accum_out"] = accum
            S.op("act", lambda e: e.activation(out=out, in_=in_, func=func, **kw), r, w, 0.2 + fsz(out) / 1100.0)
            S.trace[-1]["tbl"] = {AF.Exp: "exp", AF.Ln: "exp", AF.Silu: "silu", AF.Sigmoid: "sigm"}.get(func, None)

        def tt(out, in0, in1, op, r, w, eng="dve"):
            S.op(eng, lambda e: e.tensor_tensor(out=out, in0=in0, in1=in1, op=op), r, w, 0.1 + fsz(out) / 900.0)

        def stt(out, in0, scalar, in1, op0, op1, r, w, eng="dve"):
            S.op(eng, lambda e: e.scalar_tensor_tensor(out=out, in0=in0, scalar=scalar, in1=in1, op0=op0, op1=op1), r, w, 0.1 + fsz(out) / 700.0)

        def ts(out, in0, s1, op0, r, w, s2=None, op1=None, eng="dve"):
            if op1 is None:
                S.op(eng, lambda e: e.tensor_scalar(out=out, in0=in0, scalar1=s1, scalar2=None, op0=op0), r, w, 0.1 + fsz(out) / 900.0)
            else:
                S.op(eng, lambda e: e.tensor_scalar(out=out, in0=in0, scalar1=s1, scalar2=s2, op0=op0, op1=op1), r, w, 0.1 + fsz(out) / 900.0)

        def cp(out, in_, r, w, eng="dve"):
            if eng == "act":
                S.op("act", lambda e: e.activation(out=out, in_=in_, func=AF.Copy), r, w, 0.2 + fsz(out) / 1100.0)
            else:
                S.op(eng, lambda e: e.tensor_copy(out=out, in_=in_), r, w, 0.1 + fsz(out) / 1500.0)

        def mset(ap, val, w, eng="dve"):
            S.op(eng, lambda e: e.memset(ap, val), (), w)

        def load(dst_ap, src_ap, names, key="ld", q="sp", r=()):
            S.dma(q, key, lambda e: e.dma_start(out=dst_ap, in_=src_ap), r=r, w=names)

        dbg_col = {"c": 0}

        def dump(ap, names, npart, ncol):
            if not dbg:
                return
            c0 = dbg_col["c"]
            dbg_col["c"] += ncol
            S.dma("pool", "dbg", lambda e: e.dma_start(out=dbg_o[0:npart, c0:c0 + ncol], in_=ap), r=names)
            return c0

        def cast(dst, src, rows, name, r=()):
            for r0 in range(0, rows, 128):
                S.dma("pool", "wc", lambda e, r0=r0: e.dma_start(out=dst[r0:r0 + 128, :], in_=src[r0:r0 + 128, :]), r=r, w=[name])

        for (t, d_, n) in ((identf, ident_d, "identf"), (triu, triu_d, "triu"), (trisl, trisl_d, "trisl"),
                           (cT, cT_d, "cT"), (b_adaT, b_adaT_d, "b_adaT"), (wnT, wnT_d, "wnT"),
                           (bcm, bcm_d, "bcm"), (bcf, bcf_d, "bcf"), (dcol, dcol_d, "dcol"), (gnT, gnT_d, "gnT"),
                           (snT, snT_d, "snT"), (LC, lc_d, "LC")):
            load(t[:], d_[:, :], [n])
        load(blkm[:].rearrange("p m i -> p (m i)"), blkm_d[:, :], ["blkm"])
        load(wcm[:].rearrange("p t i -> p (t i)"), wcm_d[:, :], ["wcm"])
        load(wcf[:].rearrange("p t i -> p (t i)"), wcf_d[:, :], ["wcf"])
        load(wfin[:], wfin_d.partition_broadcast(128).rearrange("p o n -> p (o n)"), ["wfin"])
        load(gpar[:], gpar_d.partition_broadcast(64).rearrange("p o n -> p (o n)"), ["gpar"])
        load(spar[:], spar_d.partition_broadcast(64).rearrange("p o n -> p (o n)"), ["spar"])
        S.dma("pool", "ldc", lambda e: e.dma_start(out=identb[:], in_=ident_d[:, :]), w=["identb"])
        S.dma("pool", "ldc", lambda e: e.dma_start(out=wsm[:], in_=w_in_d[:, OFF_SM:OFF_SM + 48].rearrange("(k p) n -> p k n", p=128)), w=["wsm"])
        mset(onesf[:], 1.0, ["onesf"])
        cp(LCb[:], LC[:], ["LC"], ["LCb"])
        for hh in range(16):
            S.dma("pool", "ldc", lambda e, hh=hh: e.dma_start(out=YGb[64:128, hh * 64:(hh + 1) * 64], in_=mut_d[64:128, :]), w=["YGblo"])
        mset(onesb[:], 1.0, ["onesb"])
        mset(bias48[:], 0.0, ["bias48"])
        cp(bias48[:, 8:16], gpar[:, 8:16], ["gpar"], ["bias48"])
        cp(bias48[:, 16:48], spar[:, 32:64], ["spar"], ["bias48"])
        acti(nA[:, 8:16], gpar[:, 0:8], AF.Exp, ["gpar"], ["nA"])
        acti(nA[:, 16:48], spar[:, 0:32], AF.Exp, ["spar"], ["nA"])
        ts(nA[:, 8:48], nA[:, 8:48], -1.0, ALU.mult, ["nA"], ["nA"])

        wst = {"i": 0}

        def wload(scr, srcname, c0, ncol, nk, k0=0):
            b = wst["i"] % 2
            wst["i"] += 1
            view = wblk[:, b, 0:nk * ncol].rearrange("p (k n) -> p k n", k=nk)
            src = scr[k0 * 128:(k0 + nk) * 128, c0:c0 + ncol].rearrange("(k p) n -> p k n", p=128)
            S.dma("sp", "w%d" % b, lambda e: e.dma_start(out=view, in_=src), r=[srcname], w=["wblk%d" % b], cost=3.0 + nk * ncol / 1500.0)
            return view, "wblk%d" % b

        acti(scT[:], cT[:], AF.Silu, ["cT"], ["scT"])
        pb_ada = fbank()
        for blk in range(6):
            b_ = wst["i"] % 2
            wst["i"] += 1
            wv = wblk[:, b_, 0:8192].rearrange("p (k n) -> p k n", k=8)
            wn = "wblk%d" % b_
            S.dma("pool", "w%d" % b_, lambda e, wv=wv, blk=blk: e.dma_start(out=wv, in_=w_ada_d[:, blk * 1024:(blk + 1) * 1024].rearrange("(k p) n -> p k n", p=128)),
                  w=[wn], cost=12.0)
            for jj in range(8):
                j = blk * 8 + jj
                for kc in range(8):
                    mm(psf[:, pb_ada, 2 * j:2 * j + 2], wv[:, kc, jj * 128:(jj + 1) * 128], scT[:, 2 * kc:2 * kc + 2],
                       kc == 0, kc == 7, [wn, "scT"], [PF(pb_ada)])
        tt(modT[:], psf[:, pb_ada, 0:96].rearrange("p (j s) -> p j s", s=2), b_adaT[:].unsqueeze(2).to_broadcast([128, 48, 2]),
           ALU.add, [PF(pb_ada), "b_adaT"], ["modT"])
        cast(wi_s, w_in_d, D, "wi_s")
        cast(wbg_s, w_bg_d, D, "wbg_s"); cast(wbs_s, w_bs_d, 2 * D, "wbs_s"); cast(wo_s, w_out_d, D, "wo_s")
        cast(wu_s, w_up_d, D, "wu_s"); cast(wd_s, w_down_d, DFF, "wd_s")

        def stream(sidx, x_d, y_d, ntok, T, has_init, sfx):
            NCH = T // 64
            TT = min(128, T)
            NT = T // TT
            nsc = ntok // T

            stt(sc1[:], modT[:, 8:16, sidx], 1.0, wnT[:, 0:8], ALU.add, ALU.mult, ["modT", "wnT"], ["sc1"])
            stt(sc2[:], modT[:, 32:40, sidx], 1.0, wnT[:, 8:16], ALU.add, ALU.mult, ["modT", "wnT"], ["sc2"])
            for gi, base in ((0, 16), (1, 40)):
                for kc in range(8):
                    dg, dgn = Fv(0, (kc % 2) * 512, 128)
                    ts(dg, identf[:], modT[:, base + kc, sidx:sidx + 1], ALU.mult, ["identf", "modT"], dgn)
                    pbk = fbank()
                    mm(psf[:, pbk, 0:128], onesf[:], dg, True, True, ["onesf"] + dgn, [PF(pbk)])
                    cp(gtb[:, gi, kc * 128:(kc + 1) * 128], psf[:, pbk, 0:128], [PF(pbk)], ["gtb"], eng="act")

            CARRY = ["carry%d" % t for t in range(56)]
            CARRYF = ["carryf%d" % t for t in range(44)]
            if not has_init:
                mset(S_f[:], 0.0, ["S_f"]); mset(S_b[:], 0.0, ["S_b"])
                for hh in range(2):
                    mset(H_f[:, hh * 16:(hh + 1) * 16, :], 0.0, ["H_f%d" % hh])
                    mset(H_b[:, hh * 16:(hh + 1) * 16, :], 0.0, ["H_b%d" % hh])
                mset(carry[:], 0.0, CARRY); mset(carryf[:], 0.0, CARRYF)
            else:
                load(S_f[:].rearrange("p h v -> p (h v)"), s0_d[:, :], ["S_f"])
                load(H_f[:].rearrange("p h v -> p (h v)"), h0_d[:, :], ["H_f0", "H_f1"])
                cp(S_b[:], S_f[:], ["S_f"], ["S_b"], eng="act")
                for hh in range(2):
                    cp(H_b[:, hh * 16:(hh + 1) * 16, :], H_f[:, hh * 16:(hh + 1) * 16, :], ["H_f%d" % hh], ["H_b%d" % hh], eng="act")
                S.dma("pool", "ldc", lambda e: e.dma_start(out=carry[:].rearrange("p t r -> p (t r)"), in_=cm0_d[:, :]), w=CARRY)
                S.dma("pool", "ldc", lambda e: e.dma_start(out=carryf[:].rearrange("p t r -> p (t r)"), in_=cf0_d[:, :]), w=CARRYF)

            def rms_to_hT(scv, shbase):
                S.tag = "rms"
                junk, junkn = Bv(2, 0, 1024, TT)
                for nt in range(NT):
                    acti(junk, x_sb[:TT, nt, :], AF.Square, [("x", nt)], junkn + ["ssq"], accum=ssq[:TT, nt:nt + 1])
                acti(ssq[:TT, 4:4 + NT], ssq[:TT, 0:NT], AF.Ln, ["ssq"], ["ssq"], bias=EPS, scale=1.0 / D)
                acti(ssq[:TT, 8:8 + NT], ssq[:TT, 4:4 + NT], AF.Exp, ["ssq"], ["ssq"], scale=-0.5)
                xns = []
                for nt in range(NT):
                    xn, xnn = Bv(nt // 2, (nt % 2) * 1024, 1024, TT)
                    acti(xn, x_sb[:TT, nt, :], AF.Copy, [("x", nt), "ssq"], xnn, scale=ssq[:TT, 8 + nt:9 + nt])
                    xns.append((xn, xnn))
                for kc in range(8):
                    bk = bbank()
                    for nt in range(NT):
                        xn, xnn = xns[nt]
                        tr(psb[:, bk, nt * TT:(nt + 1) * TT], xn[:, kc * 128:(kc + 1) * 128], identb[:TT, :TT], xnn + ["identb"], [PB(bk)])
                    acti(hT[:, kc, 0:T], psb[:, bk, 0:T], AF.Identity, [PB(bk), "modT", "sc1", "sc2"], ["hT"],
                         bias=modT[:, shbase + kc, sidx:sidx + 1], scale=scv[:, kc:kc + 1])

            def conv_tile(ps_ap, psn, wt, bt, ntap, cry, cryn, out_ap, outn, accslot, func, k):
                nb = ntap - 1
                sgi = k % 2
                sg = stg[:, sgi, :]
                sgn = "stg%d" % sgi
                acc, accl = Fv(accslot, (k % 2) * 512, T)
                accn = accl[0]
                cp(sg[:, 0:nb], cry, [cryn], [sgn], eng="dve")
                cp(sg[:, nb:nb + T], ps_ap, [psn], [sgn], eng="act")
                acti(acc, ps_ap, AF.Identity, [psn], [accn], bias=bt, scale=wt[:, nb:nb + 1])
                for i in range(nb - 1, -1, -1):
                    stt(acc, sg[:, i:i + T], wt[:, i:i + 1], acc, ALU.mult, ALU.add, [sgn, accn], [accn])
                if func is None:
                    cp(out_ap, acc, [accn], [outn], eng="act")
                else:
                    acti(out_ap, acc, func, [accn], [outn])
                cp(cry, sg[:, T:T + nb], [sgn], [cryn], eng="dve")

            def proj_conv(col0, ntiles, xc0, cch0, direct=False):
                S.tag = "projconv"
                k = 0
                for b0 in range(0, ntiles, 4):
                    nb_ = min(4, ntiles - b0)
                    if direct:
                        b_ = wst["i"] % 2
                        wst["i"] += 1
                        wv = wblk[:, b_, 0:8 * nb_ * 128].rearrange("p (k n) -> p k n", k=8)
                        wn = "wblk%d" % b_
                        c0_ = col0 + b0 * 128
                        S.dma("pool", "w%d" % b_, lambda e, wv=wv, c0_=c0_, nb_=nb_: e.dma_start(out=wv, in_=w_in_d[:, c0_:c0_ + nb_ * 128].rearrange("(k p) n -> p k n", p=128)),
                              w=[wn], cost=10.0)
                    else:
                        wv, wn = wload(wi_s, "wi_s", col0 + b0 * 128, nb_ * 128, 8)
                    for jj in range(nb_):
                        t = b0 + jj
                        pb = fbank()
                        for kc in range(8):
                            mm(psf[:, pb, 0:T], wv[:, kc, jj * 128:(jj + 1) * 128], hT[:, kc, 0:T], kc == 0, kc == 7, [wn, "hT"], [PF(pb)])
                        ct = cch0 + t
                        conv_tile(psf[:, pb, 0:T], PF(pb), wcm[:, ct, :], bcm[:, ct:ct + 1], 4, carry[:, ct, :], "carry%d" % ct,
                                  XC[:, xc0 + t, 0:T], ("XC", xc0 + t), 0, AF.Silu, k)
                        k += 1

            def l2norm_tiles(tiles, qscale_tiles):
                S.tag = "l2norm"
                for k, t in enumerate(tiles):
                    o = (k % 2) * 512
                    sq, sql = Bv(2, 1024 + o, T); sqn = sql[0]
                    acti(sq, XC[:, t, 0:T], AF.Square, [("XC", t)], [sqn])
                    pb = fbank()
                    mm(psf[:, pb, 0:T], onesb[:], sq, True, True, ["onesb", sqn], [PF(pb)])
                    ln, lnl = Fv(1, o, T); lnn = lnl[0]
                    acti(ln, psf[:, pb, 0:T], AF.Ln, [PF(pb)], [lnn], bias=EPS)
                    rs, rsl = Bv(3, o, T); rsn = rsl[0]
                    if t in qscale_tiles:
                        acti(rs, ln, AF.Exp, [lnn], [rsn], scale=-0.5, bias=float(-0.5 * np.log(128.0)))
                    else:
                        acti(rs, ln, AF.Exp, [lnn], [rsn], scale=-0.5)
                    tt(XC[:, t, 0:T], XC[:, t, 0:T], rs, ALU.mult, [("XC", t), rsn], [("XC", t)])

            def smalls():
                S.tag = "smalls"
                pb = fbank()
                for c in range(NCH):
                    for kc in range(8):
                        mm(psf[:64, pb, c * 48:(c + 1) * 48], hT[:, kc, c * 64:(c + 1) * 64], wsm[:, kc, :], kc == 0, kc == 7, ["hT", "wsm"], [PF(pb)])
                n3 = lambda a: a[:, 0:NCH, :]
                tt(n3(smT), psf[:64, pb, 0:NCH * 48].rearrange("p (c n) -> p c n", n=48), bias48[:].unsqueeze(1).to_broadcast([64, NCH, 48]),
                   ALU.add, [PF(pb), "bias48"], ["smT"])
                ts(smT[:, 0:NCH, 0:8], smT[:, 0:NCH, 0:8], -1.0, ALU.mult, ["smT"], ["smT"])
                acti(n3(smT), n3(smT), AF.Exp, ["smT"], ["smT"])
                acti(n3(smL), n3(smT), AF.Ln, ["smT"], ["smL"], bias=1.0)
                tt(smG[:, 0:NCH, 8:48], smL[:, 0:NCH, 8:48], nA[:, 8:48].unsqueeze(1).to_broadcast([64, NCH, 40]), ALU.mult, ["smL", "nA"], ["smG"])
                mset(smG[:, 0:NCH, 0:8], 0.0, ["smG"])
                pg = fbank()
                mm(psf[:64, pg, 0:NCH * 48], triu[:], smG[:, 0:NCH, :].rearrange("p c n -> p (c n)"), True, True, ["triu", "smG"], [PF(pg)])
                cp(n3(smGAM), psf[:64, pg, 0:NCH * 48].rearrange("p (c n) -> p c n", n=48), [PF(pg)], ["smGAM"])
                pr = fbank()
                mm(psf[:64, pr, 0:NCH * 48], trisl[:], smG[:, 0:NCH, :].rearrange("p c n -> p (c n)"), True, True, ["trisl", "smG"], [PF(pr)])
                acti(n3(smREV), psf[:64, pr, 0:NCH * 48].rearrange("p (c n) -> p c n", n=48), AF.Exp, [PF(pr)], ["smREV"])
                tt(smWJ[:, 0:NCH, :], smREV[:, 0:NCH, 16:48], smL[:, 0:NCH, 16:48], ALU.mult, ["smREV", "smL"], ["smWJ"])
                cp(smGG[:, 0:NCH, 0, :], smGAM[:, 0:NCH, 8:16], ["smGAM"], ["smGG"])
                tt(smGG[:, 0:NCH, 1, :], smGAM[:, 0:NCH, 8:16], smL[:, 0:NCH, 0:8], ALU.subtract, ["smGAM", "smL"], ["smGG"])
                acti(smB[:, 0:NCH, 0, :], smL[:, 0:NCH, 0:8], AF.Exp, ["smL"], ["smB"], scale=-1.0)
                acti(smB[:, 0:NCH, 1, :], smGG[:, 0:NCH, 1, :], AF.Exp, ["smGG"], ["smB"])
                acti(smGS[:, 0:NCH, :], smL[:, 0:NCH, 16:48], AF.Ln, ["smL"], ["smGS"])
                tt(smGS[:, 0:NCH, :], smGAM[:, 0:NCH, 16:48], smGS[:, 0:NCH, :], ALU.subtract, ["smGAM", "smGS"], ["smGS"])

            def gdn_chunk(c, precise):
                S.tag = "gdnG1"
                cs = slice(c * 64, (c + 1) * 64)
                qT = lambda h: XC[:, h, cs]
                kT = lambda h: XC[:, 8 + h, cs]
                vT = lambda h: XC[:, 16 + h, cs]
                QN = [("XC", h) for h in range(8)]; KN = [("XC", 8 + h) for h in range(8)]; VN = [("XC", 16 + h) for h in range(8)]
                h8 = lambda ap: ap.rearrange("p (h k) -> p h k", h=8)
                bk = bbank()
                for h in range(8):
                    tr(psb[:64, bk, h * 128:(h + 1) * 128], kT(h), identb[:, :], [KN[h], "identb"], [PB(bk)])
                kd_, kdn = Bv(3, 0, 1024, 64); kd = h8(kd_)
                tt(kd, h8(psb[:64, bk, :]), smREV[:, c, 8:16].unsqueeze(2).to_broadcast([64, 8, 128]), ALU.mult, [PB(bk), "smREV"], kdn)
                bv_ = bbank()
                for h in range(8):
                    tr(psb[:64, bv_, h * 128:(h + 1) * 128], vT(h), identb[:, :], [VN[h], "identb"], [PB(bv_)])
                bvt_, bvn = Bv(3, 1024, 1024, 64); bvt = h8(bvt_)
                tt(bvt, h8(psb[:64, bv_, :]), smB[:, c, 0, :].unsqueeze(2).to_broadcast([64, 8, 128]), ALU.mult, [PB(bv_), "smB"], bvn)
                tt(YG[0:64, :].rearrange("p (a h i) -> p a h i", a=2, h=8), identf[0:64, 0:64].unsqueeze(1).unsqueeze(1).to_broadcast([64, 2, 8, 64]),
                   smGG[:, c, :, :].unsqueeze(3).to_broadcast([64, 2, 8, 64]), ALU.mult, ["identf", "smGG"], ["YGhi"])
                r0 = fbank(2)
                mm(psf[:64, r0, :], LC[:, :], YG[:, 0:512], True, True, ["LC", "YGhi", "YGlo"], [PF(r0)])
                mm(psf[:64, r0 + 1, :], LC[:, :], YG[:, 512:1024], True, True, ["LC", "YGhi", "YGlo"], [PF(r0 + 1)])
                rb = fbank()
                mm(psf[:, rb, :], onesf[0:64, :], YG[0:64, 0:512], True, True, ["onesf", "YGhi"], [PF(rb)])
                Eb, Ebn = Fv(2, 0, 512)
                acti(Eb, psf[:, rb, :], AF.Exp, [PF(rb)], Ebn)
                X, Xn = Fv(1, 0, 1024, 64)
                tt(X.rearrange("p (a h i) -> p a h i", a=2, h=8), psf[:64, r0:r0 + 2, :].rearrange("p a (h i) -> p a h i", h=8),
                   smGG[:, c, 0, :].unsqueeze(1).unsqueeze(3).to_broadcast([64, 2, 8, 64]), ALU.subtract, [PF(r0), PF(r0 + 1), "smGG"], Xn)
                acti(X, X, AF.Exp, Xn, Xn)
                k0 = fbank(2)
                for h in range(8):
                    mm(psf[:64, k0, h * 64:(h + 1) * 64], kT(h), qT(h), True, True, [KN[h], QN[h]], [PF(k0)])
                for h in range(8):
                    mm(psf[:64, k0 + 1, h * 64:(h + 1) * 64], kT(h), kT(h), True, True, [KN[h]], [PF(k0 + 1)])
                Aqk_, Aqkn = Bv(0, 1024, 512, 64)
                tt(Aqk_, psf[:64, k0, :], X[:, 0:512], ALU.mult, [PF(k0), Xn[0]], Aqkn)
                if precise:
                    LTf = X[:, 512:1024]; LTn = [Xn[1]]
                    tt(LTf, psf[:64, k0 + 1, :], LTf, ALU.mult, [PF(k0 + 1)] + LTn, LTn)
                    Aqk = lambda h: Aqk_[:, h * 64:(h + 1) * 64]
                    qTe, qTen = Bv(5, 0, 512)
                    tt(qTe.rearrange("p (h i) -> p h i", h=8), XC[:, 0:8, cs], Eb.rearrange("p (h i) -> p h i", h=8), ALU.mult, QN + Ebn, qTen)
                    bl = fbank()
                    for h in range(8):
                        hs = slice(h * 64, (h + 1) * 64)
                        tr(psf[:64, bl, hs], LTf[:, hs], identf[:64, :64], LTn + ["identf"], [PF(bl)])
                    Lf = X[:, 0:512]; Lfn = [Xn[0]]
                    cp(Lf, psf[:64, bl, :], [PF(bl)], Lfn, eng="act")
                    def f32v(k, half):
                        ap_, nm = Bv(k, half * 1024, 1024, 64)
                        return ap_.bitcast(F32), nm
                    v8 = lambda ap_: ap_.rearrange("p (h i) -> p h i", h=8)
                    mk = lambda m: blkm[:, m, :].unsqueeze(1).to_broadcast([64, 8, 64])
                    idb = identf[0:64, 0:64].unsqueeze(1).to_broadcast([64, 8, 64])
                    Ld, Ldn = f32v(1, 0); LdT, LdTn = f32v(1, 1)
                    L1, L1n = f32v(2, 0); L1T, L1Tn = f32v(2, 1)
                    L2, L2n = Fv(2, 512, 512, 64)
                    tt(v8(Ld), v8(Lf), mk(0), ALU.mult, Lfn + ["blkm"], Ldn)
                    tt(v8(LdT), v8(LTf), mk(0), ALU.mult, LTn + ["blkm"], LdTn)
                    tt(v8(L1), v8(Lf), mk(1), ALU.mult, Lfn + ["blkm"], L1n)
                    tt(v8(L1T), v8(LTf), mk(1), ALU.mult, LTn + ["blkm"], L1Tn)
                    tt(v8(L2), v8(Lf), mk(2), ALU.mult, Lfn + ["blkm"], L2n)
                    Pb = [(Fv(0, 0, 512, 64), Fv(0, 512, 512, 64)), (f32v(4, 0), f32v(4, 1))]
                    stt(v8(Pb[0][0][0]), v8(Ld), -1.0, idb, ALU.mult, ALU.add, Ldn + ["identf"], Pb[0][0][1])
                    stt(v8(Pb[0][1][0]), v8(LdT), -1.0, idb, ALU.mult, ALU.add, LdTn + ["identf"], Pb[0][1][1])
                    pairs = [((Ld, Ldn), (LdT, LdTn)), ((Lf, Lfn), (LTf, LTn))]

                else:
                    LTf, LTn = Bv(0, 0, 512, 64)
                    tt(LTf, psf[:64, k0 + 1, :], X[:, 512:1024], ALU.mult, [PF(k0 + 1), Xn[1]], LTn)
                    Aqk = lambda h: Aqk_[:, h * 64:(h + 1) * 64]
                    qTe, qTen = Bv(5, 0, 512)
                    tt(qTe.rearrange("p (h i) -> p h i", h=8), XC[:, 0:8, cs], Eb.rearrange("p (h i) -> p h i", h=8), ALU.mult, QN + Ebn, qTen)
                    bl = bbank()
                    for h in range(8):
                        hs = slice(h * 64, (h + 1) * 64)
                        tr(psb[:64, bl, hs], LTf[:, hs], identb[:64, :64], LTn + ["identb"], [PB(bl)])
                    Lf, Lfn = Bv(0, 512, 512, 64)
                    cp(Lf, psb[:64, bl, 0:512], [PB(bl)], Lfn, eng="act")
                    v8 = lambda ap_: ap_.rearrange("p (h i) -> p h i", h=8)
                    mk = lambda m: blkm[:, m, :].unsqueeze(1).to_broadcast([64, 8, 64])
                    idb = identb[0:64, 0:64].unsqueeze(1).to_broadcast([64, 8, 64])
                    Ld, Ldn = Bv(1, 0, 512, 64); LdT, LdTn = Bv(1, 512, 512, 64)
                    L1, L1n = Bv(1, 1024, 512, 64); L1T, L1Tn = Bv(1, 1536, 512, 64)
                    L2, L2n = Bv(0, 1536, 512, 64)
                    tt(v8(Ld), v8(Lf), mk(0), ALU.mult, Lfn + ["blkm"], Ldn)
                    tt(v8(LdT), v8(LTf), mk(0), ALU.mult, LTn + ["blkm"], LdTn)
                    tt(v8(L1), v8(Lf), mk(1), ALU.mult, Lfn + ["blkm"], L1n)
                    tt(v8(L1T), v8(LTf), mk(1), ALU.mult, LTn + ["blkm"], L1Tn)
                    tt(v8(L2), v8(Lf), mk(2), ALU.mult, Lfn + ["blkm"], L2n)
                    Pb = [(Bv(2, 0, 512, 64), Bv(2, 512, 512, 64)), (Bv(2, 1024, 512, 64), Bv(2, 1536, 512, 64))]
                    stt(v8(Pb[0][0][0]), v8(Ld), -1.0, idb, ALU.mult, ALU.add, Ldn + ["identb"], Pb[0][0][1])
                    stt(v8(Pb[0][1][0]), v8(LdT), -1.0, idb, ALU.mult, ALU.add, LdTn + ["identb"], Pb[0][1][1])
                    pairs = [((Ld, Ldn), (LdT, LdTn)), ((Lf, Lfn), (LTf, LTn))]

                def grp(lhs, lhsn, rhs, rhsn):
                    bnk = fbank()
                    for h in range(8):
                        hs = slice(h * 64, (h + 1) * 64)
                        mm(psf[:64, bnk, hs], lhs[:, hs], rhs[:, hs], True, True, lhsn + rhsn, [PF(bnk)])
                    return bnk
                cur = 0
                pc = 0
                for lvl in range(1, 4):
                    (A_, A_n), (AT_, AT_n) = pairs[cur]
                    (nA_, nAn), (nAT, nATn) = pairs[1 - cur]
                    a0 = grp(AT_, AT_n, A_, A_n)
                    a1 = grp(A_, A_n, AT_, AT_n)
                    cp(nA_, psf[:64, a0, :], [PF(a0)], nAn, eng="act")
                    cp(nAT, psf[:64, a1, :], [PF(a1)], nATn, eng="dve")
                    (P_, P_n), (PT_, PT_n) = Pb[pc]
                    (P2, P2n), (PT2, PT2n) = Pb[1 - pc]
                    p0 = grp(nAT, nATn, P_, P_n)
                    p1 = grp(nA_, nAn, PT_, PT_n)
                    tt(P2, psf[:64, p0, :], P_, ALU.add, [PF(p0)] + P_n, P2n)
                    tt(PT2, psf[:64, p1, :], PT_, ALU.add, [PF(p1)] + PT_n, PT2n)
                    cur = 1 - cur
                    pc = 1 - pc
                (Dm_, Dn), (DT_, DTn) = Pb[pc]
                (E_, En_), (ET_, ETn) = Pb[1 - pc]
                (U_, Un), (UT_, UTn) = pairs[0]
                g0 = grp(L1T, L1Tn, Dm_, Dn)
                g1 = grp(L1, L1n, DT_, DTn)
                cp(U_, psf[:64, g0, :], [PF(g0)], Un, eng="act")
                cp(UT_, psf[:64, g1, :], [PF(g1)], UTn, eng="dve")
                g2 = grp(DT_, DTn, U_, Un)
                g3 = grp(Dm_, Dn, UT_, UTn)
                tt(E_, Dm_, psf[:64, g2, :], ALU.subtract, Dn + [PF(g2)], En_)
                tt(ET_, DT_, psf[:64, g3, :], ALU.subtract, DTn + [PF(g3)], ETn)
                g4 = grp(L2, L2n, ET_, ETn)
                cp(UT_, psf[:64, g4, :], [PF(g4)], UTn, eng="act")
                g5 = grp(E_, En_, UT_, UTn)
                MinvT, MinvTn = Bv(5, 512, 512, 64)
                tt(MinvT, ET_, psf[:64, g5, :], ALU.subtract, ETn + [PF(g5)], MinvTn)
                S.tag = "gdnG2"
                s0 = fbank(2)
                for h in range(8):
                    mm(psf[:64, s0 + h // 4, (h % 4) * 128:(h % 4 + 1) * 128], kT(h), S_b[:, h, :], True, True, [KN[h], "S_b"], [PF(s0 + h // 4)])
                vb_, vbn = Bv(4, 0, 1024, 64); vb = h8(vb_)
                tt(vb, psf[:64, s0:s0 + 2, :].rearrange("p a (h v) -> p (a h) v", h=4), smB[:, c, 1, :].unsqueeze(2).to_broadcast([64, 8, 128]),
                   ALU.mult, [PF(s0), PF(s0 + 1), "smB"], vbn)
                tt(vb, bvt, vb, ALU.subtract, bvn + vbn, vbn)
                u0 = fbank(2)
                for h in range(8):
                    mm(psf[:64, u0 + h // 4, (h % 4) * 128:(h % 4 + 1) * 128], MinvT[:, h * 64:(h + 1) * 64], vb[:, h, :], True, True,
                       MinvTn + vbn, [PF(u0 + h // 4)])
                u_, un = Bv(4, 1024, 1024, 64); u = h8(u_)
                cp(u_.rearrange("p (a n) -> p a n", a=2), psf[:64, u0:u0 + 2, :], [PF(u0), PF(u0 + 1)], un, eng="act")
                o0 = fbank()
                for h in range(8):
                    hs = slice(h * 64, (h + 1) * 64)
                    mm(psf[:, o0, hs], u[:, h, :], Aqk(h), True, False, un + Aqkn, [PF(o0)])
                    mm(psf[:, o0, hs], S_b[:, h, :], qTe[:, hs], False, True, ["S_b"] + qTen, [PF(o0)])
                cp(oT[:, :, cs], psf[:, o0, :].rearrange("p (h i) -> p h i", h=8), [PF(o0)], [("oT", h) for h in range(8)], eng="act")
                n0 = fbank(2)
                for h in range(8):
                    mm(psf[:, n0 + h // 4, (h % 4) * 128:(h % 4 + 1) * 128], kd[:, h, :], u[:, h, :], True, True, kdn + un, [PF(n0 + h // 4)])
                elast = Eb.rearrange("p (h i) -> p h i", h=8)[:, :, 63:64].to_broadcast([128, 8, 128])
                tt(S_f[:], S_f[:], elast, ALU.mult, ["S_f"] + Ebn, ["S_f"])
                tt(S_f[:], psf[:, n0:n0 + 2, :].rearrange("p a (h v) -> p (a h) v", h=4), S_f[:], ALU.add, [PF(n0), PF(n0 + 1), "S_f"], ["S_f"])
                cp(S_b[:], S_f[:], ["S_f"], ["S_b"], eng="act")

            def ssd_chunk(c):
                S.tag = "ssd"
                cs = slice(c * 64, (c + 1) * 64)
                xts = [Bv(0, 0, 1024, 64), Bv(0, 1024, 1024, 64)]
                for half in range(2):
                    bk = bbank()
                    for j in range(8):
                        t = half * 8 + j
                        tr(psb[:64, bk, j * 128:(j + 1) * 128], XC[:, t, cs], identb[:, :], [("XC", t), "identb"], [PB(bk)])
                    cp(xts[half][0], psb[:64, bk, :], [PB(bk)], xts[half][1], eng="act")
                bk = bbank()
                for g in range(8):
                    tr(psb[:64, bk, g * 128:(g + 1) * 128], XC[:, 16 + g, cs], identb[:, :], [("XC", 16 + g), "identb"], [PB(bk)])
                B_tok, Btn = Bv(1, 0, 1024, 64)
                cp(B_tok, psb[:64, bk, :], [PB(bk)], Btn, eng="dve")
                cb = fbank()
                for g in range(8):
                    mm(psf[:64, cb, g * 64:(g + 1) * 64], XC[:, 16 + g, cs], XC[:, 24 + g, cs], True, True, [("XC", 16 + g), ("XC", 24 + g)], [PF(cb)])
                cbs, cbsn = Fv(0, 0, 512, 64)
                cp(cbs, psf[:64, cb, :], [PF(cb)], cbsn, eng="act")
                for half in range(2):
                    h0 = half * 16
                    x_tok, xtn = xts[half]
                    tt(YGb[0:64, :].rearrange("p (h i) -> p h i", h=16), triu[:, :].unsqueeze(1).to_broadcast([64, 16, 64]),
                       smG[:, c, 16 + h0:32 + h0].unsqueeze(2).to_broadcast([64, 16, 64]), ALU.mult, ["triu", "smG"], ["YGbhi"])
                    rm = fbank(2)
                    mm(psf[:64, rm, :], LCb[:, :], YGb[:, 0:512], True, True, ["LCb", "YGbhi", "YGblo"], [PF(rm)])
                    mm(psf[:64, rm + 1, :], LCb[:, :], YGb[:, 512:1024], True, True, ["LCb", "YGbhi", "YGblo"], [PF(rm + 1)])
                    r1 = fbank(2)
                    mm(psf[:, r1, :], onesb[0:64, :], YGb[0:64, 0:512], True, True, ["onesb", "YGbhi"], [PF(r1)])
                    mm(psf[:, r1 + 1, :], onesb[0:64, :], YGb[0:64, 512:1024], True, True, ["onesb", "YGbhi"], [PF(r1 + 1)])
                    E128, E128n = Fv(1, 0, 1024)
                    acti(E128.rearrange("p (a n) -> p a n", a=2), psf[:, r1:r1 + 2, :], AF.Exp, [PF(r1), PF(r1 + 1)], E128n)
                    CTe, CTen = Bv(2, 1024, 1024)
                    tt(CTe.rearrange("p (g e i) -> p g e i", g=4, e=4), XC[:, 24 + 4 * half:28 + 4 * half, cs].unsqueeze(2).to_broadcast([128, 4, 4, 64]),
                       E128.rearrange("p (g e i) -> p g e i", g=4, e=4), ALU.mult, [("XC", 24 + 4 * half + g) for g in range(4)] + E128n, CTen)
                    Dm, Dmn = Fv(2, 0, 1024, 64)
                    tt(Dm.rearrange("p (h i) -> p h i", h=16), psf[:64, rm:rm + 2, :].rearrange("p a (h i) -> p (a h) i", h=8),
                       smGS[:, c, h0:h0 + 16].unsqueeze(2).to_broadcast([64, 16, 64]), ALU.subtract, [PF(rm), PF(rm + 1), "smGS"], Dmn)
                    ex, exn = Bv(3, 1024, 1024, 64)
                    acti(ex, Dm, AF.Exp, Dmn, exn)
                    wT, wTn = Bv(3, 0, 1024, 64)
                    tt(wT.rearrange("p (g e i) -> p g e i", g=4, e=4),
                       cbs[:, 256 * half:256 * half + 256].rearrange("p (g i) -> p g i", g=4).unsqueeze(2).to_broadcast([64, 4, 4, 64]),
                       ex.rearrange("p (g e i) -> p g e i", g=4, e=4), ALU.mult, cbsn + exn, wTn)
                    xw, xwn = Bv(1, 1024, 1024, 64)
                    tt(xw.rearrange("p (h q) -> p h q", h=16), x_tok.rearrange("p (h q) -> p h q", h=16),
                       smWJ[:, c, h0:h0 + 16].unsqueeze(2).to_broadcast([64, 16, 64]), ALU.mult, xtn + ["smWJ"], xwn)
                    yb = fbank()
                    for hp in range(8):
                        for e_ in range(2):
                            hl = 2 * hp + e_
                            hg = h0 + hl
                            o_ap = psf[64 * e_:64 * e_ + 64, yb, hp * 64:(hp + 1) * 64]
                            mm(o_ap, x_tok[:, hl * 64:(hl + 1) * 64], wT[:, hl * 64:(hl + 1) * 64], True, False, xtn + wTn, [PF(yb)], tp=(0, 64 * e_))
                            mm(o_ap, H_b[:, hg, :], CTe[:, hl * 64:(hl + 1) * 64], False, True, ["H_b%d" % half] + CTen, [PF(yb)], tp=(0, 64 * e_))
                    cp(yT[:, 8 * half:8 * half + 8, cs], psf[:, yb, :].rearrange("p (h i) -> p h i", h=8), [PF(yb)],
                       [("yT", 8 * half + j) for j in range(8)], eng="act")
                    hb = fbank(2)
                    for gl in range(4):
                        g = 4 * half + gl
                        mm(psf[:, hb + gl // 2, (gl % 2) * 256:(gl % 2 + 1) * 256], B_tok[:, g * 128:(g + 1) * 128], xw[:, gl * 256:(gl + 1) * 256],
                           True, True, Btn + xwn, [PF(hb + gl // 2)])
                    Hh = H_f[:, h0:h0 + 16, :]
                    elast = E128.rearrange("p (h i) -> p h i", h=16)[:, :, 63:64].to_broadcast([128, 16, 64])
                    tt(Hh, Hh, elast, ALU.mult, ["H_f%d" % half] + E128n, ["H_f%d" % half])
                    tt(Hh, psf[:, hb:hb + 2, :].rearrange("p a (h q) -> p (a h) q", h=8), Hh, ALU.add, [PF(hb), PF(hb + 1), "H_f%d" % half], ["H_f%d" % half])
                    cp(H_b[:, h0:h0 + 16, :], Hh, ["H_f%d" % half], ["H_b%d" % half], eng="act")

            def zproj_tile(col, k):
                raise NotImplementedError

            def rms_feat(sq_list, ktiles, scale):
                pb = fbank()
                for i, (sq, sqn) in enumerate(sq_list):
                    mm(psf[:, pb, 0:T], onesb[:], sq, i == 0, i == len(sq_list) - 1, ["onesb", sqn], [PF(pb)])
                return pb

            def gdn_finalize():
                S.tag = "gdnfin"
                for h in range(8):
                    o = (h % 2) * 512
                    sq, sql = Bv(2, 1024 + o, T); sqn = sql[0]
                    acti(sq, oT[:, h, 0:T], AF.Square, [("oT", h)], [sqn])
                    pb = rms_feat([(sq, sqn)], 1, 1.0)
                    ln, lnl = Fv(1, o, T); lnn = lnl[0]
                    acti(ln, psf[:, pb, 0:T], AF.Ln, [PF(pb)], [lnn], bias=EPS, scale=1.0 / 128)
                    rs, rsl = Bv(3, o, T); rsn = rsl[0]
                    acti(rs, ln, AF.Exp, [lnn], [rsn], scale=-0.5)
                    tt(oT[:, h, 0:T], oT[:, h, 0:T], rs, ALU.mult, [("oT", h), rsn], [("oT", h)])
                wv, wn = wload(wi_s, "wi_s", OFF_GZ, 1024, 8)
                for h in range(8):
                    o = (h % 2) * 512
                    pz = fbank()
                    for kc in range(8):
                        mm(psf[:, pz, 0:T], wv[:, kc, h * 128:(h + 1) * 128], hT[:, kc, 0:T], kc == 0, kc == 7, [wn, "hT"], [PF(pz)])
                    zs, zsl = Bv(4, o, T); zsn = zsl[0]
                    acti(zs, psf[:, pz, 0:T], AF.Silu, [PF(pz)], [zsn])
                    stt(oT[:, h, 0:T], oT[:, h, 0:T], gnT[:, 0:1], zs, ALU.mult, ALU.mult, [("oT", h), "gnT", zsn], [("oT", h)])

            def ssd_finalize():
                S.tag = "ssdfin"
                for blk in range(2):
                    wv, wn = wload(wi_s, "wi_s", OFF_SZ + blk * 1024, 1024, 8)
                    for jj in range(8):
                        hp = blk * 8 + jj
                        o = (hp % 2) * 512
                        pz = fbank()
                        for kc in range(8):
                            mm(psf[:, pz, 0:T], wv[:, kc, jj * 128:(jj + 1) * 128], hT[:, kc, 0:T], kc == 0, kc == 7, [wn, "hT"], [PF(pz)])
                        zs, zsl = Bv(4, o, T); zsn = zsl[0]
                        acti(zs, psf[:, pz, 0:T], AF.Silu, [PF(pz)], [zsn])
                        stt(yT[:, hp, 0:T], XC[:, hp, 0:T], dcol[:, hp:hp + 1], yT[:, hp, 0:T], ALU.mult, ALU.add, [("XC", hp), "dcol", ("yT", hp)], [("yT", hp)])
                        tt(yT[:, hp, 0:T], yT[:, hp, 0:T], zs, ALU.mult, [("yT", hp), zsn], [("yT", hp)])
                for g in range(8):
                    sqs = []
                    for e_ in range(2):
                        hp = 2 * g + e_
                        sq, sql = Bv(2, 1024 + e_ * 512, T); sqn = sql[0]
                        acti(sq, yT[:, hp, 0:T], AF.Square, [("yT", hp)], [sqn])
                        sqs.append((sq, sqn))
                    pb = rms_feat(sqs, 2, 1.0)
                    o = (g % 2) * 512
                    ln, lnl = Fv(1, o, T); lnn = lnl[0]
                    acti(ln, psf[:, pb, 0:T], AF.Ln, [PF(pb)], [lnn], bias=EPS, scale=1.0 / 256)
                    rs, rsl = Bv(3, o, T); rsn = rsl[0]
                    acti(rs, ln, AF.Exp, [lnn], [rsn], scale=-0.5)
                    for e_ in range(2):
                        hp = 2 * g + e_
                        stt(yT[:, hp, 0:T], yT[:, hp, 0:T], snT[:, hp:hp + 1], rs, ALU.mult, ALU.mult, [("yT", hp), "snT", rsn], [("yT", hp)])

            MX = lambda f: ARB[4 + f // 4][:, (f % 4) * 512:(f % 4) * 512 + T]
            MXN = lambda f: "arB%d_%d" % (4 + f // 4, (f % 4) * 512)

            def merge_out():
                S.tag = "merge"
                for blk in range(2):
                    wv, wn = wload(wi_s, "wi_s", OFF_MG + blk * 1024, 1024, 8)
                    for jj in range(8):
                        t = blk * 8 + jj
                        pz = fbank()
                        for kc in range(8):
                            mm(psf[:, pz, 0:T], wv[:, kc, jj * 128:(jj + 1) * 128], hT[:, kc, 0:T], kc == 0, kc == 7, [wn, "hT"], [PF(pz)])
                        acti(XC[:, t, 0:T], psf[:, pz, 0:T], AF.Sigmoid, [PF(pz)], [("XC", t)])
                wv, wn = wload(wbg_s, "wbg_s", 0, 1024, 8)
                for f in range(8):
                    pz = fbank()
                    for kc in range(8):
                        mm(psf[:, pz, 0:T], wv[:, kc, f * 128:(f + 1) * 128], oT[:, kc, 0:T], kc == 0, kc == 7, [wn, ("oT", kc)], [PF(pz)])
                    tt(MX(f), psf[:, pz, 0:T], XC[:, f, 0:T], ALU.mult, [PF(pz), ("XC", f)], [MXN(f)])
                for blk in range(2):
                    wv, wn = wload(wbs_s, "wbs_s", blk * 512, 512, 16)
                    for jj in range(4):
                        f = blk * 4 + jj
                        pz = fbank()
                        for kc in range(16):
                            mm(psf[:, pz, 0:T], wv[:, kc, jj * 128:(jj + 1) * 128], yT[:, kc, 0:T], kc == 0, kc == 15, [wn, ("yT", kc)], [PF(pz)])
                        tmp = ARB[3][:, (f % 2) * 512:(f % 2) * 512 + T]; tmpn = "arB3_%d" % ((f % 2) * 512)
                        tt(tmp, psf[:, pz, 0:T], XC[:, 8 + f, 0:T], ALU.mult, [PF(pz), ("XC", 8 + f)], [tmpn])
                        tt(MX(f), MX(f), tmp, ALU.add, [MXN(f), tmpn], [MXN(f)])
                wv, wn = wload(wo_s, "wo_s", 0, 1024, 8)
                for nt in range(NT):
                    for hf in range(2):
                        pz = fbank()
                        for kc in range(8):
                            mm(psf[:TT, pz, :], MX(kc)[:, nt * TT:(nt + 1) * TT], wv[:, kc, hf * 512:(hf + 1) * 512], kc == 0, kc == 7,
                               [wn, MXN(kc)], [PF(pz)])
                        tmp = ARF[1][:TT, 0:512] if hf == 0 else ARF[1][:TT, 512:1024]
                        tmpn = "arF1_%d" % (hf * 512)
                        tt(tmp, psf[:TT, pz, :], gtb[:TT, 0, hf * 512:(hf + 1) * 512], ALU.mult, [PF(pz), "gtb"], [tmpn])
                        tt(x_sb[:TT, nt, hf * 512:(hf + 1) * 512], x_sb[:TT, nt, hf * 512:(hf + 1) * 512], tmp, ALU.add, [("x", nt), tmpn], [("x", nt)])

            def ffn():
                S.tag = "ffn"
                k = 0
                for b0 in range(0, 22, 4):
                    nb_ = min(4, 22 - b0)
                    b = wst["i"] % 2
                    wst["i"] += 1
                    wn = "wblk%d" % b
                    view = wblk[:, b, 0:8 * 2 * nb_ * 128].rearrange("p (k a n) -> p k a n", k=8, a=2)
                    for a_ in range(2):
                        src = wu_s[:, a_ * DFF + b0 * 128:a_ * DFF + (b0 + nb_) * 128].rearrange("(k p) n -> p k n", p=128)
                        S.dma("sp", "w%d" % b, lambda e, a_=a_, src=src, view=view: e.dma_start(out=view[:, :, a_, :], in_=src), r=["wu_s"], w=[wn])
                    for jj in range(nb_):
                        j = b0 + jj
                        res = []
                        for a_ in range(2):
                            pz = fbank()
                            for kc in range(8):
                                mm(psf[:, pz, 0:T], view[:, kc, a_, jj * 128:(jj + 1) * 128], hT[:, kc, 0:T], kc == 0, kc == 7, [wn, "hT"], [PF(pz)])
                            ct = a_ * 22 + j
                            o = (k % 2) * 512
                            if a_ == 0:
                                dst = ARB[3][:, o:o + T]; dstn = "arB3_%d" % o
                                fn_ = AF.Silu
                            else:
                                dst = ARB[3][:, 1024 + o:1024 + o + T]; dstn = "arB3_%d" % (1024 + o)
                                fn_ = None
                            conv_tile(psf[:, pz, 0:T], PF(pz), wcf[:, ct, :], bcf[:, ct:ct + 1], 3, carryf[:, ct, :], "carryf%d" % ct,
                                      dst, dstn, 0 if a_ == 0 else 2, fn_, k)
                            res.append((dst, dstn))
                        tt(XC[:, j, 0:T], res[0][0], res[1][0], ALU.mult, [res[0][1], res[1][1]], [("XC", j)])
                        k += 1
                for cb_ in range(4):
                    wv, wn = wload(wd_s, "wd_s", cb_ * 256, 256, 22)
                    for nt in range(NT):
                        pz = fbank()
                        for kc in range(22):
                            mm(psf[:TT, pz, 0:256], XC[:, kc, nt * TT:(nt + 1) * TT], wv[:, kc, :], kc == 0, kc == 21, [wn, ("XC", kc)], [PF(pz)])
                        tmp = ARF[1][:TT, (nt % 2) * 512:(nt % 2) * 512 + 256]; tmpn = "arF1_%d" % ((nt % 2) * 512)
                        tt(tmp, psf[:TT, pz, 0:256], gtb[:TT, 1, cb_ * 256:(cb_ + 1) * 256], ALU.mult, [PF(pz), "gtb"], [tmpn])
                        tt(x_sb[:TT, nt, cb_ * 256:(cb_ + 1) * 256], x_sb[:TT, nt, cb_ * 256:(cb_ + 1) * 256], tmp, ALU.add, [("x", nt), tmpn], [("x", nt)])

            def final_norm_store(sc):
                S.tag = "final"
                junk, junkn = Bv(2, 0, 1024, TT)
                for nt in range(NT):
                    acti(junk, x_sb[:TT, nt, :], AF.Square, [("x", nt)], junkn + ["ssq"], accum=ssq[:TT, nt:nt + 1])
                acti(ssq[:TT, 4:4 + NT], ssq[:TT, 0:NT], AF.Ln, ["ssq"], ["ssq"], bias=EPS, scale=1.0 / D)
                acti(ssq[:TT, 8:8 + NT], ssq[:TT, 4:4 + NT], AF.Exp, ["ssq"], ["ssq"], scale=-0.5)
                for nt in range(NT):
                    yb_ = ARF[1 + nt % 2][:TT, :]; ybn = ["arF%d_0" % (1 + nt % 2), "arF%d_512" % (1 + nt % 2)]
                    acti(yb_, x_sb[:TT, nt, :], AF.Copy, [("x", nt), "ssq"], ybn, scale=ssq[:TT, 8 + nt:9 + nt])
                    tt(yb_, yb_, wfin[:TT, :], ALU.mult, ybn + ["wfin"], ybn)
                    r0 = sc * T + nt * TT
                    S.dma("pool", "yst", lambda e, yb_=yb_, r0=r0: e.dma_start(out=y_d[r0:r0 + TT, :], in_=yb_), r=ybn)

            def yg_masks(gdn):
                for hh in range(16):
                    src = mut_d if (not gdn or hh < 8) else msut_d
                    S.dma("sp", "ld", lambda e, hh=hh, src=src: e.dma_start(out=YG[64:128, hh * 64:(hh + 1) * 64], in_=src[64:128, :]), w=["YGlo"])

            for sc in range(nsc):
                for nt in range(NT):
                    r0 = sc * T + nt * TT
                    load(x_sb[:TT, nt, :], x_d[r0:r0 + TT, :], [("x", nt)], key="xld%d" % nt)
                rms_to_hT(sc1, 0)
                if stages >= 1:
                    proj_conv(0, 24, 0, 0, direct=(sc == 0 and not has_init))
                    l2norm_tiles(list(range(16)), set(range(8)))
                    smalls()
                if dbg and sc == nsc - 1 and stages == 1:
                    dump(hT[:, 0, 0:64], ["hT"], 128, 64)
                    dump(XC[:, 0, 0:64], [("XC", 0)], 128, 64)
                    dump(XC[:, 8, 0:64], [("XC", 8)], 128, 64)
                    dump(XC[:, 16, 0:64], [("XC", 16)], 128, 64)
                    dump(smL[:, 0, :], ["smL"], 64, 48)
                    dump(smGAM[:, 0, :], ["smGAM"], 64, 48)
                    dump(smREV[:, 0, :], ["smREV"], 64, 48)
                if stages >= 2:
                    if sc == 0:
                        yg_masks(True)
                    for c in range(NCH):
                        gdn_chunk(c, has_init)
                if dbg and sc == nsc - 1 and stages == 2:
                    for h in range(8):
                        dump(oT[:, h, T - 64:T], [("oT", h)], 128, 64)
                if stages >= 3:
                    gdn_finalize()
                    proj_conv(3072, 32, 0, 24)
                    for c in range(NCH):
                        ssd_chunk(c)
                if dbg and sc == nsc - 1 and stages == 3:
                    for h in range(16):
                        dump(yT[:, h, T - 64:T], [("yT", h)], 128, 64)
                if stages >= 4:
                    ssd_finalize()
                    merge_out()
                    rms_to_hT(sc2, 24)
                    ffn()
                final_norm_store(sc)

            S.dma("pool", "sst", lambda e: e.dma_start(out=outs["gd_" + sfx].rearrange("h k v -> k h v"), in_=S_f[:]), r=["S_f"])
            S.dma("pool", "sst", lambda e: e.dma_start(out=outs["ss_" + sfx].rearrange("h n q -> n h q"), in_=H_f[:]), r=["H_f0", "H_f1"])
            for (cr, names, ntile, nr, key) in ((carry, CARRY, 56, 3, "cm_"), (carryf, CARRYF, 44, 2, "cf_")):
                ncol = ntile * nr
                cp(cst[:, 0:ncol].rearrange("p (r t) -> p r t", r=nr), cr[:].rearrange("p t r -> p r t"), names, ["cst"])
                for c0 in range(0, ncol, 128):
                    n = min(128, ncol - c0)
                    pb = fbank()
                    S.op("pe", lambda e, pb=pb, c0=c0, n=n: e.transpose(psf[:n, pb, 0:128], cst[:, c0:c0 + n], identf[:, :]), ["cst", "identf"], [PF(pb)])
                    o_sb = ARF[1][:n, 0:128]
                    cp(o_sb, psf[:n, pb, 0:128], [PF(pb)], ["arF1_0"], eng="act")
                    S.dma("pool", "sst", lambda e, o_sb=o_sb, c0=c0, n=n, key=key: e.dma_start(out=outs[key + sfx][c0:c0 + n, :], in_=o_sb), r=["arF1_0"])

        stream(0, xp, yp, ntok_p, 512, False, "p")
        stream(1, xs, ys, 64, 64, True, "s")
        S.finish("pool")
        emit_program(nc, S, st)
    return nc


def _prep_shared(inp):
    f = np.float32
    A = lambda x: np.ascontiguousarray(x, dtype=f)
    sh = {}
    sh["w_ada"] = A(inp["w_ada"][0]); sh["w_in"] = A(inp["w_in"][0])
    sh["b_adaT"] = A(inp["b_ada"][0].reshape(48, 128).T)
    sh["wnT"] = A(np.concatenate([inp["w_norm_mix"][0].reshape(8, 128).T, inp["w_norm_ffn"][0].reshape(8, 128).T], axis=1))
    sh["wfin"] = A(inp["w_norm_final"].reshape(1, D))
    sh["wcm"] = A(inp["w_conv_mix"][0].reshape(4, 56, 128).transpose(2, 1, 0).reshape(128, 56 * 4))
    sh["bcm"] = A(inp["b_conv_mix"][0].reshape(56, 128).T)
    sh["wcf"] = A(inp["w_ffn_conv"][0].reshape(3, 44, 128).transpose(2, 1, 0).reshape(128, 44 * 3))
    sh["bcf"] = A(inp["b_ffn_conv"][0].reshape(44, 128).T)
    sh["gpar"] = A(np.concatenate([inp["gdn_a_log"][0], inp["gdn_dt_bias"][0]]).reshape(1, 16))
    sh["spar"] = A(np.concatenate([inp["ssd_a_log"][0], inp["ssd_dt_bias"][0]]).reshape(1, 64))
    sh["dcol"] = A(np.repeat(inp["ssd_d"][0], 64).reshape(16, 128).T)
    sh["gnT"] = A(inp["gdn_norm"][0].reshape(128, 1))
    sh["snT"] = A(inp["ssd_norm"][0].reshape(16, 128).T)
    sh["w_bg"] = A(inp["w_branch_gdn"][0]); sh["w_bs"] = A(inp["w_branch_ssd"][0]); sh["w_out"] = A(inp["w_out"][0])
    sh["w_up"] = A(inp["w_ffn_up"][0]); sh["w_down"] = A(inp["w_ffn_down"][0])
    sh["ident"] = np.eye(128, dtype=f)
    lc = np.zeros((128, 64), f); lc[:64] = 1.0; lc[64:] = np.eye(64, dtype=f)
    sh["lc"] = lc
    t = np.arange(64)
    sh["triu"] = (t[:, None] <= t[None, :]).astype(f)
    sh["trisl"] = (t[:, None] > t[None, :]).astype(f)
    mut = np.zeros((128, 64), f); msut = np.zeros((128, 64), f)
    mut[64:] = np.where(t[:, None] <= t[None, :], 0.0, NEG)
    msut[64:] = np.where(t[:, None] < t[None, :], 0.0, NEG)
    sh["mut"] = mut; sh["msut"] = msut
    b16 = t // 16; b32 = t // 32
    md = (b16[:, None] == b16[None, :]); m2 = (b32[:, None] != b32[None, :]); m1 = (~md) & (~m2)
    sh["blkm"] = np.ascontiguousarray(np.stack([md, m1, m2], axis=1).astype(f).reshape(64, 192))
    return sh


def kernel(x_prompt, x_sample, state_conv_mix, state_gdn, state_ssd, state_conv_ffn, c_prompt, c_sample,
           w_ada, b_ada, w_norm_mix, w_in, w_conv_mix, b_conv_mix, gdn_a_log, gdn_dt_bias, gdn_norm,
           ssd_a_log, ssd_dt_bias, ssd_d, ssd_norm, w_branch_gdn, w_branch_ssd, w_out, w_norm_ffn,
           w_ffn_up, w_ffn_conv, b_ffn_conv, w_ffn_down, w_norm_final, _ntok_p=SEQ, _stages=99, _dbg=False, _trace=False):
    inp = dict(w_ada=w_ada, b_ada=b_ada, w_norm_mix=w_norm_mix, w_in=w_in, w_conv_mix=w_conv_mix, b_conv_mix=b_conv_mix,
               gdn_a_log=gdn_a_log, gdn_dt_bias=gdn_dt_bias, gdn_norm=gdn_norm, ssd_a_log=ssd_a_log, ssd_dt_bias=ssd_dt_bias,
               ssd_d=ssd_d, ssd_norm=ssd_norm, w_branch_gdn=w_branch_gdn, w_branch_ssd=w_branch_ssd, w_out=w_out,
               w_norm_ffn=w_norm_ffn, w_ffn_up=w_ffn_up, w_ffn_conv=w_ffn_conv, b_ffn_conv=b_ffn_conv, w_ffn_down=w_ffn_down,
               w_norm_final=w_norm_final)
    inp = {k: np.asarray(v) for k, v in inp.items()}
    sh = _prep_shared(inp)
    f = np.float32
    x_prompt = np.asarray(x_prompt); x_sample = np.asarray(x_sample)
    c_prompt = np.asarray(c_prompt); c_sample = np.asarray(c_sample)
    scm = np.asarray(state_conv_mix)[0]; sgd = np.asarray(state_gdn)[0]; sss = np.asarray(state_ssd)[0]; scf = np.asarray(state_conv_ffn)[0]
    in_maps = []
    for c in range(8):
        m = dict(sh)
        b = c % 4
        m["xp"] = np.ascontiguousarray(x_prompt[b, :_ntok_p], dtype=f)
        m["xs"] = np.ascontiguousarray(x_sample[c], dtype=f)
        cT = np.zeros((128, 16), f)
        cT[:, 0::2] = c_prompt[b].reshape(8, 128).T
        cT[:, 1::2] = c_sample[c].reshape(8, 128).T
        m["cT"] = cT
        m["cm0T"] = np.ascontiguousarray(scm[c].reshape(3, 56, 128).transpose(2, 1, 0).reshape(128, 56 * 3), dtype=f)
        m["s0"] = np.ascontiguousarray(sgd[c].transpose(1, 0, 2).reshape(128, 8 * 128), dtype=f)
        m["h0"] = np.ascontiguousarray(sss[c].transpose(1, 0, 2).reshape(128, 32 * 64), dtype=f)
        m["cf0T"] = np.ascontiguousarray(scf[c].reshape(2, 44, 128).transpose(2, 1, 0).reshape(128, 44 * 2), dtype=f)
        in_maps.append(m)
    nc = build(_ntok_p, _stages, _dbg)
    if _trace:
        res = run_bass_kernel_spmd(nc, in_maps, core_ids=list(range(8)), trace=True)
        print('EXEC_TIME_NS', res.exec_time_ns)
    else:
        res = run_bass_kernel_spmd(nc, in_maps, core_ids=list(range(8)))
    R = res.results
    y_p = np.stack([R[b]["yp"] for b in range(4)])
    y_s = np.stack([R[c]["ys"] for c in range(8)])

    def gather(key, cores, shape):
        return np.stack([np.asarray(R[c][key]).reshape(shape) for c in cores])[None]
    out = (y_p, y_s,
           gather("cm_p", range(4), (3, CONV_CH)), gather("gd_p", range(4), (8, 128, 128)),
           gather("ss_p", range(4), (32, 128, 64)), gather("cf_p", range(4), (2, 2 * DFF)),
           gather("cm_s", range(8), (3, CONV_CH)), gather("gd_s", range(8), (8, 128, 128)),
           gather("ss_s", range(8), (32, 128, 64)), gather("cf_s", range(8), (2, 2 * DFF)))
    out = tuple(np.ascontiguousarray(o, dtype=f) for o in out)
    if _dbg:
        return out, R
    return out
```

```python
from contextlib import ExitStack
import numpy as np
import concourse.bass as bass
import concourse.mybir as mybir
from concourse.bass_utils import run_bass_kernel_spmd

F32 = mybir.dt.float32
BF16 = mybir.dt.bfloat16
AF = mybir.ActivationFunctionType
ALU = mybir.AluOpType
ENGS = ("pe", "act", "dve", "pool", "sp")

D = 1024
SEQ = 8192
DFF = 2816
CONV_CH = 7168
IN_COLS = 12336
OFF_GZ = 7168
OFF_SZ = 8192
OFF_SM = 10240
OFF_MG = 10288
EPS = 1e-6
NEG = -30000.0


class Sched:
    WINDOW = 32

    def __init__(self):
        self.trace = []
        self.final_queue = None
        self.tag = ""

    def op(self, eng, fn, r=(), w=(), cost=0.3):
        self.trace.append(dict(eng=eng, fn=fn, kind="c", key=None, r=tuple(r), w=tuple(w), cost=cost, tag=self.tag))

    def dma(self, queue, key, fn, r=(), w=(), cost=3.0):
        self.trace.append(dict(eng=queue, fn=fn, kind="d", key=key, r=tuple(r), w=tuple(w), cost=cost, tag=self.tag))

    def finish(self, queue):
        self.final_queue = queue

    def schedule(self):
        tr = self.trace
        n = len(tr)
        deps = [set() for _ in range(n)]
        state = {}
        dma_by_key = {}
        for i, o in enumerate(tr):
            d = deps[i]
            for nm in o["r"]:
                st = state.get(nm)
                if st and st[0] is not None:
                    d.add(st[0])
            for nm in o["w"]:
                st = state.get(nm)
                if st:
                    if st[0] is not None:
                        d.add(st[0])
                    d.update(st[1])
            for j in list(d):
                if tr[j]["kind"] == "d":
                    d.add(dma_by_key[tr[j]["key"]][-1])
            d.discard(i)
            for nm in o["r"]:
                st = state.setdefault(nm, [None, []])
                st[1].append(i)
            for nm in o["w"]:
                state[nm] = [i, []]
            if o["kind"] == "d":
                dma_by_key.setdefault(o["key"], []).append(i)
        self.deps = deps
        self.dma_by_key = dma_by_key
        users = [[] for _ in range(n)]
        ndep = [0] * n
        for i in range(n):
            ndep[i] = len(deps[i])
            for j in deps[i]:
                users[j].append(i)
        queues = {e: [] for e in ENGS}
        for i, o in enumerate(tr):
            queues[o["eng"]].append(i)
        ptr = {e: 0 for e in ENGS}
        window = {e: [] for e in ENGS}
        inorder = {"sp", "pool"}
        fin = [0.0] * n
        ready = [0.0] * n
        etime = {e: 0.0 for e in ENGS}
        order = {e: [] for e in ENGS}
        done = [False] * n

        def refill(e):
            w_ = window[e]
            q = queues[e]
            lim = 1 if e in inorder else self.WINDOW
            while len(w_) < lim and ptr[e] < len(q):
                w_.append(q[ptr[e]])
                ptr[e] += 1
        for e in ENGS:
            refill(e)
        remaining = n
        cur_tbl = [None]
        while remaining:
            best = None
            for e in ENGS:
                et = etime[e]
                for i in window[e]:
                    if ndep[i]:
                        continue
                    s_ = ready[i] if ready[i] > et else et
                    if e == "act":
                        tb = tr[i].get("tbl")
                        if tb is not None and tb != cur_tbl[0]:
                            s_ += 1.3
                    if best is None or s_ < best[0] - 1e-9 or (abs(s_ - best[0]) <= 1e-9 and i < best[1]):
                        best = (s_, i, e)
                    if e in inorder:
                        break
            s_, i, e = best
            o = tr[i]
            if o["kind"] == "d":
                etime[e] = s_ + 0.06
                fin[i] = s_ + o["cost"]
            else:
                etime[e] = s_ + o["cost"]
                fin[i] = s_ + o["cost"] + 0.15
                if e == "act" and o.get("tbl") is not None:
                    cur_tbl[0] = o["tbl"]
            done[i] = True
            order[e].append(i)
            window[e].remove(i)
            refill(e)
            for u in users[i]:
                ndep[u] -= 1
                if fin[i] > ready[u]:
                    ready[u] = fin[i]
            remaining -= 1
        self.order = order
        self.est_time = max(fin) if n else 0.0


def emit_program(nc, S, stack):
    S.schedule()
    tr = S.trace
    pos = {}
    for e in ENGS:
        for k, i in enumerate(S.order[e]):
            pos[i] = k
    dma_seq = {}
    for key, lst in S.dma_by_key.items():
        for k, i in enumerate(lst):
            dma_seq[i] = k + 1
    needed = {e: set() for e in ENGS}
    waits = {}
    for e in ENGS:
        waited = {}
        for i in S.order[e]:
            w_ = {}
            for j in S.deps[i]:
                oj = tr[j]
                if oj["kind"] == "d":
                    tgt = ("dma", oj["key"]); val = dma_seq[j]
                else:
                    tgt = oj["eng"]; val = pos[j] + 1
                    if e == "pe" and tgt == "pe":
                        continue
                if w_.get(tgt, 0) < val:
                    w_[tgt] = val
            out = {}
            for tgt, val in w_.items():
                if waited.get(tgt, 0) >= val:
                    continue
                waited[tgt] = val
                out[tgt] = val
                if not isinstance(tgt, tuple):
                    needed[tgt].add(val)
            waits[i] = out
    sems = {}
    for e in ENGS:
        sems[e] = stack.enter_context(nc.semaphore("s_" + e))
    for key in S.dma_by_key:
        sems[("dma", key)] = stack.enter_context(nc.semaphore("d_" + str(key)))
    cnt = {}
    for e in ENGS:
        c = 0
        m = {}
        for k, i in enumerate(S.order[e]):
            if tr[i]["kind"] == "c" and (k + 1) in needed[e]:
                c += 1
                m[k + 1] = c
        cnt[e] = m

    def wait_val(tgt, val):
        if isinstance(tgt, tuple):
            return 16 * val
        return cnt[tgt][val]

    block = stack.enter_context(nc.Block())

    def run(e):
        def body(h):
            for k, i in enumerate(S.order[e]):
                o = tr[i]
                for tgt, val in waits[i].items():
                    h.wait_ge(sems[tgt], wait_val(tgt, val))
                ins = o["fn"](h)
                if o["kind"] == "d":
                    ins.then_inc(sems[("dma", o["key"])], 16)
                elif (k + 1) in cnt[e]:
                    ins.then_inc(sems[e], 1)
            if e == S.final_queue:
                for key, lst in S.dma_by_key.items():
                    h.wait_ge(sems[("dma", key)], 16 * len(lst))
        return body

    block.tensor(run("pe"))
    block.scalar(run("act"))
    block.vector(run("dve"))
    block.gpsimd(run("pool"))
    block.sync(run("sp"))


def build(ntok_p, stages=99, dbg=False):
    nc = bass.Bass("TRN2", target_bir_lowering=False)
    S = Sched()
    st = ExitStack()
    with st:
        def din(name, shape, dt=F32):
            return nc.dram_tensor(name, list(shape), dt, kind="ExternalInput").ap()

        def dout(name, shape, dt=F32):
            return nc.dram_tensor(name, list(shape), dt, kind="ExternalOutput").ap()

        def dscr(name, shape, dt=BF16):
            return nc.dram_tensor(name, list(shape), dt, kind="Internal").ap()

        def sb(name, shape, dt):
            return st.enter_context(nc.sbuf_tensor(name + "_t", list(shape), dt))

        xp = din("xp", [ntok_p, D]); xs = din("xs", [64, D])
        cT_d = din("cT", [128, 16]); b_adaT_d = din("b_adaT", [128, 48]); wnT_d = din("wnT", [128, 16])
        wfin_d = din("wfin", [1, D])
        w_ada_d = din("w_ada", [D, 6 * D]); w_in_d = din("w_in", [D, IN_COLS])
        wcm_d = din("wcm", [128, 56 * 4]); bcm_d = din("bcm", [128, 56])
        wcf_d = din("wcf", [128, 44 * 3]); bcf_d = din("bcf", [128, 44])
        gpar_d = din("gpar", [1, 16]); spar_d = din("spar", [1, 64])
        dcol_d = din("dcol", [128, 16]); gnT_d = din("gnT", [128, 1]); snT_d = din("snT", [128, 16])
        w_bg_d = din("w_bg", [D, D]); w_bs_d = din("w_bs", [2 * D, D]); w_out_d = din("w_out", [D, D])
        w_up_d = din("w_up", [D, 2 * DFF]); w_down_d = din("w_down", [DFF, D])
        cm0_d = din("cm0T", [128, 56 * 3]); s0_d = din("s0", [128, 8 * 128]); h0_d = din("h0", [128, 32 * 64])
        cf0_d = din("cf0T", [128, 44 * 2])
        ident_d = din("ident", [128, 128]); lc_d = din("lc", [128, 64])
        triu_d = din("triu", [64, 64]); trisl_d = din("trisl", [64, 64])
        mut_d = din("mut", [128, 64]); msut_d = din("msut", [128, 64]); blkm_d = din("blkm", [64, 192])

        yp = dout("yp", [ntok_p, D]); ys = dout("ys", [64, D])
        outs = {}
        for sfx in ("p", "s"):
            outs["cm_" + sfx] = dout("cm_" + sfx, [3 * 56, 128])
            outs["gd_" + sfx] = dout("gd_" + sfx, [8, 128, 128])
            outs["ss_" + sfx] = dout("ss_" + sfx, [32, 128, 64])
            outs["cf_" + sfx] = dout("cf_" + sfx, [2 * 44, 128])
        dbg_o = dout("dbg", [128, 16384]) if dbg else None

        wi_s = dscr("wi_s", [D, IN_COLS])
        wbg_s = dscr("wbg_s", [D, D]); wbs_s = dscr("wbs_s", [2 * D, D]); wo_s = dscr("wo_s", [D, D])
        wu_s = dscr("wu_s", [D, 2 * DFF]); wd_s = dscr("wd_s", [DFF, D])

        x_sb = sb("x_sb", [128, 4, D], F32)
        hT = sb("hT", [128, 8, 512], BF16)
        XC = sb("XC", [128, 32, 512], BF16)
        oT = sb("oT", [128, 8, 512], BF16)
        yT = sb("yT", [128, 16, 512], BF16)
        wblk = sb("wblk", [128, 2, 8192], BF16)
        S_f = sb("S_f", [128, 8, 128], F32); S_b = sb("S_b", [128, 8, 128], BF16)
        H_f = sb("H_f", [128, 32, 64], F32); H_b = sb("H_b", [128, 32, 64], BF16)
        ARF = [sb("arF%d" % i, [128, 1024], F32) for i in range(3)]
        ARB = [sb("arB%d" % i, [128, 2048], BF16) for i in range(6)]
        stg = sb("stg", [128, 2, 516], BF16)
        YG = sb("YG", [128, 1024], F32)
        YGb = sb("YGb", [128, 1024], BF16)
        LCb = sb("LCb", [128, 64], BF16)
        carry = sb("carry", [128, 56, 3], BF16); carryf = sb("carryf", [128, 44, 2], BF16)
        identf = sb("identf", [128, 128], F32); identb = sb("identb", [128, 128], BF16)
        onesf = sb("onesf", [128, 128], F32); onesb = sb("onesb", [128, 128], BF16)
        LC = sb("LC", [128, 64], F32)
        triu = sb("triu", [64, 64], F32); trisl = sb("trisl", [64, 64], F32)
        blkm = sb("blkm", [64, 3, 64], F32)
        cT = sb("cT", [128, 16], F32); scT = sb("scT", [128, 16], BF16)
        b_adaT = sb("b_adaT", [128, 48], F32); wnT = sb("wnT", [128, 16], F32)
        modT = sb("modT", [128, 48, 2], F32)
        sc1 = sb("sc1", [128, 8], F32); sc2 = sb("sc2", [128, 8], F32)
        gtb = sb("gtb", [128, 2, D], F32)
        wfin = sb("wfin", [128, D], F32)
        wcm = sb("wcm", [128, 56, 4], F32); bcm = sb("bcm", [128, 56], F32)
        wcf = sb("wcf", [128, 44, 3], F32); bcf = sb("bcf", [128, 44], F32)
        gpar = sb("gpar", [64, 16], F32); spar = sb("spar", [64, 64], F32)
        nA = sb("nA", [64, 48], F32)
        bias48 = sb("bias48", [64, 48], F32)
        dcol = sb("dcol", [128, 16], F32); gnT = sb("gnT", [128, 1], F32); snT = sb("snT", [128, 16], F32)
        wsm = sb("wsm", [128, 8, 48], BF16)
        smT = sb("smT", [64, 8, 48], F32); smL = sb("smL", [64, 8, 48], F32); smG = sb("smG", [64, 8, 48], F32)
        smGAM = sb("smGAM", [64, 8, 48], F32); smREV = sb("smREV", [64, 8, 48], F32)
        smWJ = sb("smWJ", [64, 8, 32], F32); smGG = sb("smGG", [64, 8, 2, 8], F32)
        smB = sb("smB", [64, 8, 2, 8], F32)
        smGS = sb("smGS", [64, 8, 32], F32)
        ssq = sb("ssq", [128, 16], F32)
        cst = sb("cst", [128, 3 * 56], F32)
        psf = st.enter_context(nc.psum_tensor("psf", [128, 6, 512], F32))
        psb = st.enter_context(nc.psum_tensor("psb", [128, 2, 1024], BF16))

        def Bv(k, lo, n, parts=128):
            return ARB[k][:parts, lo:lo + n], ["arB%d_%d" % (k, o) for o in range((lo // 512) * 512, lo + n, 512)]

        def Fv(k, lo, n, parts=128):
            return ARF[k][:parts, lo:lo + n], ["arF%d_%d" % (k, o) for o in range((lo // 512) * 512, lo + n, 512)]

        PF = lambda i: "psf%d" % i
        PB = lambda i: "psb%d" % i
        rot = {"f": 0, "b": 0}

        def fbank(n=1):
            if n == 1:
                i = rot["f"] % 6
                rot["f"] += 1
                return i
            i = ((rot["f"] + 1) // 2 * 2) % 6
            rot["f"] = i + 2
            return i

        def bbank():
            i = rot["b"] % 2
            rot["b"] += 1
            return i

        def fsz(ap):
            n_ = 1
            for d_ in ap.shape[1:]:
                n_ *= d_
            return n_

        def mm(out, lhsT, rhs, start, stop, r, w, tp=None):
            c_ = 0.07 + fsz(rhs) * (4 if rhs.dtype == F32 else 1) / 1800.0
            if tp is None:
                S.op("pe", lambda e: e.matmul(out, lhsT=lhsT, rhs=rhs, start=start, stop=stop), r, w, c_)
            else:
                S.op("pe", lambda e: e.matmul(out, lhsT=lhsT, rhs=rhs, start=start, stop=stop, tile_position=tp), r, w, c_)

        def tr(out, in_, idn, r, w):
            S.op("pe", lambda e: e.transpose(out, in_, idn), r, w, 0.1 + fsz(in_) * (4 if in_.dtype == F32 else 1) / 1800.0)

        def acti(out, in_, func, r, w, bias=None, scale=None, accum=None):
            kw = {}
            if bias is not None:
                kw["bias"] = bias
            if scale is not None:
                kw["scale"] = scale
            if accum is not None:
                kw["accum_out"] = accum
            S.op("act", lambda e: e.activation(out=out, in_=in_, func=func, **kw), r, w, 0.2 + fsz(out) / 1100.0)
            S.trace[-1]["tbl"] = {AF.Exp: "exp", AF.Ln: "exp", AF.Silu: "silu", AF.Sigmoid: "sigm"}.get(func, None)

        def tt(out, in0, in1, op, r, w, eng="dve"):
            S.op(eng, lambda e: e.tensor_tensor(out=out, in0=in0, in1=in1, op=op), r, w, 0.1 + fsz(out) / 900.0)

        def stt(out, in0, scalar, in1, op0, op1, r, w, eng="dve"):
            S.op(eng, lambda e: e.scalar_tensor_tensor(out=out, in0=in0, scalar=scalar, in1=in1, op0=op0, op1=op1), r, w, 0.1 + fsz(out) / 700.0)

        def ts(out, in0, s1, op0, r, w, s2=None, op1=None, eng="dve"):
            if op1 is None:
                S.op(eng, lambda e: e.tensor_scalar(out=out, in0=in0, scalar1=s1, scalar2=None, op0=op0), r, w, 0.1 + fsz(out) / 900.0)
            else:
                S.op(eng, lambda e: e.tensor_scalar(out=out, in0=in0, scalar1=s1, scalar2=s2, op0=op0, op1=op1), r, w, 0.1 + fsz(out) / 900.0)

        def cp(out, in_, r, w, eng="dve"):
            if eng == "act":
                S.op("act", lambda e: e.activation(out=out, in_=in_, func=AF.Copy), r, w, 0.2 + fsz(out) / 1100.0)
            else:
                S.op(eng, lambda e: e.tensor_copy(out=out, in_=in_), r, w, 0.1 + fsz(out) / 1500.0)

        def mset(ap, val, w, eng="dve"):
            S.op(eng, lambda e: e.memset(ap, val), (), w)

        def load(dst_ap, src_ap, names, key="ld", q="sp", r=()):
            S.dma(q, key, lambda e: e.dma_start(out=dst_ap, in_=src_ap), r=r, w=names)

        dbg_col = {"c": 0}

        def dump(ap, names, npart, ncol):
            if not dbg:
                return
            c0 = dbg_col["c"]
            dbg_col["c"] += ncol
            S.dma("pool", "dbg", lambda e: e.dma_start(out=dbg_o[0:npart, c0:c0 + ncol], in_=ap), r=names)
            return c0

        def cast(dst, src, rows, name, r=()):
            for r0 in range(0, rows, 128):
                S.dma("pool", "wc", lambda e, r0=r0: e.dma_start(out=dst[r0:r0 + 128, :], in_=src[r0:r0 + 128, :]), r=r, w=[name])

        for (t, d_, n) in ((identf, ident_d, "identf"), (triu, triu_d, "triu"), (trisl, trisl_d, "trisl"),
                           (cT, cT_d, "cT"), (b_adaT, b_adaT_d, "b_adaT"), (wnT, wnT_d, "wnT"),
                           (bcm, bcm_d, "bcm"), (bcf, bcf_d, "bcf"), (dcol, dcol_d, "dcol"), (gnT, gnT_d, "gnT"),
                           (snT, snT_d, "snT"), (LC, lc_d, "LC")):
            load(t[:], d_[:, :], [n])
        load(blkm[:].rearrange("p m i -> p (m i)"), blkm_d[:, :], ["blkm"])
        load(wcm[:].rearrange("p t i -> p (t i)"), wcm_d[:, :], ["wcm"])
        load(wcf[:].rearrange("p t i -> p (t i)"), wcf_d[:, :], ["wcf"])
        load(wfin[:], wfin_d.partition_broadcast(128).rearrange("p o n -> p (o n)"), ["wfin"])
        load(gpar[:], gpar_d.partition_broadcast(64).rearrange("p o n -> p (o n)"), ["gpar"])
        load(spar[:], spar_d.partition_broadcast(64).rearrange("p o n -> p (o n)"), ["spar"])
        S.dma("pool", "ldc", lambda e: e.dma_start(out=identb[:], in_=ident_d[:, :]), w=["identb"])
        S.dma("pool", "ldc", lambda e: e.dma_start(out=wsm[:], in_=w_in_d[:, OFF_SM:OFF_SM + 48].rearrange("(k p) n -> p k n", p=128)), w=["wsm"])
        mset(onesf[:], 1.0, ["onesf"])
        cp(LCb[:], LC[:], ["LC"], ["LCb"])
        for hh in range(16):
            S.dma("pool", "ldc", lambda e, hh=hh: e.dma_start(out=YGb[64:128, hh * 64:(hh + 1) * 64], in_=mut_d[64:128, :]), w=["YGblo"])
        mset(onesb[:], 1.0, ["onesb"])
        mset(bias48[:], 0.0, ["bias48"])
        cp(bias48[:, 8:16], gpar[:, 8:16], ["gpar"], ["bias48"])
        cp(bias48[:, 16:48], spar[:, 32:64], ["spar"], ["bias48"])
        acti(nA[:, 8:16], gpar[:, 0:8], AF.Exp, ["gpar"], ["nA"])
        acti(nA[:, 16:48], spar[:, 0:32], AF.Exp, ["spar"], ["nA"])
        ts(nA[:, 8:48], nA[:, 8:48], -1.0, ALU.mult, ["nA"], ["nA"])

        wst = {"i": 0}

        def wload(scr, srcname, c0, ncol, nk, k0=0):
            b = wst["i"] % 2
            wst["i"] += 1
            view = wblk[:, b, 0:nk * ncol].rearrange("p (k n) -> p k n", k=nk)
            src = scr[k0 * 128:(k0 + nk) * 128, c0:c0 + ncol].rearrange("(k p) n -> p k n", p=128)
            S.dma("sp", "w%d" % b, lambda e: e.dma_start(out=view, in_=src), r=[srcname], w=["wblk%d" % b], cost=3.0 + nk * ncol / 1500.0)
            return view, "wblk%d" % b

        acti(scT[:], cT[:], AF.Silu, ["cT"], ["scT"])
        pb_ada = fbank()
        for blk in range(6):
            b_ = wst["i"] % 2
            wst["i"] += 1
            wv = wblk[:, b_, 0:8192].rearrange("p (k n) -> p k n", k=8)
            wn = "wblk%d" % b_
            S.dma("pool", "w%d" % b_, lambda e, wv=wv, blk=blk: e.dma_start(out=wv, in_=w_ada_d[:, blk * 1024:(blk + 1) * 1024].rearrange("(k p) n -> p k n", p=128)),
                  w=[wn], cost=12.0)
            for jj in range(8):
                j = blk * 8 + jj
                for kc in range(8):
                    mm(psf[:, pb_ada, 2 * j:2 * j + 2], wv[:, kc, jj * 128:(jj + 1) * 128], scT[:, 2 * kc:2 * kc + 2],
                       kc == 0, kc == 7, [wn, "scT"], [PF(pb_ada)])
        tt(modT[:], psf[:, pb_ada, 0:96].rearrange("p (j s) -> p j s", s=2), b_adaT[:].unsqueeze(2).to_broadcast([128, 48, 2]),
           ALU.add, [PF(pb_ada), "b_adaT"], ["modT"])
        cast(wi_s, w_in_d, D, "wi_s")

        def stream(sidx, x_d, y_d, ntok, T, has_init, sfx):
            NCH = T // 64
            TT = min(128, T)
            NT = T // TT
            nsc = ntok // T

            stt(sc1[:], modT[:, 8:16, sidx], 1.0, wnT[:, 0:8], ALU.add, ALU.mult, ["modT", "wnT"], ["sc1"])
            stt(sc2[:], modT[:, 32:40, sidx], 1.0, wnT[:, 8:16], ALU.add, ALU.mult, ["modT", "wnT"], ["sc2"])
            for gi, base in ((0, 16), (1, 40)):
                for kc in range(8):
                    dg, dgn = Fv(0, (kc % 2) * 512, 128)
                    ts(dg, identf[:], modT[:, base + kc, sidx:sidx + 1], ALU.mult, ["identf", "modT"], dgn)
                    pbk = fbank()
                    mm(psf[:, pbk, 0:128], onesf[:], dg, True, True, ["onesf"] + dgn, [PF(pbk)])
                    cp(gtb[:, gi, kc * 128:(kc + 1) * 128], psf[:, pbk, 0:128], [PF(pbk)], ["gtb"], eng="act")

            CARRY = ["carry%d" % t for t in range(56)]
            CARRYF = ["carryf%d" % t for t in range(44)]
            if not has_init:
                mset(S_f[:], 0.0, ["S_f"]); mset(S_b[:], 0.0, ["S_b"])
                for hh in range(2):
                    mset(H_f[:, hh * 16:(hh + 1) * 16, :], 0.0, ["H_f%d" % hh])
                    mset(H_b[:, hh * 16:(hh + 1) * 16, :], 0.0, ["H_b%d" % hh])
                mset(carry[:], 0.0, CARRY); mset(carryf[:], 0.0, CARRYF)
            else:
                load(S_f[:].rearrange("p h v -> p (h v)"), s0_d[:, :], ["S_f"])
                load(H_f[:].rearrange("p h v -> p (h v)"), h0_d[:, :], ["H_f0", "H_f1"])
                cp(S_b[:], S_f[:], ["S_f"], ["S_b"], eng="act")
                for hh in range(2):
                    cp(H_b[:, hh * 16:(hh + 1) * 16, :], H_f[:, hh * 16:(hh + 1) * 16, :], ["H_f%d" % hh], ["H_b%d" % hh], eng="act")
                S.dma("pool", "ldc", lambda e: e.dma_start(out=carry[:].rearrange("p t r -> p (t r)"), in_=cm0_d[:, :]), w=CARRY)
                S.dma("pool", "ldc", lambda e: e.dma_start(out=carryf[:].rearrange("p t r -> p (t r)"), in_=cf0_d[:, :]), w=CARRYF)

            def rms_to_hT(scv, shbase):
                S.tag = "rms"
                junk, junkn = Bv(2, 0, 1024, TT)
                for nt in range(NT):
                    acti(junk, x_sb[:TT, nt, :], AF.Square, [("x", nt)], junkn + ["ssq"], accum=ssq[:TT, nt:nt + 1])
                acti(ssq[:TT, 4:4 + NT], ssq[:TT, 0:NT], AF.Ln, ["ssq"], ["ssq"], bias=EPS, scale=1.0 / D)
                acti(ssq[:TT, 8:8 + NT], ssq[:TT, 4:4 + NT], AF.Exp, ["ssq"], ["ssq"], scale=-0.5)
                xns = []
                for nt in range(NT):
                    xn, xnn = Bv(nt // 2, (nt % 2) * 1024, 1024, TT)
                    acti(xn, x_sb[:TT, nt, :], AF.Copy, [("x", nt), "ssq"], xnn, scale=ssq[:TT, 8 + nt:9 + nt])
                    xns.append((xn, xnn))
                for kc in range(8):
                    bk = bbank()
                    for nt in range(NT):
                        xn, xnn = xns[nt]
                        tr(psb[:, bk, nt * TT:(nt + 1) * TT], xn[:, kc * 128:(kc + 1) * 128], identb[:TT, :TT], xnn + ["identb"], [PB(bk)])
                    acti(hT[:, kc, 0:T], psb[:, bk, 0:T], AF.Identity, [PB(bk), "modT", "sc1", "sc2"], ["hT"],
                         bias=modT[:, shbase + kc, sidx:sidx + 1], scale=scv[:, kc:kc + 1])

            def conv_tile(ps_ap, psn, wt, bt, ntap, cry, cryn, out_ap, outn, accslot, func, k):
                nb = ntap - 1
                sgi = k % 2
                sg = stg[:, sgi, :]
                sgn = "stg%d" % sgi
                acc, accl = Fv(accslot, (k % 2) * 512, T)
                accn = accl[0]
                cp(sg[:, 0:nb], cry, [cryn], [sgn], eng="dve")
                cp(sg[:, nb:nb + T], ps_ap, [psn], [sgn], eng="act")
                acti(acc, ps_ap, AF.Identity, [psn], [accn], bias=bt, scale=wt[:, nb:nb + 1])
                for i in range(nb - 1, -1, -1):
                    stt(acc, sg[:, i:i + T], wt[:, i:i + 1], acc, ALU.mult, ALU.add, [sgn, accn], [accn])
                if func is None:
                    cp(out_ap, acc, [accn], [outn], eng="act")
                else:
                    acti(out_ap, acc, func, [accn], [outn])
                cp(cry, sg[:, T:T + nb], [sgn], [cryn], eng="dve")

            def proj_conv(col0, ntiles, xc0, cch0, direct=False):
                S.tag = "projconv"
                k = 0
                for b0 in range(0, ntiles, 4):
                    nb_ = min(4, ntiles - b0)
                    if direct:
                        b_ = wst["i"] % 2
                        wst["i"] += 1
                        wv = wblk[:, b_, 0:8 * nb_ * 128].rearrange("p (k n) -> p k n", k=8)
                        wn = "wblk%d" % b_
                        c0_ = col0 + b0 * 128
                        S.dma("pool", "w%d" % b_, lambda e, wv=wv, c0_=c0_, nb_=nb_: e.dma_start(out=wv, in_=w_in_d[:, c0_:c0_ + nb_ * 128].rearrange("(k p) n -> p k n", p=128)),
                              w=[wn], cost=10.0)
                    else:
                        wv, wn = wload(wi_s, "wi_s", col0 + b0 * 128, nb_ * 128, 8)
                    for jj in range(nb_):
                        t = b0 + jj
                        pb = fbank()
                        for kc in range(8):
                            mm(psf[:, pb, 0:T], wv[:, kc, jj * 128:(jj + 1) * 128], hT[:, kc, 0:T], kc == 0, kc == 7, [wn, "hT"], [PF(pb)])
                        ct = cch0 + t
                        conv_tile(psf[:, pb, 0:T], PF(pb), wcm[:, ct, :], bcm[:, ct:ct + 1], 4, carry[:, ct, :], "carry%d" % ct,
                                  XC[:, xc0 + t, 0:T], ("XC", xc0 + t), 0, AF.Silu, k)
                        k += 1

            def l2norm_tiles(tiles, qscale_tiles):
                S.tag = "l2norm"
                for k, t in enumerate(tiles):
                    o = (k % 2) * 512
                    sq, sql = Bv(2, 1024 + o, T); sqn = sql[0]
                    acti(sq, XC[:, t, 0:T], AF.Square, [("XC", t)], [sqn])
                    pb = fbank()
                    mm(psf[:, pb, 0:T], onesb[:], sq, True, True, ["onesb", sqn], [PF(pb)])
                    ln, lnl = Fv(1, o, T); lnn = lnl[0]
                    acti(ln, psf[:, pb, 0:T], AF.Ln, [PF(pb)], [lnn], bias=EPS)
                    rs, rsl = Bv(3, o, T); rsn = rsl[0]
                    if t in qscale_tiles:
                        acti(rs, ln, AF.Exp, [lnn], [rsn], scale=-0.5, bias=float(-0.5 * np.log(128.0)))
                    else:
                        acti(rs, ln, AF.Exp, [lnn], [rsn], scale=-0.5)
                    tt(XC[:, t, 0:T], XC[:, t, 0:T], rs, ALU.mult, [("XC", t), rsn], [("XC", t)])

            def smalls():
                S.tag = "smalls"
                pb = fbank()
                for c in range(NCH):
                    for kc in range(8):
                        mm(psf[:64, pb, c * 48:(c + 1) * 48], hT[:, kc, c * 64:(c + 1) * 64], wsm[:, kc, :], kc == 0, kc == 7, ["hT", "wsm"], [PF(pb)])
                n3 = lambda a: a[:, 0:NCH, :]
                tt(n3(smT), psf[:64, pb, 0:NCH * 48].rearrange("p (c n) -> p c n", n=48), bias48[:].unsqueeze(1).to_broadcast([64, NCH, 48]),
                   ALU.add, [PF(pb), "bias48"], ["smT"])
                ts(smT[:, 0:NCH, 0:8], smT[:, 0:NCH, 0:8], -1.0, ALU.mult, ["smT"], ["smT"])
                acti(n3(smT), n3(smT), AF.Exp, ["smT"], ["smT"])
                acti(n3(smL), n3(smT), AF.Ln, ["smT"], ["smL"], bias=1.0)
                tt(smG[:, 0:NCH, 8:48], smL[:, 0:NCH, 8:48], nA[:, 8:48].unsqueeze(1).to_broadcast([64, NCH, 40]), ALU.mult, ["smL", "nA"], ["smG"])
                mset(smG[:, 0:NCH, 0:8], 0.0, ["smG"])
                pg = fbank()
                mm(psf[:64, pg, 0:NCH * 48], triu[:], smG[:, 0:NCH, :].rearrange("p c n -> p (c n)"), True, True, ["triu", "smG"], [PF(pg)])
                cp(n3(smGAM), psf[:64, pg, 0:NCH * 48].rearrange("p (c n) -> p c n", n=48), [PF(pg)], ["smGAM"])
                pr = fbank()
                mm(psf[:64, pr, 0:NCH * 48], trisl[:], smG[:, 0:NCH, :].rearrange("p c n -> p (c n)"), True, True, ["trisl", "smG"], [PF(pr)])
                acti(n3(smREV), psf[:64, pr, 0:NCH * 48].rearrange("p (c n) -> p c n", n=48), AF.Exp, [PF(pr)], ["smREV"])
                tt(smWJ[:, 0:NCH, :], smREV[:, 0:NCH, 16:48], smL[:, 0:NCH, 16:48], ALU.mult, ["smREV", "smL"], ["smWJ"])
                cp(smGG[:, 0:NCH, 0, :], smGAM[:, 0:NCH, 8:16], ["smGAM"], ["smGG"])
                tt(smGG[:, 0:NCH, 1, :], smGAM[:, 0:NCH, 8:16], smL[:, 0:NCH, 0:8], ALU.subtract, ["smGAM", "smL"], ["smGG"])
                acti(smB[:, 0:NCH, 0, :], smL[:, 0:NCH, 0:8], AF.Exp, ["smL"], ["smB"], scale=-1.0)
                acti(smB[:, 0:NCH, 1, :], smGG[:, 0:NCH, 1, :], AF.Exp, ["smGG"], ["smB"])
                acti(smGS[:, 0:NCH, :], smL[:, 0:NCH, 16:48], AF.Ln, ["smL"], ["smGS"])
                tt(smGS[:, 0:NCH, :], smGAM[:, 0:NCH, 16:48], smGS[:, 0:NCH, :], ALU.subtract, ["smGAM", "smGS"], ["smGS"])

            def gdn_chunk(c, precise):
                S.tag = "gdnG1"
                cs = slice(c * 64, (c + 1) * 64)
                qT = lambda h: XC[:, h, cs]
                kT = lambda h: XC[:, 8 + h, cs]
                vT = lambda h: XC[:, 16 + h, cs]
                QN = [("XC", h) for h in range(8)]; KN = [("XC", 8 + h) for h in range(8)]; VN = [("XC", 16 + h) for h in range(8)]
                h8 = lambda ap: ap.rearrange("p (h k) -> p h k", h=8)
                bk = bbank()
                for h in range(8):
                    tr(psb[:64, bk, h * 128:(h + 1) * 128], kT(h), identb[:, :], [KN[h], "identb"], [PB(bk)])
                kd_, kdn = Bv(3, 0, 1024, 64); kd = h8(kd_)
                tt(kd, h8(psb[:64, bk, :]), smREV[:, c, 8:16].unsqueeze(2).to_broadcast([64, 8, 128]), ALU.mult, [PB(bk), "smREV"], kdn)
                bv_ = bbank()
                for h in range(8):
                    tr(psb[:64, bv_, h * 128:(h + 1) * 128], vT(h), identb[:, :], [VN[h], "identb"], [PB(bv_)])
                bvt_, bvn = Bv(3, 1024, 1024, 64); bvt = h8(bvt_)
                tt(bvt, h8(psb[:64, bv_, :]), smB[:, c, 0, :].unsqueeze(2).to_broadcast([64, 8, 128]), ALU.mult, [PB(bv_), "smB"], bvn)
                tt(YG[0:64, :].rearrange("p (a h i) -> p a h i", a=2, h=8), identf[0:64, 0:64].unsqueeze(1).unsqueeze(1).to_broadcast([64, 2, 8, 64]),
                   smGG[:, c, :, :].unsqueeze(3).to_broadcast([64, 2, 8, 64]), ALU.mult, ["identf", "smGG"], ["YGhi"])
                r0 = fbank(2)
                mm(psf[:64, r0, :], LC[:, :], YG[:, 0:512], True, True, ["LC", "YGhi", "YGlo"], [PF(r0)])
                mm(psf[:64, r0 + 1, :], LC[:, :], YG[:, 512:1024], True, True, ["LC", "YGhi", "YGlo"], [PF(r0 + 1)])
                rb = fbank()
                mm(psf[:, rb, :], onesf[0:64, :], YG[0:64, 0:512], True, True, ["onesf", "YGhi"], [PF(rb)])
                Eb, Ebn = Fv(2, 0, 512)
                acti(Eb, psf[:, rb, :], AF.Exp, [PF(rb)], Ebn)
                X, Xn = Fv(1, 0, 1024, 64)
                tt(X.rearrange("p (a h i) -> p a h i", a=2, h=8), psf[:64, r0:r0 + 2, :].rearrange("p a (h i) -> p a h i", h=8),
                   smGG[:, c, 0, :].unsqueeze(1).unsqueeze(3).to_broadcast([64, 2, 8, 64]), ALU.subtract, [PF(r0), PF(r0 + 1), "smGG"], Xn)
                acti(X, X, AF.Exp, Xn, Xn)
                k0 = fbank(2)
                for h in range(8):
                    mm(psf[:64, k0, h * 64:(h + 1) * 64], kT(h), qT(h), True, True, [KN[h], QN[h]], [PF(k0)])
                for h in range(8):
                    mm(psf[:64, k0 + 1, h * 64:(h + 1) * 64], kT(h), kT(h), True, True, [KN[h]], [PF(k0 + 1)])
                Aqk_, Aqkn = Bv(0, 1024, 512, 64)
                tt(Aqk_, psf[:64, k0, :], X[:, 0:512], ALU.mult, [PF(k0), Xn[0]], Aqkn)
                if precise:
                    LTf = X[:, 512:1024]; LTn = [Xn[1]]
                    tt(LTf, psf[:64, k0 + 1, :], LTf, ALU.mult, [PF(k0 + 1)] + LTn, LTn)
                    Aqk = lambda h: Aqk_[:, h * 64:(h + 1) * 64]
                    qTe, qTen = Bv(5, 0, 512)
                    tt(qTe.rearrange("p (h i) -> p h i", h=8), XC[:, 0:8, cs], Eb.rearrange("p (h i) -> p h i", h=8), ALU.mult, QN + Ebn, qTen)
                    bl = fbank()
                    for h in range(8):
                        hs = slice(h * 64, (h + 1) * 64)
                        tr(psf[:64, bl, hs], LTf[:, hs], identf[:64, :64], LTn + ["identf"], [PF(bl)])
                    Lf = X[:, 0:512]; Lfn = [Xn[0]]
                    cp(Lf, psf[:64, bl, :], [PF(bl)], Lfn, eng="act")
                    def f32v(k, half):
                        ap_, nm = Bv(k, half * 1024, 1024, 64)
                        return ap_.bitcast(F32), nm
                    v8 = lambda ap_: ap_.rearrange("p (h i) -> p h i", h=8)
                    mk = lambda m: blkm[:, m, :].unsqueeze(1).to_broadcast([64, 8, 64])
                    idb = identf[0:64, 0:64].unsqueeze(1).to_broadcast([64, 8, 64])
                    Ld, Ldn = f32v(1, 0); LdT, LdTn = f32v(1, 1)
                    L1, L1n = f32v(2, 0); L1T, L1Tn = f32v(2, 1)
                    L2, L2n = Fv(2, 512, 512, 64)
                    tt(v8(Ld), v8(Lf), mk(0), ALU.mult, Lfn + ["blkm"], Ldn)
                    tt(v8(LdT), v8(LTf), mk(0), ALU.mult, LTn + ["blkm"], LdTn)
                    tt(v8(L1), v8(Lf), mk(1), ALU.mult, Lfn + ["blkm"], L1n)
                    tt(v8(L1T), v8(LTf), mk(1), ALU.mult, LTn + ["blkm"], L1Tn)
                    tt(v8(L2), v8(Lf), mk(2), ALU.mult, Lfn + ["blkm"], L2n)
                    Pb = [(Fv(0, 0, 512, 64), Fv(0, 512, 512, 64)), (f32v(4, 0), f32v(4, 1))]
                    stt(v8(Pb[0][0][0]), v8(Ld), -1.0, idb, ALU.mult, ALU.add, Ldn + ["identf"], Pb[0][0][1])
                    stt(v8(Pb[0][1][0]), v8(LdT), -1.0, idb, ALU.mult, ALU.add, LdTn + ["identf"], Pb[0][1][1])
                    pairs = [((Ld, Ldn), (LdT, LdTn)), ((Lf, Lfn), (LTf, LTn))]

                else:
                    LTf, LTn = Bv(0, 0, 512, 64)
                    tt(LTf, psf[:64, k0 + 1, :], X[:, 512:1024], ALU.mult, [PF(k0 + 1), Xn[1]], LTn)
                    Aqk = lambda h: Aqk_[:, h * 64:(h + 1) * 64]
                    qTe, qTen = Bv(5, 0, 512)
                    tt(qTe.rearrange("p (h i) -> p h i", h=8), XC[:, 0:8, cs], Eb.rearrange("p (h i) -> p h i", h=8), ALU.mult, QN + Ebn, qTen)
                    bl = bbank()
                    for h in range(8):
                        hs = slice(h * 64, (h + 1) * 64)
                        tr(psb[:64, bl, hs], LTf[:, hs], identb[:64, :64], LTn + ["identb"], [PB(bl)])
                    Lf, Lfn = Bv(0, 512, 512, 64)
                    cp(Lf, psb[:64, bl, 0:512], [PB(bl)], Lfn, eng="act")
                    v8 = lambda ap_: ap_.rearrange("p (h i) -> p h i", h=8)
                    mk = lambda m: blkm[:, m, :].unsqueeze(1).to_broadcast([64, 8, 64])
                    idb = identb[0:64, 0:64].unsqueeze(1).to_broadcast([64, 8, 64])
                    Ld, Ldn = Bv(1, 0, 512, 64); LdT, LdTn = Bv(1, 512, 512, 64)
                    L1, L1n = Bv(1, 1024, 512, 64); L1T, L1Tn = Bv(1, 1536, 512, 64)
                    L2, L2n = Bv(0, 1536, 512, 64)
                    tt(v8(Ld), v8(Lf), mk(0), ALU.mult, Lfn + ["blkm"], Ldn)
                    tt(v8(LdT), v8(LTf), mk(0), ALU.mult, LTn + ["blkm"], LdTn)
                    tt(v8(L1), v8(Lf), mk(1), ALU.mult, Lfn + ["blkm"], L1n)
                    tt(v8(L1T), v8(LTf), mk(1), ALU.mult, LTn + ["blkm"], L1Tn)
                    tt(v8(L2), v8(Lf), mk(2), ALU.mult, Lfn + ["blkm"], L2n)
                    Pb = [(Bv(2, 0, 512, 64), Bv(2, 512, 512, 64)), (Bv(2, 1024, 512, 64), Bv(2, 1536, 512, 64))]
                    stt(v8(Pb[0][0][0]), v8(Ld), -1.0, idb, ALU.mult, ALU.add, Ldn + ["identb"], Pb[0][0][1])
                    stt(v8(Pb[0][1][0]), v8(LdT), -1.0, idb, ALU.mult, ALU.add, LdTn + ["identb"], Pb[0][1][1])
                    pairs = [((Ld, Ldn), (LdT, LdTn)), ((Lf, Lfn), (LTf, LTn))]

                def grp(lhs, lhsn, rhs, rhsn):
                    bnk = fbank()
                    for h in range(8):
                        hs = slice(h * 64, (h + 1) * 64)
                        mm(psf[:64, bnk, hs], lhs[:, hs], rhs[:, hs], True, True, lhsn + rhsn, [PF(bnk)])
                    return bnk
                cur = 0
                pc = 0
                for lvl in range(1, 4):
                    (A_, A_n), (AT_, AT_n) = pairs[cur]
                    (nA_, nAn), (nAT, nATn) = pairs[1 - cur]
                    a0 = grp(AT_, AT_n, A_, A_n)
                    a1 = grp(A_, A_n, AT_, AT_n)
                    cp(nA_, psf[:64, a0, :], [PF(a0)], nAn, eng="act")
                    cp(nAT, psf[:64, a1, :], [PF(a1)], nATn, eng="dve")
                    (P_, P_n), (PT_, PT_n) = Pb[pc]
                    (P2, P2n), (PT2, PT2n) = Pb[1 - pc]
                    p0 = grp(nAT, nATn, P_, P_n)
                    p1 = grp(nA_, nAn, PT_, PT_n)
                    tt(P2, psf[:64, p0, :], P_, ALU.add, [PF(p0)] + P_n, P2n)
                    tt(PT2, psf[:64, p1, :], PT_, ALU.add, [PF(p1)] + PT_n, PT2n)
                    cur = 1 - cur
                    pc = 1 - pc
                (Dm_, Dn), (DT_, DTn) = Pb[pc]
                (E_, En_), (ET_, ETn) = Pb[1 - pc]
                (U_, Un), (UT_, UTn) = pairs[0]
                g0 = grp(L1T, L1Tn, Dm_, Dn)
                g1 = grp(L1, L1n, DT_, DTn)
                cp(U_, psf[:64, g0, :], [PF(g0)], Un, eng="act")
                cp(UT_, psf[:64, g1, :], [PF(g1)], UTn, eng="dve")
                g2 = grp(DT_, DTn, U_, Un)
                g3 = grp(Dm_, Dn, UT_, UTn)
                tt(E_, Dm_, psf[:64, g2, :], ALU.subtract, Dn + [PF(g2)], En_)
                tt(ET_, DT_, psf[:64, g3, :], ALU.subtract, DTn + [PF(g3)], ETn)
                g4 = grp(L2, L2n, ET_, ETn)
                cp(UT_, psf[:64, g4, :], [PF(g4)], UTn, eng="act")
                g5 = grp(E_, En_, UT_, UTn)
                MinvT, MinvTn = Bv(5, 512, 512, 64)
                tt(MinvT, ET_, psf[:64, g5, :], ALU.subtract, ETn + [PF(g5)], MinvTn)
                S.tag = "gdnG2"
                s0 = fbank(2)
                for h in range(8):
                    mm(psf[:64, s0 + h // 4, (h % 4) * 128:(h % 4 + 1) * 128], kT(h), S_b[:, h, :], True, True, [KN[h], "S_b"], [PF(s0 + h // 4)])
                vb_, vbn = Bv(4, 0, 1024, 64); vb = h8(vb_)
                tt(vb, psf[:64, s0:s0 + 2, :].rearrange("p a (h v) -> p (a h) v", h=4), smB[:, c, 1, :].unsqueeze(2).to_broadcast([64, 8, 128]),
                   ALU.mult, [PF(s0), PF(s0 + 1), "smB"], vbn)
                tt(vb, bvt, vb, ALU.subtract, bvn + vbn, vbn)
                u0 = fbank(2)
                for h in range(8):
                    mm(psf[:64, u0 + h // 4, (h % 4) * 128:(h % 4 + 1) * 128], MinvT[:, h * 64:(h + 1) * 64], vb[:, h, :], True, True,
                       MinvTn + vbn, [PF(u0 + h // 4)])
                u_, un = Bv(4, 1024, 1024, 64); u = h8(u_)
                cp(u_.rearrange("p (a n) -> p a n", a=2), psf[:64, u0:u0 + 2, :], [PF(u0), PF(u0 + 1)], un, eng="act")
                o0 = fbank()
                for h in range(8):
                    hs = slice(h * 64, (h + 1) * 64)
                    mm(psf[:, o0, hs], u[:, h, :], Aqk(h), True, False, un + Aqkn, [PF(o0)])
                    mm(psf[:, o0, hs], S_b[:, h, :], qTe[:, hs], False, True, ["S_b"] + qTen, [PF(o0)])
                cp(oT[:, :, cs], psf[:, o0, :].rearrange("p (h i) -> p h i", h=8), [PF(o0)], [("oT", h) for h in range(8)], eng="act")
                n0 = fbank(2)
                for h in range(8):
                    mm(psf[:, n0 + h // 4, (h % 4) * 128:(h % 4 + 1) * 128], kd[:, h, :], u[:, h, :], True, True, kdn + un, [PF(n0 + h // 4)])
                elast = Eb.rearrange("p (h i) -> p h i", h=8)[:, :, 63:64].to_broadcast([128, 8, 128])
                tt(S_f[:], S_f[:], elast, ALU.mult, ["S_f"] + Ebn, ["S_f"])
                tt(S_f[:], psf[:, n0:n0 + 2, :].rearrange("p a (h v) -> p (a h) v", h=4), S_f[:], ALU.add, [PF(n0), PF(n0 + 1), "S_f"], ["S_f"])
                cp(S_b[:], S_f[:], ["S_f"], ["S_b"], eng="act")

            def ssd_chunk(c):
                S.tag = "ssd"
                cs = slice(c * 64, (c + 1) * 64)
                xts = [Bv(0, 0, 1024, 64), Bv(0, 1024, 1024, 64)]
                for half in range(2):
                    bk = bbank()
                    for j in range(8):
                        t = half * 8 + j
                        tr(psb[:64, bk, j * 128:(j + 1) * 128], XC[:, t, cs], identb[:, :], [("XC", t), "identb"], [PB(bk)])
                    cp(xts[half][0], psb[:64, bk, :], [PB(bk)], xts[half][1], eng="act")
                bk = bbank()
                for g in range(8):
                    tr(psb[:64, bk, g * 128:(g + 1) * 128], XC[:, 16 + g, cs], identb[:, :], [("XC", 16 + g), "identb"], [PB(bk)])
                B_tok, Btn = Bv(1, 0, 1024, 64)
                cp(B_tok, psb[:64, bk, :], [PB(bk)], Btn, eng="dve")
                cb = fbank()
                for g in range(8):
                    mm(psf[:64, cb, g * 64:(g + 1) * 64], XC[:, 16 + g, cs], XC[:, 24 + g, cs], True, True, [("XC", 16 + g), ("XC", 24 + g)], [PF(cb)])
                cbs, cbsn = Fv(0, 0, 512, 64)
                cp(cbs, psf[:64, cb, :], [PF(cb)], cbsn, eng="act")
                for half in range(2):
                    h0 = half * 16
                    x_tok, xtn = xts[half]
                    tt(YGb[0:64, :].rearrange("p (h i) -> p h i", h=16), triu[:, :].unsqueeze(1).to_broadcast([64, 16, 64]),
                       smG[:, c, 16 + h0:32 + h0].unsqueeze(2).to_broadcast([64, 16, 64]), ALU.mult, ["triu", "smG"], ["YGbhi"])
                    rm = fbank(2)
                    mm(psf[:64, rm, :], LCb[:, :], YGb[:, 0:512], True, True, ["LCb", "YGbhi", "YGblo"], [PF(rm)])
                    mm(psf[:64, rm + 1, :], LCb[:, :], YGb[:, 512:1024], True, True, ["LCb", "YGbhi", "YGblo"], [PF(rm + 1)])
                    r1 = fbank(2)
                    mm(psf[:, r1, :], onesb[0:64, :], YGb[0:64, 0:512], True, True, ["onesb", "YGbhi"], [PF(r1)])
                    mm(psf[:, r1 + 1, :], onesb[0:64, :], YGb[0:64, 512:1024], True, True, ["onesb", "YGbhi"], [PF(r1 + 1)])
                    E128, E128n = Fv(1, 0, 1024)
                    acti(E128.rearrange("p (a n) -> p a n", a=2), psf[:, r1:r1 + 2, :], AF.Exp, [PF(r1), PF(r1 + 1)], E128n)
                    CTe, CTen = Bv(2, 1024, 1024)
                    tt(CTe.rearrange("p (g e i) -> p g e i", g=4, e=4), XC[:, 24 + 4 * half:28 + 4 * half, cs].unsqueeze(2).to_broadcast([128, 4, 4, 64]),
                       E128.rearrange("p (g e i) -> p g e i", g=4, e=4), ALU.mult, [("XC", 24 + 4 * half + g) for g in range(4)] + E128n, CTen)
                    Dm, Dmn = Fv(2, 0, 1024, 64)
                    tt(Dm.rearrange("p (h i) -> p h i", h=16), psf[:64, rm:rm + 2, :].rearrange("p a (h i) -> p (a h) i", h=8),
                       smGS[:, c, h0:h0 + 16].unsqueeze(2).to_broadcast([64, 16, 64]), ALU.subtract, [PF(rm), PF(rm + 1), "smGS"], Dmn)
                    ex, exn = Bv(3, 1024, 1024, 64)
                    acti(ex, Dm, AF.Exp, Dmn, exn)
                    wT, wTn = Bv(3, 0, 1024, 64)
                    tt(wT.rearrange("p (g e i) -> p g e i", g=4, e=4),
                       cbs[:, 256 * half:256 * half + 256].rearrange("p (g i) -> p g i", g=4).unsqueeze(2).to_broadcast([64, 4, 4, 64]),
                       ex.rearrange("p (g e i) -> p g e i", g=4, e=4), ALU.mult, cbsn + exn, wTn)
                    xw, xwn = Bv(1, 1024, 1024, 64)
                    tt(xw.rearrange("p (h q) -> p h q", h=16), x_tok.rearrange("p (h q) -> p h q", h=16),
                       smWJ[:, c, h0:h0 + 16].unsqueeze(2).to_broadcast([64, 16, 64]), ALU.mult, xtn + ["smWJ"], xwn)
                    yb = fbank()
                    for hp in range(8):
                        for e_ in range(2):
                            hl = 2 * hp + e_
                            hg = h0 + hl
                            o_ap = psf[64 * e_:64 * e_ + 64, yb, hp * 64:(hp + 1) * 64]
                            mm(o_ap, x_tok[:, hl * 64:(hl + 1) * 64], wT[:, hl * 64:(hl + 1) * 64], True, False, xtn + wTn, [PF(yb)], tp=(0, 64 * e_))
                            mm(o_ap, H_b[:, hg, :], CTe[:, hl * 64:(hl + 1) * 64], False, True, ["H_b%d" % half] + CTen, [PF(yb)], tp=(0, 64 * e_))
                    cp(yT[:, 8 * half:8 * half + 8, cs], psf[:, yb, :].rearrange("p (h i) -> p h i", h=8), [PF(yb)],
                       [("yT", 8 * half + j) for j in range(8)], eng="act")
                    hb = fbank(2)
                    for gl in range(4):
                        g = 4 * half + gl
                        mm(psf[:, hb + gl // 2, (gl % 2) * 256:(gl % 2 + 1) * 256], B_tok[:, g * 128:(g + 1) * 128], xw[:, gl * 256:(gl + 1) * 256],
                           True, True, Btn + xwn, [PF(hb + gl // 2)])
                    Hh = H_f[:, h0:h0 + 16, :]
                    elast = E128.rearrange("p (h i) -> p h i", h=16)[:, :, 63:64].to_broadcast([128, 16, 64])
                    tt(Hh, Hh, elast, ALU.mult, ["H_f%d" % half] + E128n, ["H_f%d" % half])
                    tt(Hh, psf[:, hb:hb + 2, :].rearrange("p a (h q) -> p (a h) q", h=8), Hh, ALU.add, [PF(hb), PF(hb + 1), "H_f%d" % half], ["H_f%d" % half])
                    cp(H_b[:, h0:h0 + 16, :], Hh, ["H_f%d" % half], ["H_b%d" % half], eng="act")

            def zproj_tile(col, k):
                raise NotImplementedError

            def rms_feat(sq_list, ktiles, scale):
                pb = fbank()
                for i, (sq, sqn) in enumerate(sq_list):
                    mm(psf[:, pb, 0:T], onesb[:], sq, i == 0, i == len(sq_list) - 1, ["onesb", sqn], [PF(pb)])
                return pb

            def gdn_finalize():
                S.tag = "gdnfin"
                for h in range(8):
                    o = (h % 2) * 512
                    sq, sql = Bv(2, 1024 + o, T); sqn = sql[0]
                    acti(sq, oT[:, h, 0:T], AF.Square, [("oT", h)], [sqn])
                    pb = rms_feat([(sq, sqn)], 1, 1.0)
                    ln, lnl = Fv(1, o, T); lnn = lnl[0]
                    acti(ln, psf[:, pb, 0:T], AF.Ln, [PF(pb)], [lnn], bias=EPS, scale=1.0 / 128)
                    rs, rsl = Bv(3, o, T); rsn = rsl[0]
                    acti(rs, ln, AF.Exp, [lnn], [rsn], scale=-0.5)
                    tt(oT[:, h, 0:T], oT[:, h, 0:T], rs, ALU.mult, [("oT", h), rsn], [("oT", h)])
                wv, wn = wload(wi_s, "wi_s", OFF_GZ, 1024, 8)
                for h in range(8):
                    o = (h % 2) * 512
                    pz = fbank()
                    for kc in range(8):
                        mm(psf[:, pz, 0:T], wv[:, kc, h * 128:(h + 1) * 128], hT[:, kc, 0:T], kc == 0, kc == 7, [wn, "hT"], [PF(pz)])
                    zs, zsl = Bv(4, o, T); zsn = zsl[0]
                    acti(zs, psf[:, pz, 0:T], AF.Silu, [PF(pz)], [zsn])
                    stt(oT[:, h, 0:T], oT[:, h, 0:T], gnT[:, 0:1], zs, ALU.mult, ALU.mult, [("oT", h), "gnT", zsn], [("oT", h)])

            def ssd_finalize():
                S.tag = "ssdfin"
                for blk in range(2):
                    wv, wn = wload(wi_s, "wi_s", OFF_SZ + blk * 1024, 1024, 8)
                    for jj in range(8):
                        hp = blk * 8 + jj
                        o = (hp % 2) * 512
                        pz = fbank()
                        for kc in range(8):
                            mm(psf[:, pz, 0:T], wv[:, kc, jj * 128:(jj + 1) * 128], hT[:, kc, 0:T], kc == 0, kc == 7, [wn, "hT"], [PF(pz)])
                        zs, zsl = Bv(4, o, T); zsn = zsl[0]
                        acti(zs, psf[:, pz, 0:T], AF.Silu, [PF(pz)], [zsn])
                        stt(yT[:, hp, 0:T], XC[:, hp, 0:T], dcol[:, hp:hp + 1], yT[:, hp, 0:T], ALU.mult, ALU.add, [("XC", hp), "dcol", ("yT", hp)], [("yT", hp)])
                        tt(yT[:, hp, 0:T], yT[:, hp, 0:T], zs, ALU.mult, [("yT", hp), zsn], [("yT", hp)])
                for g in range(8):
                    sqs = []
                    for e_ in range(2):
                        hp = 2 * g + e_
                        sq, sql = Bv(2, 1024 + e_ * 512, T); sqn = sql[0]
                        acti(sq, yT[:, hp, 0:T], AF.Square, [("yT", hp)], [sqn])
                        sqs.append((sq, sqn))
                    pb = rms_feat(sqs, 2, 1.0)
                    o = (g % 2) * 512
                    ln, lnl = Fv(1, o, T); lnn = lnl[0]
                    acti(ln, psf[:, pb, 0:T], AF.Ln, [PF(pb)], [lnn], bias=EPS, scale=1.0 / 256)
                    rs, rsl = Bv(3, o, T); rsn = rsl[0]
                    acti(rs, ln, AF.Exp, [lnn], [rsn], scale=-0.5)
                    for e_ in range(2):
                        hp = 2 * g + e_
                        stt(yT[:, hp, 0:T], yT[:, hp, 0:T], snT[:, hp:hp + 1], rs, ALU.mult, ALU.mult, [("yT", hp), "snT", rsn], [("yT", hp)])

            MX = lambda f: ARB[4 + f // 4][:, (f % 4) * 512:(f % 4) * 512 + T]
            MXN = lambda f: "arB%d_%d" % (4 + f // 4, (f % 4) * 512)

            def merge_out():
                S.tag = "merge"
                for blk in range(2):
                    wv, wn = wload(wi_s, "wi_s", OFF_MG + blk * 1024, 1024, 8)
                    for jj in range(8):
                        t = blk * 8 + jj
                        pz = fbank()
                        for kc in range(8):
                            mm(psf[:, pz, 0:T], wv[:, kc, jj * 128:(jj + 1) * 128], hT[:, kc, 0:T], kc == 0, kc == 7, [wn, "hT"], [PF(pz)])
                        acti(XC[:, t, 0:T], psf[:, pz, 0:T], AF.Sigmoid, [PF(pz)], [("XC", t)])
                wv, wn = wload(wbg_s, "wbg_s", 0, 1024, 8)
                for f in range(8):
                    pz = fbank()
                    for kc in range(8):
                        mm(psf[:, pz, 0:T], wv[:, kc, f * 128:(f + 1) * 128], oT[:, kc, 0:T], kc == 0, kc == 7, [wn, ("oT", kc)], [PF(pz)])
                    tt(MX(f), psf[:, pz, 0:T], XC[:, f, 0:T], ALU.mult, [PF(pz), ("XC", f)], [MXN(f)])
                for blk in range(2):
                    wv, wn = wload(wbs_s, "wbs_s", blk * 512, 512, 16)
                    for jj in range(4):
                        f = blk * 4 + jj
                        pz = fbank()
                        for kc in range(16):
                            mm(psf[:, pz, 0:T], wv[:, kc, jj * 128:(jj + 1) * 128], yT[:, kc, 0:T], kc == 0, kc == 15, [wn, ("yT", kc)], [PF(pz)])
                        tmp = ARB[3][:, (f % 2) * 512:(f % 2) * 512 + T]; tmpn = "arB3_%d" % ((f % 2) * 512)
                        tt(tmp, psf[:, pz, 0:T], XC[:, 8 + f, 0:T], ALU.mult, [PF(pz), ("XC", 8 + f)], [tmpn])
                        tt(MX(f), MX(f), tmp, ALU.add, [MXN(f), tmpn], [MXN(f)])
                wv, wn = wload(wo_s, "wo_s", 0, 1024, 8)
                for nt in range(NT):
                    for hf in range(2):
                        pz = fbank()
                        for kc in range(8):
                            mm(psf[:TT, pz, :], MX(kc)[:, nt * TT:(nt + 1) * TT], wv[:, kc, hf * 512:(hf + 1) * 512], kc == 0, kc == 7,
                               [wn, MXN(kc)], [PF(pz)])
                        tmp = ARF[1][:TT, 0:512] if hf == 0 else ARF[1][:TT, 512:1024]
                        tmpn = "arF1_%d" % (hf * 512)
                        tt(tmp, psf[:TT, pz, :], gtb[:TT, 0, hf * 512:(hf + 1) * 512], ALU.mult, [PF(pz), "gtb"], [tmpn])
                        tt(x_sb[:TT, nt, hf * 512:(hf + 1) * 512], x_sb[:TT, nt, hf * 512:(hf + 1) * 512], tmp, ALU.add, [("x", nt), tmpn], [("x", nt)])

            def ffn():
                S.tag = "ffn"
                k = 0
                for b0 in range(0, 22, 4):
                    nb_ = min(4, 22 - b0)
                    b = wst["i"] % 2
                    wst["i"] += 1
                    wn = "wblk%d" % b
                    view = wblk[:, b, 0:8 * 2 * nb_ * 128].rearrange("p (k a n) -> p k a n", k=8, a=2)
                    for a_ in range(2):
                        src = wu_s[:, a_ * DFF + b0 * 128:a_ * DFF + (b0 + nb_) * 128].rearrange("(k p) n -> p k n", p=128)
                        S.dma("sp", "w%d" % b, lambda e, a_=a_, src=src, view=view: e.dma_start(out=view[:, :, a_, :], in_=src), r=["wu_s"], w=[wn])
                    for jj in range(nb_):
                        j = b0 + jj
                        res = []
                        for a_ in range(2):
                            pz = fbank()
                            for kc in range(8):
                                mm(psf[:, pz, 0:T], view[:, kc, a_, jj * 128:(jj + 1) * 128], hT[:, kc, 0:T], kc == 0, kc == 7, [wn, "hT"], [PF(pz)])
                            ct = a_ * 22 + j
                            o = (k % 2) * 512
                            if a_ == 0:
                                dst = ARB[3][:, o:o + T]; dstn = "arB3_%d" % o
                                fn_ = AF.Silu
                            else:
                                dst = ARB[3][:, 1024 + o:1024 + o + T]; dstn = "arB3_%d" % (1024 + o)
                                fn_ = None
                            conv_tile(psf[:, pz, 0:T], PF(pz), wcf[:, ct, :], bcf[:, ct:ct + 1], 3, carryf[:, ct, :], "carryf%d" % ct,
                                      dst, dstn, 0 if a_ == 0 else 2, fn_, k)
                            res.append((dst, dstn))
                        tt(XC[:, j, 0:T], res[0][0], res[1][0], ALU.mult, [res[0][1], res[1][1]], [("XC", j)])
                        k += 1
                for cb_ in range(4):
                    wv, wn = wload(wd_s, "wd_s", cb_ * 256, 256, 22)
                    for nt in range(NT):
                        pz = fbank()
                        for kc in range(22):
                            mm(psf[:TT, pz, 0:256], XC[:, kc, nt * TT:(nt + 1) * TT], wv[:, kc, :], kc == 0, kc == 21, [wn, ("XC", kc)], [PF(pz)])
                        tmp = ARF[1][:TT, (nt % 2) * 512:(nt % 2) * 512 + 256]; tmpn = "arF1_%d" % ((nt % 2) * 512)
                        tt(tmp, psf[:TT, pz, 0:256], gtb[:TT, 1, cb_ * 256:(cb_ + 1) * 256], ALU.mult, [PF(pz), "gtb"], [tmpn])
                        tt(x_sb[:TT, nt, cb_ * 256:(cb_ + 1) * 256], x_sb[:TT, nt, cb_ * 256:(cb_ + 1) * 256], tmp, ALU.add, [("x", nt), tmpn], [("x", nt)])

            def final_norm_store(sc):
                S.tag = "final"
                junk, junkn = Bv(2, 0, 1024, TT)
                for nt in range(NT):
                    acti(junk, x_sb[:TT, nt, :], AF.Square, [("x", nt)], junkn + ["ssq"], accum=ssq[:TT, nt:nt + 1])
                acti(ssq[:TT, 4:4 + NT], ssq[:TT, 0:NT], AF.Ln, ["ssq"], ["ssq"], bias=EPS, scale=1.0 / D)
                acti(ssq[:TT, 8:8 + NT], ssq[:TT, 4:4 + NT], AF.Exp, ["ssq"], ["ssq"], scale=-0.5)
                for nt in range(NT):
                    yb_ = ARF[1 + nt % 2][:TT, :]; ybn = ["arF%d_0" % (1 + nt % 2), "arF%d_512" % (1 + nt % 2)]
                    acti(yb_, x_sb[:TT, nt, :], AF.Copy, [("x", nt), "ssq"], ybn, scale=ssq[:TT, 8 + nt:9 + nt])
                    tt(yb_, yb_, wfin[:TT, :], ALU.mult, ybn + ["wfin"], ybn)
                    r0 = sc * T + nt * TT
                    S.dma("pool", "yst", lambda e, yb_=yb_, r0=r0: e.dma_start(out=y_d[r0:r0 + TT, :], in_=yb_), r=ybn)

            def yg_masks(gdn):
                for hh in range(16):
                    src = mut_d if (not gdn or hh < 8) else msut_d
                    S.dma("sp", "ld", lambda e, hh=hh, src=src: e.dma_start(out=YG[64:128, hh * 64:(hh + 1) * 64], in_=src[64:128, :]), w=["YGlo"])

            for sc in range(nsc):
                for nt in range(NT):
                    r0 = sc * T + nt * TT
                    load(x_sb[:TT, nt, :], x_d[r0:r0 + TT, :], [("x", nt)], key="xld%d" % nt)
                rms_to_hT(sc1, 0)
                if stages >= 1:
                    proj_conv(0, 24, 0, 0, direct=(sc == 0 and not has_init))
                    if sc == 0 and not has_init:
                        cast(wbg_s, w_bg_d, D, "wbg_s"); cast(wbs_s, w_bs_d, 2 * D, "wbs_s"); cast(wo_s, w_out_d, D, "wo_s")
                        cast(wu_s, w_up_d, D, "wu_s"); cast(wd_s, w_down_d, DFF, "wd_s")
                    l2norm_tiles(list(range(16)), set(range(8)))
                    smalls()
                if dbg and sc == nsc - 1 and stages == 1:
                    dump(hT[:, 0, 0:64], ["hT"], 128, 64)
                    dump(XC[:, 0, 0:64], [("XC", 0)], 128, 64)
                    dump(XC[:, 8, 0:64], [("XC", 8)], 128, 64)
                    dump(XC[:, 16, 0:64], [("XC", 16)], 128, 64)
                    dump(smL[:, 0, :], ["smL"], 64, 48)
                    dump(smGAM[:, 0, :], ["smGAM"], 64, 48)
                    dump(smREV[:, 0, :], ["smREV"], 64, 48)
                if stages >= 2:
                    if sc == 0:
                        yg_masks(True)
                    for c in range(NCH):
                        gdn_chunk(c, has_init)
                if dbg and sc == nsc - 1 and stages == 2:
                    for h in range(8):
                        dump(oT[:, h, T - 64:T], [("oT", h)], 128, 64)
                if stages >= 3:
                    gdn_finalize()
                    proj_conv(3072, 32, 0, 24)
                    for c in range(NCH):
                        ssd_chunk(c)
                if dbg and sc == nsc - 1 and stages == 3:
                    for h in range(16):
                        dump(yT[:, h, T - 64:T], [("yT", h)], 128, 64)
                if stages >= 4:
                    ssd_finalize()
                    merge_out()
                    rms_to_hT(sc2, 24)
                    ffn()
                final_norm_store(sc)

            S.dma("pool", "sst", lambda e: e.dma_start(out=outs["gd_" + sfx].rearrange("h k v -> k h v"), in_=S_f[:]), r=["S_f"])
            S.dma("pool", "sst", lambda e: e.dma_start(out=outs["ss_" + sfx].rearrange("h n q -> n h q"), in_=H_f[:]), r=["H_f0", "H_f1"])
            for (cr, names, ntile, nr, key) in ((carry, CARRY, 56, 3, "cm_"), (carryf, CARRYF, 44, 2, "cf_")):
                ncol = ntile * nr
                cp(cst[:, 0:ncol].rearrange("p (r t) -> p r t", r=nr), cr[:].rearrange("p t r -> p r t"), names, ["cst"])
                for c0 in range(0, ncol, 128):
                    n = min(128, ncol - c0)
                    pb = fbank()
                    S.op("pe", lambda e, pb=pb, c0=c0, n=n: e.transpose(psf[:n, pb, 0:128], cst[:, c0:c0 + n], identf[:, :]), ["cst", "identf"], [PF(pb)])
                    o_sb = ARF[1][:n, 0:128]
                    cp(o_sb, psf[:n, pb, 0:128], [PF(pb)], ["arF1_0"], eng="act")
                    S.dma("pool", "sst", lambda e, o_sb=o_sb, c0=c0, n=n, key=key: e.dma_start(out=outs[key + sfx][c0:c0 + n, :], in_=o_sb), r=["arF1_0"])

        stream(0, xp, yp, ntok_p, 512, False, "p")
        stream(1, xs, ys, 64, 64, True, "s")
        S.finish("pool")
        emit_program(nc, S, st)
    return nc


def _prep_shared(inp):
    f = np.float32
    A = lambda x: np.ascontiguousarray(x, dtype=f)
    sh = {}
    sh["w_ada"] = A(inp["w_ada"][0]); sh["w_in"] = A(inp["w_in"][0])
    sh["b_adaT"] = A(inp["b_ada"][0].reshape(48, 128).T)
    sh["wnT"] = A(np.concatenate([inp["w_norm_mix"][0].reshape(8, 128).T, inp["w_norm_ffn"][0].reshape(8, 128).T], axis=1))
    sh["wfin"] = A(inp["w_norm_final"].reshape(1, D))
    sh["wcm"] = A(inp["w_conv_mix"][0].reshape(4, 56, 128).transpose(2, 1, 0).reshape(128, 56 * 4))
    sh["bcm"] = A(inp["b_conv_mix"][0].reshape(56, 128).T)
    sh["wcf"] = A(inp["w_ffn_conv"][0].reshape(3, 44, 128).transpose(2, 1, 0).reshape(128, 44 * 3))
    sh["bcf"] = A(inp["b_ffn_conv"][0].reshape(44, 128).T)
    sh["gpar"] = A(np.concatenate([inp["gdn_a_log"][0], inp["gdn_dt_bias"][0]]).reshape(1, 16))
    sh["spar"] = A(np.concatenate([inp["ssd_a_log"][0], inp["ssd_dt_bias"][0]]).reshape(1, 64))
    sh["dcol"] = A(np.repeat(inp["ssd_d"][0], 64).reshape(16, 128).T)
    sh["gnT"] = A(inp["gdn_norm"][0].reshape(128, 1))
    sh["snT"] = A(inp["ssd_norm"][0].reshape(16, 128).T)
    sh["w_bg"] = A(inp["w_branch_gdn"][0]); sh["w_bs"] = A(inp["w_branch_ssd"][0]); sh["w_out"] = A(inp["w_out"][0])
    sh["w_up"] = A(inp["w_ffn_up"][0]); sh["w_down"] = A(inp["w_ffn_down"][0])
    sh["ident"] = np.eye(128, dtype=f)
    lc = np.zeros((128, 64), f); lc[:64] = 1.0; lc[64:] = np.eye(64, dtype=f)
    sh["lc"] = lc
    t = np.arange(64)
    sh["triu"] = (t[:, None] <= t[None, :]).astype(f)
    sh["trisl"] = (t[:, None] > t[None, :]).astype(f)
    mut = np.zeros((128, 64), f); msut = np.zeros((128, 64), f)
    mut[64:] = np.where(t[:, None] <= t[None, :], 0.0, NEG)
    msut[64:] = np.where(t[:, None] < t[None, :], 0.0, NEG)
    sh["mut"] = mut; sh["msut"] = msut
    b16 = t // 16; b32 = t // 32
    md = (b16[:, None] == b16[None, :]); m2 = (b32[:, None] != b32[None, :]); m1 = (~md) & (~m2)
    sh["blkm"] = np.ascontiguousarray(np.stack([md, m1, m2], axis=1).astype(f).reshape(64, 192))
    return sh


def kernel(x_prompt, x_sample, state_conv_mix, state_gdn, state_ssd, state_conv_ffn, c_prompt, c_sample,
           w_ada, b_ada, w_norm_mix, w_in, w_conv_mix, b_conv_mix, gdn_a_log, gdn_dt_bias, gdn_norm,
           ssd_a_log, ssd_dt_bias, ssd_d, ssd_norm, w_branch_gdn, w_branch_ssd, w_out, w_norm_ffn,
           w_ffn_up, w_ffn_conv, b_ffn_conv, w_ffn_down, w_norm_final, _ntok_p=SEQ, _stages=99, _dbg=False, _trace=False):
    inp = dict(w_ada=w_ada, b_ada=b_ada, w_norm_mix=w_norm_mix, w_in=w_in, w_conv_mix=w_conv_mix, b_conv_mix=b_conv_mix,
               gdn_a_log=gdn_a_log, gdn_dt_bias=gdn_dt_bias, gdn_norm=gdn_norm, ssd_a_log=ssd_a_log, ssd_dt_bias=ssd_dt_bias,
               ssd_d=ssd_d, ssd_norm=ssd_norm, w_branch_gdn=w_branch_gdn, w_branch_ssd=w_branch_ssd, w_out=w_out,
               w_norm_ffn=w_norm_ffn, w_ffn_up=w_ffn_up, w_ffn_conv=w_ffn_conv, b_ffn_conv=b_ffn_conv, w_ffn_down=w_ffn_down,
               w_norm_final=w_norm_final)
    inp = {k: np.asarray(v) for k, v in inp.items()}
    sh = _prep_shared(inp)
    f = np.float32
    x_prompt = np.asarray(x_prompt); x_sample = np.asarray(x_sample)
    c_prompt = np.asarray(c_prompt); c_sample = np.asarray(c_sample)
    scm = np.asarray(state_conv_mix)[0]; sgd = np.asarray(state_gdn)[0]; sss = np.asarray(state_ssd)[0]; scf = np.asarray(state_conv_ffn)[0]
    in_maps = []
    for c in range(8):
        m = dict(sh)
        b = c % 4
        m["xp"] = np.ascontiguousarray(x_prompt[b, :_ntok_p], dtype=f)
        m["xs"] = np.ascontiguousarray(x_sample[c], dtype=f)
        cT = np.zeros((128, 16), f)
        cT[:, 0::2] = c_prompt[b].reshape(8, 128).T
        cT[:, 1::2] = c_sample[c].reshape(8, 128).T
        m["cT"] = cT
        m["cm0T"] = np.ascontiguousarray(scm[c].reshape(3, 56, 128).transpose(2, 1, 0).reshape(128, 56 * 3), dtype=f)
        m["s0"] = np.ascontiguousarray(sgd[c].transpose(1, 0, 2).reshape(128, 8 * 128), dtype=f)
        m["h0"] = np.ascontiguousarray(sss[c].transpose(1, 0, 2).reshape(128, 32 * 64), dtype=f)
        m["cf0T"] = np.ascontiguousarray(scf[c].reshape(2, 44, 128).transpose(2, 1, 0).reshape(128, 44 * 2), dtype=f)
        in_maps.append(m)
    nc = build(_ntok_p, _stages, _dbg)
    if _trace:
        res = run_bass_kernel_spmd(nc, in_maps, core_ids=list(range(8)), trace=True)
        print('EXEC_TIME_NS', res.exec_time_ns)
    else:
        res = run_bass_kernel_spmd(nc, in_maps, core_ids=list(range(8)))
    R = res.results
    y_p = np.stack([R[b]["yp"] for b in range(4)])
    y_s = np.stack([R[c]["ys"] for c in range(8)])

    def gather(key, cores, shape):
        return np.stack([np.asarray(R[c][key]).reshape(shape) for c in cores])[None]
    out = (y_p, y_s,
           gather("cm_p", range(4), (3, CONV_CH)), gather("gd_p", range(4), (8, 128, 128)),
           gather("ss_p", range(4), (32, 128, 64)), gather("cf_p", range(4), (2, 2 * DFF)),
           gather("cm_s", range(8), (3, CONV_CH)), gather("gd_s", range(8), (8, 128, 128)),
           gather("ss_s", range(8), (32, 128, 64)), gather("cf_s", range(8), (2, 2 * DFF)))
    out = tuple(np.ascontiguousarray(o, dtype=f) for o in out)
    if _dbg:
        return out, R
    return out
```
